# Optimizing a Trainium2 kernel written in Bass

```python
import math
import jax, jax.numpy as jnp
from jax import lax
import numpy as np

D_MODEL = 4096
BATCH = 4
SEQ = 4096
DEPTH = 1

MIX_WIDTH = D_MODEL
GLA_WIDTH = MIX_WIDTH // 2
S5_WIDTH = MIX_WIDTH - GLA_WIDTH
GLA_HEADS = 4
GLA_DK = GLA_WIDTH // 2 // GLA_HEADS
GLA_DV = GLA_WIDTH // GLA_HEADS
GLA_KEY_WIDTH = GLA_HEADS * GLA_DK
GLA_GATE_RANK = 16
GLA_GATE_NORMALIZER = 16.0
GLA_CHUNK = 64
S5_GROUP = 16
S5_GROUPS = S5_WIDTH // S5_GROUP
S5_STATE = 64
S5_CHUNK = 128
D_FF = ((8 * D_MODEL // 3 + 255) // 256) * 256
N_MOD = 9
EPS = 1e-6
IN_SPLITS = (GLA_KEY_WIDTH,
             2 * GLA_KEY_WIDTH,
             2 * GLA_KEY_WIDTH + GLA_WIDTH,
             2 * GLA_KEY_WIDTH + 2 * GLA_WIDTH,
             2 * GLA_KEY_WIDTH + 2 * GLA_WIDTH + GLA_GATE_RANK)
IN_WIDTH = IN_SPLITS[-1] + S5_WIDTH

kernel_name = "hybrid_gla_s5_macaron_adaln"


def rms_norm(x):
    xf = x.astype(jnp.float32)
    return (xf * lax.rsqrt(jnp.mean(xf * xf, axis=-1, keepdims=True) + EPS)).astype(x.dtype)


def modulate(h, shift, scale):
    return rms_norm(h) * (1.0 + scale[:, None, :]) + shift[:, None, :]


def swiglu(h, w_gate, w_up, w_down):
    return (jax.nn.silu(h @ w_gate) * (h @ w_up)) @ w_down


def gla_chunked(q, k, v, g_log):
    bsz, length, heads, dk = q.shape
    dv = v.shape[-1]
    n_chunks = length // GLA_CHUNK

    def blocks(t):
        return t.reshape(bsz, n_chunks, GLA_CHUNK, heads, t.shape[-1]).transpose(1, 0, 3, 2, 4)

    q, k, v, g = blocks(q), blocks(k), blocks(v), blocks(g_log)
    b = jnp.cumsum(g, axis=-2)
    b_last = b[..., -1:, :]
    q_t = q * jnp.exp(b) * (dk ** -0.5)
    k_t = k * jnp.exp(-b)
    k_dec = k * jnp.exp(b_last - b)
    chunk_decay = jnp.exp(b_last[..., 0, :])
    causal = jnp.tril(jnp.ones((GLA_CHUNK, GLA_CHUNK), dtype=bool))
    att = jnp.where(causal, jnp.einsum('nbhid,nbhjd->nbhij', q_t, k_t), 0.0)
    o_intra = jnp.einsum('nbhij,nbhjv->nbhiv', att, v)

    def step(state, inp):
        q_n, k_n, v_n, d_n = inp
        o_n = jnp.einsum('bhid,bhdv->bhiv', q_n, state)
        state = d_n[..., None] * state + jnp.einsum('bhid,bhiv->bhdv', k_n, v_n)
        return state, o_n

    state0 = jnp.zeros((bsz, heads, dk, dv), jnp.float32)
    _, o_inter = lax.scan(step, state0, (q_t, k_dec, v, chunk_decay))
    o = o_intra + o_inter
    return o.transpose(1, 0, 3, 2, 4).reshape(bsz, length, heads, dv)


def s5_ssm(u, lam_re, lam_im, log_dt, b_re, b_im, c_re, c_im):
    bsz, length, groups, gw = u.shape
    n_chunks = length // S5_CHUNK
    dt = jnp.exp(log_dt.astype(jnp.float32))[:, None]
    lam_re = lam_re.astype(jnp.float32)
    lam_im = lam_im.astype(jnp.float32)
    mag = jnp.exp(lam_re * dt)
    ang = lam_im * dt
    abar_re, abar_im = mag * jnp.cos(ang), mag * jnp.sin(ang)
    den = lam_re * lam_re + lam_im * lam_im
    nr, ni = abar_re - 1.0, abar_im
    f_re = (nr * lam_re + ni * lam_im) / den
    f_im = (ni * lam_re - nr * lam_im) / den
    bb_re = f_re[..., None] * b_re - f_im[..., None] * b_im
    bb_im = f_re[..., None] * b_im + f_im[..., None] * b_re
    a_re = jnp.broadcast_to(abar_re, (bsz, S5_CHUNK, groups, S5_STATE))
    a_im = jnp.broadcast_to(abar_im, (bsz, S5_CHUNK, groups, S5_STATE))
    u_blocks = u.reshape(bsz, n_chunks, S5_CHUNK, groups, gw).transpose(1, 0, 2, 3, 4)

    def combine(e1, e2):
        a1r, a1i, b1r, b1i = e1
        a2r, a2i, b2r, b2i = e2
        return (a2r * a1r - a2i * a1i,
                a2r * a1i + a2i * a1r,
                a2r * b1r - a2i * b1i + b2r,
                a2r * b1i + a2i * b1r + b2i)

    def step(carry, u_c):
        xr0, xi0 = carry
        bu_re = jnp.einsum('btgh,gph->btgp', u_c, bb_re)
        bu_im = jnp.einsum('btgh,gph->btgp', u_c, bb_im)
        ar, ai, sr, si = lax.associative_scan(combine, (a_re, a_im, bu_re, bu_im), axis=1)
        xr = ar * xr0[:, None] - ai * xi0[:, None] + sr
        xi = ar * xi0[:, None] + ai * xr0[:, None] + si
        y = jnp.einsum('btgp,ghp->btgh', xr, c_re) - jnp.einsum('btgp,ghp->btgh', xi, c_im)
        return (xr[:, -1], xi[:, -1]), y

    zeros = jnp.zeros((bsz, groups, S5_STATE), jnp.float32)
    _, y = lax.scan(step, (zeros, zeros), u_blocks)
    return y.transpose(1, 0, 2, 3, 4).reshape(bsz, length, groups, gw)


def hybrid_mixer(n, w_in, w_gk2, b_gk, gla_norm_w, lam_re, lam_im, log_dt, b_re, b_im,
                 c_re, c_im, s5_d, w_glu, b_glu, s5_norm_w, w_out):
    bsz, length, _ = n.shape
    proj = n @ w_in
    q, k, v, g_out, gk_lr, u = jnp.split(proj, IN_SPLITS, axis=-1)
    gk = jax.nn.log_sigmoid((gk_lr @ w_gk2 + b_gk).astype(jnp.float32)) / GLA_GATE_NORMALIZER

    def heads(t, d):
        return t.reshape(bsz, length, GLA_HEADS, d).astype(jnp.float32)

    o = gla_chunked(heads(q, GLA_DK), heads(k, GLA_DK), heads(v, GLA_DV), heads(gk, GLA_DK))
    o = rms_norm(o) * gla_norm_w
    o = o.reshape(bsz, length, GLA_WIDTH) * jax.nn.silu(g_out.astype(jnp.float32))
    uf = u.astype(jnp.float32)
    y = s5_ssm(uf.reshape(bsz, length, S5_GROUPS, S5_GROUP), lam_re, lam_im, log_dt,
               b_re, b_im, c_re, c_im).reshape(bsz, length, S5_WIDTH) + s5_d * uf
    y = jax.nn.gelu(y)
    y = y * jax.nn.sigmoid(y @ w_glu + b_glu)
    y = rms_norm(y) * s5_norm_w
    merged = jnp.concatenate([o.astype(n.dtype), y.astype(n.dtype)], axis=-1)
    return merged @ w_out


def setup_inputs(seed: int = 0) -> dict:
    key = jax.random.key(seed)
    ks = iter(jax.random.split(key, 40))

    def nrm(shape, scale):
        return jax.random.normal(next(ks), shape, jnp.float32) * scale

    L_ = DEPTH
    x = nrm((BATCH, SEQ, D_MODEL), 1.0)
    c = nrm((BATCH, D_MODEL), 1.0)
    w_ada = nrm((L_, D_MODEL, N_MOD * D_MODEL), 0.5 * D_MODEL ** -0.5)
    b_ada = nrm((L_, N_MOD * D_MODEL), 0.02)
    ffn1_w_gate = nrm((L_, D_MODEL, D_FF), D_MODEL ** -0.5)
    ffn1_w_up = nrm((L_, D_MODEL, D_FF), D_MODEL ** -0.5)
    ffn1_w_down = nrm((L_, D_FF, D_MODEL), D_FF ** -0.5)
    w_in = nrm((L_, D_MODEL, IN_WIDTH), D_MODEL ** -0.5)
    w_gk2 = nrm((L_, GLA_GATE_RANK, GLA_KEY_WIDTH), GLA_GATE_RANK ** -0.5)
    b_gk = nrm((L_, GLA_KEY_WIDTH), 0.1)
    gla_norm_w = 1.0 + nrm((L_, GLA_DV), 0.02)
    n_idx = jnp.arange(S5_STATE, dtype=jnp.float32)
    s5_lambda_re = -0.5 + nrm((L_, S5_GROUPS, S5_STATE), 0.01)
    s5_lambda_im = math.pi * n_idx + nrm((L_, S5_GROUPS, S5_STATE), 0.01)
    s5_log_dt = jax.random.uniform(next(ks), (L_, S5_GROUPS), jnp.float32,
                                   math.log(1e-3), math.log(1e-1))
    s5_b_re = nrm((L_, S5_GROUPS, S5_STATE, S5_GROUP), (2.0 * S5_GROUP) ** -0.5)
    s5_b_im = nrm((L_, S5_GROUPS, S5_STATE, S5_GROUP), (2.0 * S5_GROUP) ** -0.5)
    s5_c_re = nrm((L_, S5_GROUPS, S5_GROUP, S5_STATE), (2.0 * S5_STATE) ** -0.5)
    s5_c_im = nrm((L_, S5_GROUPS, S5_GROUP, S5_STATE), (2.0 * S5_STATE) ** -0.5)
    s5_d = nrm((L_, S5_WIDTH), 1.0)
    w_glu = nrm((L_, S5_WIDTH, S5_WIDTH), S5_WIDTH ** -0.5)
    b_glu = nrm((L_, S5_WIDTH), 0.02)
    s5_norm_w = 1.0 + nrm((L_, S5_WIDTH), 0.02)
    w_out = nrm((L_, MIX_WIDTH, D_MODEL), MIX_WIDTH ** -0.5)
    ffn2_w_gate = nrm((L_, D_MODEL, D_FF), D_MODEL ** -0.5)
    ffn2_w_up = nrm((L_, D_MODEL, D_FF), D_MODEL ** -0.5)
    ffn2_w_down = nrm((L_, D_FF, D_MODEL), D_FF ** -0.5)
    final_norm_w = 1.0 + nrm((D_MODEL,), 0.02)
    return {"x": x, "c": c, "w_ada": w_ada, "b_ada": b_ada,
            "ffn1_w_gate": ffn1_w_gate, "ffn1_w_up": ffn1_w_up, "ffn1_w_down": ffn1_w_down,
            "w_in": w_in, "w_gk2": w_gk2, "b_gk": b_gk, "gla_norm_w": gla_norm_w,
            "s5_lambda_re": s5_lambda_re, "s5_lambda_im": s5_lambda_im, "s5_log_dt": s5_log_dt,
            "s5_b_re": s5_b_re, "s5_b_im": s5_b_im, "s5_c_re": s5_c_re, "s5_c_im": s5_c_im,
            "s5_d": s5_d, "w_glu": w_glu, "b_glu": b_glu, "s5_norm_w": s5_norm_w,
            "w_out": w_out,
            "ffn2_w_gate": ffn2_w_gate, "ffn2_w_up": ffn2_w_up, "ffn2_w_down": ffn2_w_down,
            "final_norm_w": final_norm_w}


def reference(x, c, w_ada, b_ada, ffn1_w_gate, ffn1_w_up, ffn1_w_down, w_in, w_gk2, b_gk,
              gla_norm_w, s5_lambda_re, s5_lambda_im, s5_log_dt, s5_b_re, s5_b_im, s5_c_re,
              s5_c_im, s5_d, w_glu, b_glu, s5_norm_w, w_out, ffn2_w_gate, ffn2_w_up,
              ffn2_w_down, final_norm_w):
    h = x
    for l in range(DEPTH):
        mod = jax.nn.silu(c) @ w_ada[l] + b_ada[l]
        sh1, sc1, g1, sh2, sc2, g2, sh3, sc3, g3 = jnp.split(mod, N_MOD, axis=-1)
        n = modulate(h, sh1, sc1)
        h = h + 0.5 * g1[:, None, :] * swiglu(n, ffn1_w_gate[l], ffn1_w_up[l], ffn1_w_down[l])
        n = modulate(h, sh2, sc2)
        mix = hybrid_mixer(n, w_in[l], w_gk2[l], b_gk[l], gla_norm_w[l], s5_lambda_re[l],
                           s5_lambda_im[l], s5_log_dt[l], s5_b_re[l], s5_b_im[l], s5_c_re[l],
                           s5_c_im[l], s5_d[l], w_glu[l], b_glu[l], s5_norm_w[l], w_out[l])
        h = h + g2[:, None, :] * mix
        n = modulate(h, sh3, sc3)
        h = h + 0.5 * g3[:, None, :] * swiglu(n, ffn2_w_gate[l], ffn2_w_up[l], ffn2_w_down[l])
    return rms_norm(h) * final_norm_w
```

```python
import contextlib
import os
import math
import numpy as np
import concourse.bass as bass
import concourse.mybir as mybir
from concourse.bass_utils import run_bass_kernel_spmd

F32 = mybir.dt.float32
F32R = mybir.dt.float32r
AF = mybir.ActivationFunctionType
ALU = mybir.AluOpType

ENGS = ("pe", "act", "dve", "pool", "sp")


class Res:
    __slots__ = ("w", "rs")

    def __init__(self):
        self.w = None
        self.rs = []


class Op:
    __slots__ = ("eng", "fn", "deps", "inc", "val", "dma", "dsem", "dval", "prewait", "dinc")

    def __init__(self, eng, fn, dma=False):
        self.eng = eng
        self.fn = fn
        self.deps = []
        self.inc = False
        self.val = 0
        self.dma = dma
        self.dsem = None
        self.dval = 0
        self.prewait = None
        self.dinc = 16


class Prog:
    KDMA = 8

    def __init__(self, nc, same_engine_sync=True):
        self.nc = nc
        self.q = {e: [] for e in ENGS}
        self.ndma = {e: 0 for e in ENGS}
        self.same = same_engine_sync
        self.stack = contextlib.ExitStack()
        self.esem = {}
        self.dsems = {}
        self.lastdma = {}
        self.rd = {}

    def R(self, *key):
        r = self.rd.get(key)
        if r is None:
            r = self.rd[key] = Res()
        return r

    def sbuf(self, name, shape, dt=F32):
        return self.stack.enter_context(self.nc.sbuf_tensor(name, list(shape), dt))

    def psum(self, name, shape, dt=F32):
        return self.stack.enter_context(self.nc.psum_tensor(name, list(shape), dt))

    def sem(self, name):
        return self.stack.enter_context(self.nc.semaphore(name))

    def _track(self, op, reads, writes):
        deps = op.deps
        for r in reads:
            if r.w is not None:
                deps.append(r.w)
            if not op.dma:
                rs = r.rs
                for i in range(len(rs)):
                    if (not rs[i].dma) and rs[i].eng == op.eng:
                        rs[i] = op
                        break
                else:
                    rs.append(op)
            else:
                r.rs.append(op)
        for w in writes:
            if w.w is not None:
                deps.append(w.w)
            for x in w.rs:
                if x is not op:
                    deps.append(x)
            w.w = op
            w.rs = []
        for d in deps:
            if d.dma:
                continue
            if d.eng == op.eng and not op.dma and (op.eng == "pe" or not self.same):
                continue
            d.inc = True

    def op(self, eng, fn, reads=(), writes=()):
        o = Op(eng, fn)
        self._track(o, reads, writes)
        self.q[eng].append(o)
        return o

    def dma(self, eng, fn, reads=(), writes=()):
        o = Op(eng, fn, dma=True)
        self._track(o, reads, writes)
        n = self.ndma[eng]
        self.ndma[eng] = n + 1
        o.dsem = (eng, n % self.KDMA)
        o.dval = 16 * (n // self.KDMA + 1)
        if n >= self.KDMA:
            o.prewait = (o.dsem, 16 * (n // self.KDMA))
        self.lastdma[o.dsem] = o
        self.q[eng].append(o)
        return o

    def coll(self, eng, fn, reads=(), writes=()):
        o = Op(eng, fn, dma=True)
        self._track(o, reads, writes)
        self.ncoll = getattr(self, "ncoll", 0) + 1
        o.dsem = ("cc", self.ncoll)
        o.dval = 1
        o.dinc = 1
        self.lastdma[o.dsem] = o
        self.q[eng].append(o)
        return o

    def barrier(self):
        lasts = []
        for e in ENGS:
            for o in reversed(self.q[e]):
                if not o.dma and o.fn is not None:
                    lasts.append(o)
                    break
        lasts += list(self.lastdma.values())
        for e in ENGS:
            o = Op(e, None)
            o.deps = list(lasts)
            for d in lasts:
                d.inc = True
            self.q[e].append(o)

    def emit(self, final_wait_ops=()):
        nc = self.nc
        for e in ENGS:
            self.esem[e] = self.sem("es_" + e)
            for k in range(self.KDMA):
                if self.ndma[e] > k:
                    self.dsems[(e, k)] = self.sem("ds_%s%d" % (e, k))
        for i in range(getattr(self, "ncoll", 0)):
            self.dsems[("cc", i + 1)] = self.sem("cc_%d" % (i + 1))
        for e in ENGS:
            c = 0
            for o in self.q[e]:
                if not o.dma and o.inc and o.fn is not None:
                    c += 1
                o.val = c
        same = self.same

        def run(e, h):
            known = {}
            esem = self.esem
            for o in self.q[e]:
                if o.prewait is not None:
                    sk, v = o.prewait
                    if known.get(sk, 0) < v:
                        h.wait_ge(self.dsems[sk], v)
                        known[sk] = v
                for d in o.deps:
                    if d.dma:
                        sk, v, s = d.dsem, d.dval, self.dsems[d.dsem]
                    else:
                        if d.eng == e and (e == "pe" or not same):
                            continue
                        sk, v, s = d.eng, d.val, esem[d.eng]
                    if known.get(sk, 0) < v:
                        h.wait_ge(s, v)
                        known[sk] = v
                if o.fn is None:
                    continue
                ins = o.fn(h)
                if o.dma:
                    ins.then_inc(self.dsems[o.dsem], o.dinc)
                elif o.inc:
                    ins.then_inc(esem[e], 1)
            if e == "sp":
                for o in final_wait_ops:
                    sk, v = o.dsem, o.dval
                    if known.get(sk, 0) < v:
                        h.wait_ge(self.dsems[sk], v)
                        known[sk] = v

        with nc.Block() as block:
            @block.tensor
            def _(h):
                run("pe", h)

            @block.scalar
            def _(h):
                run("act", h)

            @block.vector
            def _(h):
                run("dve", h)

            @block.gpsimd
            def _(h):
                run("pool", h)

            @block.sync
            def _(h):
                run("sp", h)
        self.stack.close()


class Cfg:
    def __init__(self, D, DFF, NTOK, H, DK, DV, G, T=512, R=16, NS=4, pair=True, ncores=8):
        self.ncores = ncores
        self.D, self.DFF, self.NTOK, self.H, self.DK, self.DV, self.G = D, DFF, NTOK, H, DK, DV, G
        self.T, self.R, self.NS, self.pair = T, R, NS, pair
        self.DC = D // 128
        self.FC = DFF // 128
        self.NT = NTOK // T
        self.KW, self.GW, self.SW = H * DK, H * DV, 16 * G
        self.KC, self.GC, self.SC = self.KW // 128, self.GW // 128, self.SW // 128
        self.KC2, self.VC = DK // 128, DV // 128
        self.NGP = G // 2
        self.MC = self.GC + self.SC
        self.INW = 2 * self.KW + 2 * self.GW + R + self.SW
        c = 0
        self.V_BADA = c; c += 9 * self.DC
        self.V_FNW = c; c += self.DC
        self.V_BGK = c; c += self.KC
        self.V_GNW = c; c += self.VC
        self.V_S5D = c; c += self.SC
        self.V_BGLU = c; c += self.SC
        self.V_S5NW = c; c += self.SC
        self.NV = c
        base, rem = divmod(self.FC, NS)
        self.fsplit = []
        f = 0
        for s in range(NS):
            n = base + (1 if s < rem else 0)
            self.fsplit.append((f, f + n))
            f += n
        self.FSUB = base + (1 if rem else 0)


FULL = dict(D=4096, DFF=11008, NTOK=2048, H=4, DK=256, DV=512, G=128)
DEBUG_OUT = False
_LASTP = None
GB = 256
CH = 128
EPS = 1e-6


def build(cfg, stages=("mod", "A", "M", "B")):
    nc = bass.Bass("TRN2", target_bir_lowering=False)
    nc.dge_precook = False
    P = Prog(nc)
    global _LASTP
    _LASTP = P
    R = P.R
    D, DC, FC, T, NT, NTOK = cfg.D, cfg.DC, cfg.FC, cfg.T, cfg.NT, cfg.NTOK
    KC, GC, SC, MC, KC2, VC, H = cfg.KC, cfg.GC, cfg.SC, cfg.MC, cfg.KC2, cfg.VC, cfg.H
    KW, GW, SW, NGP, RK = cfg.KW, cfg.GW, cfg.SW, cfg.NGP, cfg.R
    XC = max(DC, MC)

    def din(name, shape, dt=F32):
        return nc.dram_tensor(name, list(shape), dt, kind="ExternalInput").ap()

    def dscr(name, shape, dt=F32):
        if DEBUG_OUT:
            return nc.dram_tensor(name, list(shape), dt, kind="ExternalOutput").ap()
        return nc.dram_tensor(name, list(shape), dt).ap()

    xT = din("xT", [DC, 128, NTOK])
    cT = din("cT", [128, DC])
    vecs = din("vecs", [128, cfg.NV])
    flag = din("flag", [128, 1])
    NADA = 9 * DC * 128 // 512
    NADS = 3 if NADA % 3 == 0 else 1
    NADP = NADA // NADS
    w_ada_parts = [din("w_ada%d" % i, [NADP, 128, DC * 512]) for i in range(NADS)]
    wg = [din("wg%d" % i, [FC, 128, DC * 128]) for i in range(2)]
    wu = [din("wu%d" % i, [FC, 128, DC * 128]) for i in range(2)]
    wd = [din("wd%d" % i, [DC, 128, FC * 128]) for i in range(2)]
    NFM = 2 * KC + GC + SC
    win_fm = din("win_fm", [NFM, 128, DC * 128])
    win_lr = din("win_lr", [128, DC * RK])
    NTMB = (KW + GW) // 512
    win_tm = din("win_tm", [NTMB, 128, DC * 512])
    w_gk2 = din("w_gk2", [RK, KW])
    b_gkr = din("b_gkr", [1, KW])
    w_glu = din("w_glu", [SC, 128, SC * 128])
    w_out = din("w_out", [DC, 128, MC * 128])
    s5lam = din("s5lam", [128, 3 * NGP])
    s5B = din("s5B", [NGP, 128, 256])
    s5C = din("s5C", [NGP, 128, 256])
    outT = nc.dram_tensor("outT", [DC, 128, NTOK], F32, kind="ExternalOutput").ap()

    res = dscr("res", [DC, 128, NTOK])
    qTs = dscr("qTs", [KC, 128, NTOK])
    kTs = dscr("kTs", [KC, 128, NTOK])
    gTs = dscr("gTs", [GC, 128, NTOK])
    uTs = dscr("uTs", [SC, 128, NTOK])
    lrs = dscr("lrs", [RK, NTOK])
    ktm = dscr("ktm", [NTOK, KW])
    vtm = dscr("vtm", [NTOK, GW])
    yas = dscr("yas", [SC, 128, NTOK])
    mTs = dscr("mTs", [MC, 128, NTOK])
    SJ = KC2 * H * cfg.DV // 128
    st_src = [nc.dram_tensor("st_src%d" % k, [32 * SJ, 128], F32).ap() for k in range(4)]
    st_dst = [nc.dram_tensor("st_dst%d" % k, [2 * 32 * SJ, 128], F32).ap() for k in range(4)]
    sz_src = nc.dram_tensor("sz_src", [128, 128], F32).ap()
    sz_dst = nc.dram_tensor("sz_dst", [256, 128], F32).ap()

    ones = P.sbuf("ones", [128, 128])
    maskT = P.sbuf("maskT", [128, 128])
    Umask = P.sbuf("Umask", [128, 128])
    rmask = P.sbuf("rmask", [128, 512])
    iot = P.sbuf("iot", [128, 512])
    vec = P.sbuf("vec", [128, cfg.NV])
    modT = P.sbuf("modT", [128, 9 * DC])
    csl = P.sbuf("csl", [128, DC])
    flg = P.sbuf("flg", [128, 1])
    AW = 45056
    AWF = 5700
    arenaF = P.sbuf("arenaF", [128, AWF])
    craw = P.sbuf("craw", [128, DC])
    arena = P.sbuf("arena", [128, AW])
    ps = [P.psum("ps%d" % i, [128, 512]) for i in range(8)]
    r_ps = [R("ps", i) for i in range(8)]
    r_c = R("consts")

    def AV(off, n, inner=None):
        v = arena[:, off:off + n]
        if inner is not None:
            v = v.rearrange("p (c t) -> p c t", t=inner)
        return v

    def r32(ap):
        return ap.bitcast(F32R)

    def AF_(off, n, inner=None):
        v = arenaF[:, off:off + n]
        if inner is not None:
            v = v.rearrange("p (c t) -> p c t", t=inner)
        return v

    tmpc = P.sbuf("tmpc", [128, 128])
    P.op("pool", lambda h: h.memset(tmpc[:], 1.0), writes=[r_c])
    P.op("pool", lambda h: h.tensor_copy(out=r32(ones[:]), in_=tmpc[:]), reads=[r_c], writes=[r_c])
    P.op("pool", lambda h: h.memset(maskT[:], 1.0), writes=[r_c])
    P.op("pool", lambda h: h.affine_select(out=maskT[:], in_=maskT[:], pattern=[[1, 128]], compare_op=ALU.is_ge,
                                           fill=0.0, base=0, channel_multiplier=-1), reads=[r_c], writes=[r_c])
    P.op("pool", lambda h: h.affine_select(out=tmpc[:], in_=tmpc[:], pattern=[[-1, 128]], compare_op=ALU.is_ge,
                                           fill=0.0, base=-1, channel_multiplier=1), reads=[r_c], writes=[r_c])
    P.op("pool", lambda h: h.tensor_copy(out=r32(Umask[:]), in_=tmpc[:]), reads=[r_c], writes=[r_c])
    P.op("pool", lambda h: h.memset(rmask[:], 1.0), writes=[r_c])
    for c in range(512 // CH):
        P.op("pool", lambda h, c=c: h.memset(rmask[:, c * CH:c * CH + 1], 0.0), reads=[r_c], writes=[r_c])
    P.op("pool", lambda h: h.iota(iot[:], pattern=[[1, 512]], base=0, channel_multiplier=0,
                                  allow_small_or_imprecise_dtypes=True), writes=[r_c])
    r_vec = R("vec")
    P.dma("pool", lambda h: h.dma_start(out=vec[:], in_=vecs), writes=[r_vec])
    P.dma("pool", lambda h: h.dma_start(out=flg[:], in_=flag), writes=[r_vec])
    r_mod = R("mod")

    if "mod" in stages:
        r_csl = R("csl")
        P.dma("pool", lambda h: h.dma_start(out=craw[:], in_=cT), writes=[R("craw")])
        P.op("act", lambda h: h.activation(out=csl[:], in_=craw[:], func=AF.Silu), reads=[R("craw")], writes=[r_csl])
        WT = DC * 512
        nsl = 2 if 2 * WT <= AW else 1
        r_aw = [R("adaw", i) for i in range(nsl)]
        for j in range(NADA):
            s = j % nsl
            wt = AV(s * WT, WT)
            wsrc = w_ada_parts[j // NADP][j % NADP]
            NPC = 4 if DC % 4 == 0 else 1
            CH_ = WT // NPC
            for pc in range(NPC):
                P.dma("sp", lambda h, wsrc=wsrc, wt=wt, pc=pc: h.dma_start(out=r32(wt[:, pc * CH_:(pc + 1) * CH_]),
                                                                          in_=r32(wsrc[:, pc * CH_:(pc + 1) * CH_])), writes=[r_aw[s]])
            for sub in range(4):
                col = j * 4 + sub
                for kc in range(DC):
                    P.op("pe", lambda h, wt=wt, sub=sub, kc=kc, col=col: h.matmul(
                        ps[7][:, col:col + 1], wt[:, kc * 512 + sub * 128: kc * 512 + (sub + 1) * 128],
                        csl[:, kc:kc + 1], start=(kc == 0), stop=(kc == DC - 1)),
                        reads=[r_aw[s], r_csl], writes=[r_ps[7]])
        P.op("dve", lambda h: h.tensor_tensor(out=modT[:], in0=ps[7][:, 0:9 * DC], in1=vec[:, cfg.V_BADA:cfg.V_BADA + 9 * DC],
                                              op=ALU.add), reads=[r_ps[7], r_vec], writes=[r_mod])
        for j in (1, 4, 7):
            P.op("dve", lambda h, j=j: h.tensor_scalar_add(out=modT[:, j * DC:(j + 1) * DC], in0=modT[:, j * DC:(j + 1) * DC],
                                                           scalar1=1.0), reads=[r_mod], writes=[r_mod])
        for j in (2, 8):
            P.op("dve", lambda h, j=j: h.tensor_scalar_mul(out=modT[:, j * DC:(j + 1) * DC], in0=modT[:, j * DC:(j + 1) * DC],
                                                           scalar1=0.5), reads=[r_mod], writes=[r_mod])
        P.barrier()

    def mcol(j, c):
        return modT[:, j * DC + c: j * DC + c + 1]

    def vcol(base, c):
        return vec[:, base + c: base + c + 1]

    O_NT = 0
    O_HID = O_NT + XC * T
    O_W = O_HID + cfg.FSUB * T
    WSL = max(DC * 128, MC * 128, min(8, DC) * 512, cfg.FSUB * 128)
    NW = 4
    O_SQ = O_W + NW * WSL
    O_END = O_SQ + 2 * T
    assert O_END <= AW, (O_END, AW)
    O_RB = 0
    O_OB = O_RB + 2 * T
    O_RS = O_OB + 2 * T
    O_SG = O_RS + T
    assert O_SG + 2 * T <= AWF
    nT = AV(O_NT, XC * T, T)
    hid = AV(O_HID, cfg.FSUB * T, T)
    wsl = [AV(O_W + i * WSL, WSL) for i in range(NW)]
    rb = [AF_(O_RB + i * T, T) for i in range(2)]
    ob = [AF_(O_OB + i * T, T) for i in range(2)]
    sq = [AV(O_SQ + i * T, T) for i in range(2)]
    rstd = AF_(O_RS, T)
    sgb = [AF_(O_SG + i * T, T) for i in range(2)]
    r_sg = [R("sg", i) for i in range(2)]
    r_nT = [R("nT", c) for c in range(XC)]
    r_hid = [R("hid", i) for i in range(cfg.FSUB)]
    r_w = [R("w", i) for i in range(NW)]
    r_rb = [R("rb", i) for i in range(2)]
    r_ob = [R("ob", i) for i in range(2)]
    r_sq = [R("sq", i) for i in range(2)]
    r_rs = R("rstd")
    cnt = {"w": 0, "rb": 0, "ob": 0, "sq": 0, "pa": 0, "pg": 0, "sg": 0}

    def nxt(k, n):
        v = cnt[k] % n
        cnt[k] += 1
        return v

    def wload(src_ap, n):
        s = nxt("w", NW)
        P.dma("sp", lambda h, s=s, n=n, src_ap=src_ap: h.dma_start(out=r32(wsl[s][:, 0:n]), in_=r32(src_ap)), writes=[r_w[s]])
        return s

    def tsl(ti):
        return slice(ti * T, (ti + 1) * T)

    def load_tile(src, nch, ti, rsrc):
        for c in range(nch):
            P.dma("pool", lambda h, c=c: h.dma_start(out=r32(nT[:, c, :]), in_=r32(src[c, :, tsl(ti)])),
                  reads=[R(rsrc, c, ti)], writes=[r_nT[c]])

    def sumsq(nch, Dn):
        for c in range(nch):
            b = nxt("sq", 2)
            P.op("act", lambda h, c=c, b=b: h.activation(out=r32(sq[b]), in_=nT[:, c, :], func=AF.Square),
                 reads=[r_nT[c]], writes=[r_sq[b]])
            P.op("pe", lambda h, c=c, b=b: h.matmul(ps[6][:], r32(ones[:]), r32(sq[b]), start=(c == 0), stop=(c == nch - 1)),
                 reads=[r_sq[b], r_c], writes=[r_ps[6]])
        P.op("act", lambda h: h.activation(out=rstd, in_=ps[6][:], func=AF.Sqrt, bias=EPS, scale=1.0 / Dn),
             reads=[r_ps[6]], writes=[r_rs])
        P.op("dve", lambda h: h.reciprocal(out=rstd, in_=rstd), reads=[r_rs], writes=[r_rs])

    def modnorm(jsh, jsc):
        sumsq(DC, D)
        for c in range(DC):
            P.op("dve", lambda h, c=c: h.scalar_tensor_tensor(out=r32(nT[:, c, :]), in0=nT[:, c, :], scalar=mcol(jsc, c), in1=rstd,
                                                              op0=ALU.mult, op1=ALU.mult),
                 reads=[r_nT[c], r_rs, r_mod], writes=[r_nT[c]])
            P.op("act", lambda h, c=c: h.activation(out=r32(nT[:, c, :]), in_=nT[:, c, :], func=AF.Identity,
                                                    bias=mcol(jsh, c), scale=1.0),
                 reads=[r_nT[c], r_mod], writes=[r_nT[c]])

    def ffn(ti, wgd, wud, wdd, jg, src0, rsrc0):
        for s, (f0, f1) in enumerate(cfg.fsplit):
            last = s == cfg.NS - 1
            for f in range(f0, f1):
                sg_ = wload(wgd[f], DC * 128)
                su_ = wload(wud[f], DC * 128)
                pg = 2 * nxt("pg", 2)
                for kc in range(DC):
                    P.op("pe", lambda h, kc=kc, sg_=sg_, pg=pg: h.matmul(ps[pg][:], r32(wsl[sg_][:, kc * 128:(kc + 1) * 128]),
                                                                       r32(nT[:, kc, :]), start=(kc == 0), stop=(kc == DC - 1)),
                         reads=[r_w[sg_], r_nT[kc]], writes=[r_ps[pg]])
                for kc in range(DC):
                    P.op("pe", lambda h, kc=kc, su_=su_, pg=pg: h.matmul(ps[pg + 1][:], r32(wsl[su_][:, kc * 128:(kc + 1) * 128]),
                                                                       r32(nT[:, kc, :]), start=(kc == 0), stop=(kc == DC - 1)),
                         reads=[r_w[su_], r_nT[kc]], writes=[r_ps[pg + 1]])
                b = nxt("sg", 2)
                P.op("act", lambda h, pg=pg, b=b: h.activation(out=sgb[b], in_=ps[pg][:], func=AF.Silu),
                     reads=[r_ps[pg]], writes=[r_sg[b]])
                P.op("dve", lambda h, pg=pg, b=b, f=f, f0=f0: h.tensor_tensor(out=r32(hid[:, f - f0, :]), in0=sgb[b], in1=ps[pg + 1][:],
                                                                             op=ALU.mult),
                     reads=[r_sg[b], r_ps[pg + 1]], writes=[r_hid[f - f0]])
            nf = f1 - f0
            for m in range(DC):
                sd = wload(wdd[m][:, f0 * 128:f1 * 128], nf * 128)
                pa = 4 + nxt("pa", 2)
                for i in range(nf):
                    P.op("pe", lambda h, i=i, sd=sd, pa=pa: h.matmul(ps[pa][:], r32(wsl[sd][:, i * 128:(i + 1) * 128]), r32(hid[:, i, :]),
                                                                    start=(i == 0), stop=(i == nf - 1)),
                         reads=[r_w[sd], r_hid[i]], writes=[r_ps[pa]])
                rbi = nxt("rb", 2)
                if s == 0:
                    P.dma("pool", lambda h, m=m, rbi=rbi: h.dma_start(out=rb[rbi], in_=src0[m, :, tsl(ti)]),
                          reads=[R(rsrc0, m, ti)], writes=[r_rb[rbi]])
                else:
                    P.dma("pool", lambda h, m=m, rbi=rbi: h.dma_start(out=rb[rbi], in_=res[m, :, tsl(ti)]),
                          reads=[R("res", m, ti)], writes=[r_rb[rbi]])
                if last:
                    P.op("dve", lambda h, m=m, pa=pa, rbi=rbi: h.scalar_tensor_tensor(
                        out=r32(nT[:, m, :]), in0=ps[pa][:], scalar=mcol(jg, m), in1=rb[rbi], op0=ALU.mult, op1=ALU.add),
                        reads=[r_ps[pa], r_rb[rbi], r_mod], writes=[r_nT[m]])
                    P.dma("pool", lambda h, m=m: h.dma_start(out=res[m, :, tsl(ti)], in_=nT[:, m, :]),
                          reads=[r_nT[m]], writes=[R("res", m, ti)])
                else:
                    obi = nxt("ob", 2)
                    P.op("dve", lambda h, m=m, pa=pa, rbi=rbi, obi=obi: h.scalar_tensor_tensor(
                        out=ob[obi], in0=ps[pa][:], scalar=mcol(jg, m), in1=rb[rbi], op0=ALU.mult, op1=ALU.add),
                        reads=[r_ps[pa], r_rb[rbi], r_mod], writes=[r_ob[obi]])
                    P.dma("pool", lambda h, m=m, obi=obi: h.dma_start(out=res[m, :, tsl(ti)], in_=ob[obi]),
                          reads=[r_ob[obi]], writes=[R("res", m, ti)])

    def proj_fm(ti, wsrc, nk, dst, rdst, oc, rhs_nk=None):
        sw = wload(wsrc, nk * 128)
        pa = 4 + nxt("pa", 2)
        for kc in range(nk):
            P.op("pe", lambda h, kc=kc, sw=sw, pa=pa: h.matmul(ps[pa][:], r32(wsl[sw][:, kc * 128:(kc + 1) * 128]), r32(nT[:, kc, :]),
                                                              start=(kc == 0), stop=(kc == nk - 1)),
                 reads=[r_w[sw], r_nT[kc]], writes=[r_ps[pa]])
        return pa

    if "A" in stages:
        for ti in range(NT):
            load_tile(xT, DC, ti, "xin")
            modnorm(0, 1)
            ffn(ti, wg[0], wu[0], wd[0], 2, xT, "xin")
            modnorm(3, 4)
            dsts = ([(qTs, "qT", i) for i in range(KC)] + [(kTs, "kT", i) for i in range(KC)]
                    + [(gTs, "gT", i) for i in range(GC)] + [(uTs, "uT", i) for i in range(SC)])
            for oc, (dst, rn, i) in enumerate(dsts):
                pa = proj_fm(ti, win_fm[oc], DC, dst, rn, i)
                obi = nxt("ob", 2)
                P.op("act", lambda h, pa=pa, obi=obi: h.activation(out=ob[obi], in_=ps[pa][:], func=AF.Copy),
                     reads=[r_ps[pa]], writes=[r_ob[obi]])
                P.dma("pool", lambda h, dst=dst, i=i, obi=obi, ti=ti: h.dma_start(out=dst[i, :, tsl(ti)], in_=ob[obi]),
                      reads=[r_ob[obi]], writes=[R(rn, i, ti)])
            sw = wload(win_lr, DC * RK)
            pa = 4 + nxt("pa", 2)
            for kc in range(DC):
                P.op("pe", lambda h, kc=kc, sw=sw, pa=pa: h.matmul(ps[pa][0:RK, :], r32(wsl[sw][:, kc * RK:(kc + 1) * RK]), r32(nT[:, kc, :]),
                                                                  start=(kc == 0), stop=(kc == DC - 1)),
                     reads=[r_w[sw], r_nT[kc]], writes=[r_ps[pa]])
            obi = nxt("ob", 2)
            P.op("act", lambda h, pa=pa, obi=obi: h.activation(out=ob[obi][0:RK, :], in_=ps[pa][0:RK, :], func=AF.Copy),
                 reads=[r_ps[pa]], writes=[r_ob[obi]])
            P.dma("pool", lambda h, obi=obi, ti=ti: h.dma_start(out=lrs[:, tsl(ti)], in_=ob[obi][0:RK, :]),
                  reads=[r_ob[obi]], writes=[R("lr", ti)])
            KG = 8 if DC >= 8 else DC
            for blk in range(NTMB):
                for kg in range(DC // KG):
                    sw = wload(win_tm[blk][:, kg * KG * 512:(kg + 1) * KG * 512], KG * 512)
                    for tc in range(T // 128):
                        for k8 in range(KG):
                            kc = kg * KG + k8
                            P.op("pe", lambda h, tc=tc, k8=k8, kc=kc, sw=sw: h.matmul(
                                ps[tc][:], r32(nT[:, kc, tc * 128:(tc + 1) * 128]), r32(wsl[sw][:, k8 * 512:(k8 + 1) * 512]),
                                start=(kc == 0), stop=(kc == DC - 1)),
                                reads=[r_w[sw], r_nT[kc]], writes=[r_ps[tc]])
                for tc in range(T // 128):
                    obi = nxt("ob", 2)
                    P.op("act", lambda h, tc=tc, obi=obi: h.activation(out=ob[obi], in_=ps[tc][:], func=AF.Copy),
                         reads=[r_ps[tc]], writes=[r_ob[obi]])
                    r0 = ti * T + tc * 128
                    if blk < KW // 512:
                        P.dma("pool", lambda h, obi=obi, r0=r0, blk=blk: h.dma_start(out=ktm[r0:r0 + 128, blk * 512:(blk + 1) * 512], in_=ob[obi]),
                              reads=[r_ob[obi]], writes=[R("ktm", ti)])
                    else:
                        b2 = blk - KW // 512
                        P.dma("pool", lambda h, obi=obi, r0=r0, b2=b2: h.dma_start(out=vtm[r0:r0 + 128, b2 * 512:(b2 + 1) * 512], in_=ob[obi]),
                              reads=[r_ob[obi]], writes=[R("vtm", ti)])
        P.barrier()

    if "M" in stages:
        DK, DV = cfg.DK, cfg.DV
        NCH = GB // CH
        NBL = NTOK // GB
        SDIM = KC2 * H * DV
        o = 0
        def take(n):
            nonlocal o
            v = o
            o += n
            return v
        qTb = AV(take(KC * GB), KC * GB, GB)
        kTb = AV(take(KC * GB), KC * GB, GB)
        cum = AV(take(KC * GB), KC * GB, GB)
        eq = AV(take(KC * GB), KC * GB, GB)
        gate = AV(take(GC * GB), GC * GB, GB)
        oT = AV(take(GC * GB), GC * GB, GB)
        ktmb = AV(take(NCH * KW), NCH * KW, KW)
        vtmb = AV(take(NCH * GW), NCH * GW, GW)
        gtm = AV(take(KW), KW)
        kdec = AV(take(KW), KW)
        attT = [AV(take(128), 128) for _ in range(2)]
        S_r = AV(take(SDIM), SDIM)
        glr = AV(take(GB), GB)
        wgk = AV(take(KW), KW)
        bgk = AV(take(KW), KW)
        sqm = [AV(take(GB), GB) for _ in range(2)]
        assert o <= AW, o
        of = 0
        def takef(n):
            nonlocal of
            v = of
            of += n
            return v
        S = AF_(takef(SDIM), SDIM)
        rstm = AF_(takef(GB), GB)
        zc = AF_(takef(2 * NGP), 2 * NGP)
        NPAR = 14
        s5p = AF_(takef(NPAR * NGP), NPAR * NGP, NGP)
        tiny = AF_(takef(8), 8)
        assert of <= AWF, of
        r_S = [R("S", h) for h in range(H)]
        r_Sr = [R("Sr", h) for h in range(H)]
        r_zc = R("zc")
        r_s5p = R("s5p")
        L_R, L_I, L_DT, P_R, P_TH, P_FR, P_FI, P_CN, P_SN, P_T0, P_T1, P_T2, P_T3, P_T4 = range(14)
        PI = math.pi

        def pcol(k, gp):
            return s5p[:, k, gp:gp + 1]


        MAGIC = 12582912.0
        def sincos(xa, tb_, o_sin, o_cos, rx, rt, ro, typed):
            cast = r32 if typed else (lambda a: a)
            for which, dst in ((0, o_sin), (1, o_cos)):
                if which == 1:
                    P.op("dve", lambda h: h.tensor_scalar_add(out=xa, in0=xa, scalar1=0.5 * PI), reads=[rx], writes=[rx])
                P.op("dve", lambda h: h.tensor_scalar(out=tb_, in0=xa, scalar1=1.0 / (2 * PI), scalar2=MAGIC, op0=ALU.mult, op1=ALU.add),
                     reads=[rx], writes=[rt])
                P.op("dve", lambda h: h.tensor_scalar(out=tb_, in0=tb_, scalar1=-MAGIC, scalar2=-2 * PI, op0=ALU.add, op1=ALU.mult),
                     reads=[rt], writes=[rt])
                P.op("pool", lambda h: h.tensor_tensor(out=tb_, in0=tb_, in1=xa, op=ALU.add), reads=[rt, rx], writes=[rt])
                P.op("dve", lambda h: h.tensor_scalar(out=tb_, in0=tb_, scalar1=-PI, scalar2=PI, op0=ALU.max, op1=ALU.min),
                     reads=[rt], writes=[rt])
                if typed:
                    P.op("act", lambda h: h.activation(out=tb_, in_=tb_, func=AF.Sin), reads=[rt], writes=[rt])
                    P.op("dve", lambda h, dst=dst: h.tensor_copy(out=r32(dst), in_=tb_), reads=[rt], writes=[ro])
                else:
                    P.op("act", lambda h, dst=dst: h.activation(out=dst, in_=tb_, func=AF.Sin), reads=[rt], writes=[ro])

        def s5_params():
            P.dma("sp", lambda h: h.dma_start(out=s5p[:, 0:3, :], in_=s5lam.rearrange("p (k g) -> p k g", k=3)), writes=[r_s5p])
            def A(fn):
                P.op("act", fn, reads=[r_s5p], writes=[r_s5p])
            def V(fn):
                P.op("pool", fn, reads=[r_s5p], writes=[r_s5p])
            A(lambda h: h.activation(out=s5p[:, L_DT, :], in_=s5p[:, L_DT, :], func=AF.Exp))
            V(lambda h: h.tensor_tensor(out=s5p[:, P_R, :], in0=s5p[:, L_R, :], in1=s5p[:, L_DT, :], op=ALU.mult))
            A(lambda h: h.activation(out=s5p[:, P_R, :], in_=s5p[:, P_R, :], func=AF.Exp))
            V(lambda h: h.tensor_tensor(out=s5p[:, P_TH, :], in0=s5p[:, L_I, :], in1=s5p[:, L_DT, :], op=ALU.mult))
            V(lambda h: h.tensor_copy(out=s5p[:, P_T2, :], in_=s5p[:, P_TH, :]))
            sincos(s5p[:, P_T2, :], s5p[:, P_T3, :], s5p[:, P_T1, :], s5p[:, P_T0, :], r_s5p, r_s5p, r_s5p, False)
            V(lambda h: h.tensor_tensor(out=s5p[:, P_T2, :], in0=s5p[:, P_R, :], in1=s5p[:, P_T0, :], op=ALU.mult))
            V(lambda h: h.tensor_scalar_add(out=s5p[:, P_T2, :], in0=s5p[:, P_T2, :], scalar1=-1.0))
            V(lambda h: h.tensor_tensor(out=s5p[:, P_T3, :], in0=s5p[:, P_R, :], in1=s5p[:, P_T1, :], op=ALU.mult))
            V(lambda h: h.tensor_tensor(out=s5p[:, P_T4, :], in0=s5p[:, L_R, :], in1=s5p[:, L_R, :], op=ALU.mult))
            V(lambda h: h.tensor_tensor(out=s5p[:, P_T0, :], in0=s5p[:, L_I, :], in1=s5p[:, L_I, :], op=ALU.mult))
            V(lambda h: h.tensor_tensor(out=s5p[:, P_T4, :], in0=s5p[:, P_T4, :], in1=s5p[:, P_T0, :], op=ALU.add))
            P.op("dve", lambda h: h.reciprocal(out=s5p[:, P_T4, :], in_=s5p[:, P_T4, :]), reads=[r_s5p], writes=[r_s5p])
            V(lambda h: h.tensor_tensor(out=s5p[:, P_FR, :], in0=s5p[:, P_T2, :], in1=s5p[:, L_R, :], op=ALU.mult))
            V(lambda h: h.tensor_tensor(out=s5p[:, P_T0, :], in0=s5p[:, P_T3, :], in1=s5p[:, L_I, :], op=ALU.mult))
            V(lambda h: h.tensor_tensor(out=s5p[:, P_FR, :], in0=s5p[:, P_FR, :], in1=s5p[:, P_T0, :], op=ALU.add))
            V(lambda h: h.tensor_tensor(out=s5p[:, P_FR, :], in0=s5p[:, P_FR, :], in1=s5p[:, P_T4, :], op=ALU.mult))
            V(lambda h: h.tensor_tensor(out=s5p[:, P_FI, :], in0=s5p[:, P_T3, :], in1=s5p[:, L_R, :], op=ALU.mult))
            V(lambda h: h.tensor_tensor(out=s5p[:, P_T0, :], in0=s5p[:, P_T2, :], in1=s5p[:, L_I, :], op=ALU.mult))
            V(lambda h: h.tensor_tensor(out=s5p[:, P_FI, :], in0=s5p[:, P_FI, :], in1=s5p[:, P_T0, :], op=ALU.subtract))
            V(lambda h: h.tensor_tensor(out=s5p[:, P_FI, :], in0=s5p[:, P_FI, :], in1=s5p[:, P_T4, :], op=ALU.mult))
            V(lambda h: h.tensor_scalar(out=s5p[:, P_T2, :], in0=s5p[:, P_TH, :], scalar1=512.0, scalar2=None, op0=ALU.mult))
            sincos(s5p[:, P_T2, :], s5p[:, P_T3, :], s5p[:, P_SN, :], s5p[:, P_CN, :], r_s5p, r_s5p, r_s5p, False)

        def gla_pass(state_only):
            r_q, r_k, r_cum, r_eq, r_gate, r_oT = R("qTb"), R("kTb"), R("cum"), R("eqb"), R("gateb"), R("oTb")
            r_ktm, r_vtm, r_gtm, r_kdec, r_glr = R("ktmb"), R("vtmb"), R("gtm"), R("kdec"), R("glr")
            r_att = [R("att", i) for i in range(2)]
            r_sqm = [R("sqm", i) for i in range(2)]
            r_rstm = R("rstm")
            r_wgk = R("wgk")
            P.dma("sp", lambda h: h.dma_start(out=r32(wgk[0:RK, :]), in_=r32(w_gk2)), writes=[r_wgk])
            P.dma("sp", lambda h: h.dma_start(out=r32(bgk[0:1, :]), in_=r32(b_gkr)), writes=[r_wgk])
            kcnt = [0]
            def pb():
                kcnt[0] += 1
                return kcnt[0] % 4
            for bi in range(NBL):
                t0 = bi * GB
                tb = slice(t0, t0 + GB)
                ti_src = t0 // T
                if not state_only:
                    P.dma("sp", lambda h, tb=tb: h.dma_start(out=r32(qTb), in_=r32(qTs[:, :, tb].rearrange("c p t -> p c t"))),
                          reads=[R("qT", i, ti_src) for i in range(KC)], writes=[r_q])
                    P.dma("sp", lambda h, tb=tb: h.dma_start(out=r32(kTb), in_=r32(kTs[:, :, tb].rearrange("c p t -> p c t"))),
                          reads=[R("kT", i, ti_src) for i in range(KC)], writes=[r_k])
                    P.dma("sp", lambda h, tb=tb: h.dma_start(out=r32(gate), in_=r32(gTs[:, :, tb].rearrange("c p t -> p c t"))),
                          reads=[R("gT", i, ti_src) for i in range(GC)], writes=[r_gate])
                P.dma("sp", lambda h, tb=tb: h.dma_start(out=r32(ktmb), in_=r32(ktm[tb, :].rearrange("(c p) w -> p c w", p=128))),
                      reads=[R("ktm", ti_src)], writes=[r_ktm])
                P.dma("sp", lambda h, tb=tb: h.dma_start(out=r32(vtmb), in_=r32(vtm[tb, :].rearrange("(c p) w -> p c w", p=128))),
                      reads=[R("vtm", ti_src)], writes=[r_vtm])
                P.dma("sp", lambda h, tb=tb: h.dma_start(out=r32(glr[0:RK, :]), in_=r32(lrs[:, tb])), reads=[R("lr", ti_src)], writes=[r_glr])
                for kc in range(KC):
                    b = pb()
                    P.op("pe", lambda h, kc=kc, b=b: h.matmul(ps[b][:, 0:GB], r32(wgk[0:RK, kc * 128:(kc + 1) * 128]), r32(glr[0:RK, :]),
                                                              start=True, stop=True), reads=[r_wgk, r_glr], writes=[r_ps[b]])
                    P.op("act", lambda h, kc=kc, b=b: h.activation(out=r32(eq[:, kc, :]), in_=ps[b][:, 0:GB], func=AF.Sigmoid,
                                                                   bias=vcol(cfg.V_BGK, kc), scale=1.0),
                         reads=[r_ps[b], r_vec], writes=[r_eq])
                P.op("act", lambda h: h.activation(out=r32(eq), in_=eq, func=AF.Ln), reads=[r_eq], writes=[r_eq])
                for kc in range(KC):
                    P.op("dve", lambda h, kc=kc: h.tensor_tensor_scan(out=r32(cum[:, kc, :]), data0=rmask[:, 0:GB], data1=eq[:, kc, :],
                                                                      initial=0.0, op0=ALU.mult, op1=ALU.add),
                         reads=[r_eq, r_c], writes=[r_cum])
                P.op("act", lambda h: h.activation(out=r32(eq), in_=cum, func=AF.Exp, scale=1.0 / 16.0), reads=[r_cum], writes=[r_eq])
                if not state_only:
                    P.op("act", lambda h: h.activation(out=r32(cum), in_=cum, func=AF.Exp, scale=-1.0 / 16.0), reads=[r_cum], writes=[r_cum])
                    P.op("dve", lambda h: h.scalar_tensor_tensor(out=r32(qTb), in0=qTb, scalar=float(DK) ** -0.5, in1=eq,
                                                                 op0=ALU.mult, op1=ALU.mult), reads=[r_q, r_eq], writes=[r_q])
                    P.op("pool", lambda h: h.tensor_tensor(out=r32(kTb), in0=kTb, in1=cum, op=ALU.mult), reads=[r_k, r_cum], writes=[r_k])
                for c in range(NCH):
                    cs = slice(c * CH, (c + 1) * CH)
                    for hf in range(KW // 512):
                        b = pb()
                        hs = slice(hf * 512, (hf + 1) * 512)
                        P.op("pe", lambda h, cs=cs, hs=hs, b=b: h.matmul(ps[b][:], r32(glr[0:RK, cs]), r32(wgk[0:RK, hs]), start=True, stop=False),
                             reads=[r_glr, r_wgk], writes=[r_ps[b]])
                        P.op("pe", lambda h, hs=hs, b=b: h.matmul(ps[b][:], r32(ones[0:1, :]), r32(bgk[0:1, hs]), start=False, stop=True),
                             reads=[r_c, r_wgk], writes=[r_ps[b]])
                        P.op("act", lambda h, hs=hs, b=b: h.activation(out=r32(gtm[:, hs]), in_=ps[b][:], func=AF.Sigmoid),
                             reads=[r_ps[b]], writes=[r_gtm])
                    P.op("act", lambda h: h.activation(out=r32(gtm), in_=gtm, func=AF.Ln), reads=[r_gtm], writes=[r_gtm])
                    for hf in range(KW // 512):
                        b = pb()
                        hs = slice(hf * 512, (hf + 1) * 512)
                        P.op("pe", lambda h, hs=hs, b=b: h.matmul(ps[b][:], r32(Umask[:]), r32(gtm[:, hs]), start=True, stop=True),
                             reads=[r_c, r_gtm], writes=[r_ps[b]])
                        P.op("act", lambda h, hs=hs, b=b: h.activation(out=r32(kdec[:, hs]), in_=ps[b][:], func=AF.Exp, scale=1.0 / 16.0),
                             reads=[r_ps[b]], writes=[r_kdec])
                    P.op("dve", lambda h, c=c: h.tensor_tensor(out=r32(kdec), in0=kdec, in1=ktmb[:, c, :], op=ALU.mult),
                         reads=[r_kdec, r_ktm], writes=[r_kdec])
                    for hh in range(H):
                        if not state_only:
                            b = pb()
                            for k2 in range(KC2):
                                P.op("pe", lambda h, hh=hh, k2=k2, cs=cs, b=b: h.matmul(
                                    ps[b][:, 0:CH], r32(kTb[:, hh * KC2 + k2, cs]), r32(qTb[:, hh * KC2 + k2, cs]),
                                    start=(k2 == 0), stop=(k2 == KC2 - 1)), reads=[r_k, r_q], writes=[r_ps[b]])
                            ai = (c * H + hh) % 2
                            P.op("dve", lambda h, b=b, ai=ai: h.tensor_tensor(out=r32(attT[ai]), in0=ps[b][:, 0:CH], in1=maskT[:], op=ALU.mult),
                                 reads=[r_ps[b], r_c], writes=[r_att[ai]])
                            b2 = pb()
                            for vc in range(VC):
                                osl = slice(vc * 128, (vc + 1) * 128)
                                P.op("pe", lambda h, hh=hh, vc=vc, c=c, ai=ai, b2=b2, osl=osl: h.matmul(
                                    ps[b2][:, osl], r32(vtmb[:, c, hh * DV + vc * 128: hh * DV + (vc + 1) * 128]), r32(attT[ai]),
                                    start=True, stop=False), reads=[r_vtm, r_att[ai]], writes=[r_ps[b2]])
                                for k2 in range(KC2):
                                    so = (k2 * H + hh) * DV + vc * 128
                                    P.op("pe", lambda h, hh=hh, k2=k2, cs=cs, b2=b2, osl=osl, so=so: h.matmul(
                                        ps[b2][:, osl], r32(S_r[:, so:so + 128]), r32(qTb[:, hh * KC2 + k2, cs]),
                                        start=False, stop=(k2 == KC2 - 1)), reads=[r_Sr[hh], r_q], writes=[r_ps[b2]])
                            P.op("act", lambda h, hh=hh, cs=cs, b2=b2: h.activation(
                                out=r32(oT[:, hh * VC:(hh + 1) * VC, cs]), in_=ps[b2][:, 0:VC * 128].rearrange("p (v t) -> p v t", t=128),
                                func=AF.Copy), reads=[r_ps[b2]], writes=[r_oT])
                        for k2 in range(KC2):
                            b = pb()
                            so = (k2 * H + hh) * DV
                            P.op("pe", lambda h, hh=hh, k2=k2, c=c, b=b: h.matmul(
                                ps[b][:, 0:DV], r32(kdec[:, hh * DK + k2 * 128: hh * DK + (k2 + 1) * 128]), r32(vtmb[:, c, hh * DV:(hh + 1) * DV]),
                                start=True, stop=True), reads=[r_kdec, r_vtm], writes=[r_ps[b]])
                            col = c * CH + CH - 1
                            P.op("dve", lambda h, hh=hh, k2=k2, b=b, so=so, col=col: h.scalar_tensor_tensor(
                                out=S[:, so:so + DV], in0=S[:, so:so + DV], scalar=eq[:, hh * KC2 + k2, col:col + 1], in1=ps[b][:, 0:DV],
                                op0=ALU.mult, op1=ALU.add), reads=[r_S[hh], r_eq, r_ps[b]], writes=[r_S[hh]])
                            P.op("pool", lambda h, so=so: h.tensor_copy(out=r32(S_r[:, so:so + DV]), in_=S[:, so:so + DV]),
                                 reads=[r_S[hh]], writes=[r_Sr[hh]])
                if state_only:
                    continue
                P.op("act", lambda h: h.activation(out=r32(gate), in_=gate, func=AF.Silu), reads=[r_gate], writes=[r_gate])
                for hh in range(H):
                    b = pb()
                    for vc in range(VC):
                        si = vc % 2
                        P.op("act", lambda h, hh=hh, vc=vc, si=si: h.activation(out=r32(sqm[si]), in_=oT[:, hh * VC + vc, :], func=AF.Square),
                             reads=[r_oT], writes=[r_sqm[si]])
                        P.op("pe", lambda h, vc=vc, si=si, b=b: h.matmul(ps[b][:, 0:GB], r32(ones[:]), r32(sqm[si]), start=(vc == 0), stop=(vc == VC - 1)),
                             reads=[r_sqm[si], r_c], writes=[r_ps[b]])
                    P.op("act", lambda h, b=b: h.activation(out=rstm, in_=ps[b][:, 0:GB], func=AF.Sqrt, bias=EPS, scale=1.0 / DV),
                         reads=[r_ps[b]], writes=[r_rstm])
                    P.op("dve", lambda h: h.reciprocal(out=rstm, in_=rstm), reads=[r_rstm], writes=[r_rstm])
                    for vc in range(VC):
                        ch = hh * VC + vc
                        P.op("dve", lambda h, ch=ch: h.tensor_tensor(out=r32(oT[:, ch, :]), in0=oT[:, ch, :], in1=rstm, op=ALU.mult),
                             reads=[r_oT, r_rstm], writes=[r_oT])
                        P.op("dve", lambda h, ch=ch, vc=vc: h.scalar_tensor_tensor(out=r32(oT[:, ch, :]), in0=oT[:, ch, :], scalar=vcol(cfg.V_GNW, vc),
                                                                                     in1=gate[:, ch, :], op0=ALU.mult, op1=ALU.mult),
                             reads=[r_oT, r_gate, r_vec], writes=[r_oT])
                P.dma("pool", lambda h, tb=tb: h.dma_start(out=mTs[0:GC, :, tb].rearrange("c p t -> p c t"), in_=oT),
                      reads=[r_oT], writes=[R("mT", i, ti_src) for i in range(MC)])

        S5DBG = int(os.environ.get("S5DBG", "9"))

        def s5_pass(state_only):
            NB = 512
            NBK = NTOK // NB
            o5 = [0]
            def t5(n):
                v = o5[0]
                o5[0] += n
                return v
            uTc = AV(t5(NTOK), NTOK)
            EinR, EinI, Ec, Es = (AV(t5(NB), NB) for _ in range(4))
            wks = [[AV(t5(NB), NB) for _ in range(10)] for _ in range(2)]
            wk = wks[0]
            tinys = [tiny[:, 0:2], tiny[:, 2:4]]
            r_tinys = [R("tiny", 0), R("tiny", 1)]
            rtab = AV(t5(NB), NB)
            r_rtab = R("rtab")
            Bw = [AV(t5(256), 256) for _ in range(2)]
            Cw = [AV(t5(256), 256) for _ in range(2)]
            r_u, r_tab = R("uTc"), R("s5tab")
            angf, angg = AF_(0, NB), AF_(NB, NB)
            r_angf, r_angg = R("angf"), R("angg")
            r_wks = [[R("s5wk", j, i) for i in range(10)] for j in range(2)]
            r_wk = r_wks[0]
            r_bw = [R("s5bw", i) for i in range(2)]
            r_cw = [R("s5cw", i) for i in range(2)]
            for fc in range(SC):
                P.dma("sp", lambda h, fc=fc: h.dma_start(out=r32(uTc), in_=r32(uTs[fc])), reads=[R("uT", fc, ti) for ti in range(NT)], writes=[r_u])
                for g4 in range(4):
                    gp = fc * 4 + g4
                    wb = gp % 2
                    P.dma("sp", lambda h, gp=gp, wb=wb: h.dma_start(out=r32(Bw[wb]), in_=r32(s5B[gp])), writes=[r_bw[wb]])
                    if not state_only:
                        P.dma("sp", lambda h, gp=gp, wb=wb: h.dma_start(out=r32(Cw[wb]), in_=r32(s5C[gp])), writes=[r_cw[wb]])
                    TW = [r_tab, r_s5p]
                    P.op("act", lambda h, gp=gp: h.activation(out=angf, in_=iot[:, 0:NB], func=AF.Identity, scale=pcol(P_TH, gp)),
                         reads=[r_c, r_s5p], writes=[r_angf])
                    sincos(angf, angg, Es, Ec, r_angf, r_angg, r_tab, True)
                    P.op("act", lambda h, gp=gp: h.activation(out=r32(rtab), in_=iot[:, 0:NB], func=AF.Identity, bias=pcol(P_R, gp), scale=0.0),
                         reads=[r_c, r_s5p], writes=[r_rtab])
                    P.op("act", lambda h, gp=gp: h.activation(out=r32(wk[0]), in_=Es, func=AF.Identity, scale=pcol(P_FI, gp)),
                         reads=[r_tab, r_s5p], writes=[r_wk[0]])
                    P.op("dve", lambda h, gp=gp: h.scalar_tensor_tensor(out=r32(EinR), in0=Ec, scalar=pcol(P_FR, gp), in1=wk[0], op0=ALU.mult, op1=ALU.add),
                         reads=[r_tab, r_s5p, r_wk[0]], writes=[r_tab])
                    P.op("act", lambda h, gp=gp: h.activation(out=r32(wk[1]), in_=Es, func=AF.Identity, scale=pcol(P_FR, gp)),
                         reads=[r_tab, r_s5p], writes=[r_wk[1]])
                    P.op("dve", lambda h, gp=gp: h.scalar_tensor_tensor(out=r32(EinI), in0=Ec, scalar=pcol(P_FI, gp), in1=wk[1], op0=ALU.mult, op1=ALU.subtract),
                         reads=[r_tab, r_s5p, r_wk[1]], writes=[r_tab])
                    def s5_block(bk, wk, r_wk, tn, r_tn):
                        bs = slice(bk * NB, (bk + 1) * NB)
                        pr, pi_ = (0, 1) if (gp * NBK + bk) % 2 == 0 else (2, 3)
                        P.op("pe", lambda h, wb=wb, bs=bs, pr=pr: h.matmul(ps[pr][:], r32(Bw[wb][:, 0:128]), r32(uTc[:, bs]), start=True, stop=True),
                             reads=[r_bw[wb], r_u], writes=[r_ps[pr]])
                        P.op("pe", lambda h, wb=wb, bs=bs, pi_=pi_: h.matmul(ps[pi_][:], r32(Bw[wb][:, 128:256]), r32(uTc[:, bs]), start=True, stop=True),
                             reads=[r_bw[wb], r_u], writes=[r_ps[pi_]])
                        e1, e2 = ("dve", "pool")
                        P.op("act", lambda h, pr=pr: h.activation(out=r32(wk[4]), in_=ps[pr][:], func=AF.Copy), reads=[r_ps[pr]], writes=[r_wk[4]])
                        P.op("act", lambda h, pi_=pi_: h.activation(out=r32(wk[5]), in_=ps[pi_][:], func=AF.Copy), reads=[r_ps[pi_]], writes=[r_wk[5]])
                        P.op("dve", lambda h: h.tensor_tensor(out=r32(wk[2]), in0=wk[4], in1=EinR, op=ALU.mult), reads=[r_wk[4], r_tab], writes=[r_wk[2]])
                        P.op("dve", lambda h: h.tensor_tensor(out=r32(wk[3]), in0=wk[5], in1=EinI, op=ALU.mult), reads=[r_wk[5], r_tab], writes=[r_wk[3]])
                        P.op("pool", lambda h: h.tensor_tensor(out=r32(wk[2]), in0=wk[2], in1=wk[3], op=ALU.subtract), reads=[r_wk[2], r_wk[3]], writes=[r_wk[2]])
                        P.op("pool", lambda h: h.tensor_tensor(out=r32(wk[4]), in0=wk[4], in1=EinI, op=ALU.mult), reads=[r_wk[4], r_tab], writes=[r_wk[4]])
                        P.op("pool", lambda h: h.tensor_tensor(out=r32(wk[5]), in0=wk[5], in1=EinR, op=ALU.mult), reads=[r_wk[5], r_tab], writes=[r_wk[5]])
                        P.op("pool", lambda h: h.tensor_tensor(out=r32(wk[4]), in0=wk[4], in1=wk[5], op=ALU.add), reads=[r_wk[4], r_wk[5]], writes=[r_wk[4]])
                        P.op("dve", lambda h, gp=gp: h.scalar_tensor_tensor(out=r32(wk[2][:, 0:1]), in0=zc[:, 2 * gp:2 * gp + 1], scalar=pcol(P_R, gp),
                                                                            in1=wk[2][:, 0:1], op0=ALU.mult, op1=ALU.add),
                             reads=[r_wk[2], r_s5p, r_zc], writes=[r_wk[2]])
                        P.op("dve", lambda h, gp=gp: h.scalar_tensor_tensor(out=r32(wk[4][:, 0:1]), in0=zc[:, 2 * gp + 1:2 * gp + 2], scalar=pcol(P_R, gp),
                                                                            in1=wk[4][:, 0:1], op0=ALU.mult, op1=ALU.add),
                             reads=[r_wk[4], r_s5p, r_zc], writes=[r_wk[4]])
                        P.op("dve", lambda h: h.tensor_tensor_scan(out=r32(wk[6]), data0=rtab, data1=wk[2], initial=0.0, op0=ALU.mult, op1=ALU.add),
                             reads=[r_wk[2], r_rtab], writes=[r_wk[6]])
                        P.op("dve", lambda h: h.tensor_tensor_scan(out=r32(wk[7]), data0=rtab, data1=wk[4], initial=0.0, op0=ALU.mult, op1=ALU.add),
                             reads=[r_wk[4], r_rtab], writes=[r_wk[7]])
                        zl = NB - 1
                        P.op("act", lambda h, gp=gp: h.activation(out=tn[:, 0:1], in_=wk[7][:, zl:zl + 1], func=AF.Identity, scale=pcol(P_SN, gp)),
                             reads=[r_wk[7], r_s5p], writes=[r_tn])
                        P.op("act", lambda h, gp=gp: h.activation(out=tn[:, 1:2], in_=wk[6][:, zl:zl + 1], func=AF.Identity, scale=pcol(P_SN, gp)),
                             reads=[r_wk[6], r_s5p], writes=[r_tn])
                        P.op("dve", lambda h, gp=gp: h.scalar_tensor_tensor(out=zc[:, 2 * gp:2 * gp + 1], in0=wk[6][:, zl:zl + 1], scalar=pcol(P_CN, gp),
                                                                            in1=tn[:, 0:1], op0=ALU.mult, op1=ALU.subtract),
                             reads=[r_wk[6], r_s5p, r_tn], writes=[r_zc])
                        P.op("dve", lambda h, gp=gp: h.scalar_tensor_tensor(out=zc[:, 2 * gp + 1:2 * gp + 2], in0=wk[7][:, zl:zl + 1], scalar=pcol(P_CN, gp),
                                                                            in1=tn[:, 1:2], op0=ALU.mult, op1=ALU.add),
                             reads=[r_wk[7], r_s5p, r_tn], writes=[r_zc])
                        if state_only or S5DBG == 0:
                            return
                        P.op("pool", lambda h: h.tensor_tensor(out=r32(wk[3]), in0=wk[7], in1=Es, op=ALU.mult), reads=[r_wk[7], r_tab], writes=[r_wk[3]])
                        P.op("dve", lambda h: h.tensor_tensor(out=r32(wk[8]), in0=wk[6], in1=Ec, op=ALU.mult), reads=[r_wk[6], r_tab], writes=[r_wk[8]])
                        P.op("dve", lambda h: h.tensor_tensor(out=r32(wk[8]), in0=wk[8], in1=wk[3], op=ALU.subtract), reads=[r_wk[8], r_wk[3]], writes=[r_wk[8]])
                        if S5DBG == 2:
                            return
                        P.op("pool", lambda h: h.tensor_tensor(out=r32(wk[5]), in0=wk[6], in1=Es, op=ALU.mult), reads=[r_wk[6], r_tab], writes=[r_wk[5]])
                        P.op("dve", lambda h: h.tensor_tensor(out=r32(wk[9]), in0=wk[7], in1=Ec, op=ALU.mult), reads=[r_wk[7], r_tab], writes=[r_wk[9]])
                        if S5DBG == 3:
                            return
                        P.op("pool", lambda h: h.tensor_tensor(out=r32(wk[9]), in0=wk[9], in1=wk[5], op=ALU.add), reads=[r_wk[9], r_wk[5]], writes=[r_wk[9]])
                        P.op("act", lambda h: h.activation(out=r32(wk[9]), in_=wk[9], func=AF.Identity, scale=-1.0), reads=[r_wk[9]], writes=[r_wk[9]])
                        py = 4 + bk
                        if S5DBG == 1:
                            return
                        S5V = os.environ.get("S5V", "")
                        if S5V == "":
                            P.op("pe", lambda h, wb=wb, py=py, g4=g4: h.matmul(ps[py][:], r32(Cw[wb][:, 0:128]), r32(wk[8]), start=(g4 == 0), stop=False),
                                 reads=[r_cw[wb], r_wk[8]], writes=[r_ps[py]])
                            P.op("pe", lambda h, wb=wb, py=py, g4=g4: h.matmul(ps[py][:], r32(Cw[wb][:, 128:256]), r32(wk[9]), start=False, stop=(g4 == 3)),
                                 reads=[r_cw[wb], r_wk[9]], writes=[r_ps[py]])
                        elif S5V == "rhs":
                            P.op("pe", lambda h, wb=wb, py=py, g4=g4, bs=bs: h.matmul(ps[py][:], r32(Cw[wb][:, 0:128]), r32(uTc[:, bs]), start=(g4 == 0), stop=False),
                                 reads=[r_cw[wb], r_u], writes=[r_ps[py]])
                            P.op("pe", lambda h, wb=wb, py=py, g4=g4, bs=bs: h.matmul(ps[py][:], r32(Cw[wb][:, 128:256]), r32(uTc[:, bs]), start=False, stop=(g4 == 3)),
                                 reads=[r_cw[wb], r_u], writes=[r_ps[py]])
                        elif S5V == "sync":
                            P.op("pe", lambda h, wb=wb, py=py, g4=g4, bs=bs: h.matmul(ps[py][:], r32(Cw[wb][:, 0:128]), r32(uTc[:, bs]), start=(g4 == 0), stop=False),
                                 reads=[r_cw[wb], r_u, r_wk[8]], writes=[r_ps[py]])
                            P.op("pe", lambda h, wb=wb, py=py, g4=g4, bs=bs: h.matmul(ps[py][:], r32(Cw[wb][:, 128:256]), r32(uTc[:, bs]), start=False, stop=(g4 == 3)),
                                 reads=[r_cw[wb], r_u, r_wk[9]], writes=[r_ps[py]])
                        elif S5V == "lhs":
                            P.op("pe", lambda h, wb=wb, py=py, g4=g4: h.matmul(ps[py][:], r32(Bw[wb][:, 0:128]), r32(wk[8]), start=(g4 == 0), stop=False),
                                 reads=[r_bw[wb], r_wk[8]], writes=[r_ps[py]])
                            P.op("pe", lambda h, wb=wb, py=py, g4=g4: h.matmul(ps[py][:], r32(Bw[wb][:, 128:256]), r32(wk[9]), start=False, stop=(g4 == 3)),
                                 reads=[r_bw[wb], r_wk[9]], writes=[r_ps[py]])
                    for bk in range(NBK):
                        st_ = (gp * NBK + bk) % 2
                        s5_block(bk, wks[st_], r_wks[st_], tinys[st_], r_tinys[st_])
                if state_only or os.environ.get("S5NOOUT"):
                    continue
                for bk in range(NBK):
                    bs = slice(bk * NB, (bk + 1) * NB)
                    py = 4 + bk
                    P.op("dve", lambda h, fc=fc, bs=bs, py=py: h.scalar_tensor_tensor(out=r32(wk[0]), in0=uTc[:, bs], scalar=vcol(cfg.V_S5D, fc), in1=ps[py][:],
                                                                                      op0=ALU.mult, op1=ALU.add), reads=[r_u, r_vec, r_ps[py]], writes=[r_wk[0]])
                    P.op("act", lambda h: h.activation(out=r32(wk[1]), in_=wk[0], func=AF.Square), reads=[r_wk[0]], writes=[r_wk[1]])
                    P.op("dve", lambda h: h.tensor_scalar(out=r32(wk[1]), in0=wk[1], scalar1=0.044715, scalar2=1.0, op0=ALU.mult, op1=ALU.add),
                         reads=[r_wk[1]], writes=[r_wk[1]])
                    P.op("dve", lambda h: h.tensor_tensor(out=r32(wk[1]), in0=wk[1], in1=wk[0], op=ALU.mult), reads=[r_wk[1], r_wk[0]], writes=[r_wk[1]])
                    P.op("act", lambda h: h.activation(out=r32(wk[1]), in_=wk[1], func=AF.Sigmoid, scale=2.0 * math.sqrt(2.0 / math.pi)), reads=[r_wk[1]], writes=[r_wk[1]])
                    P.op("dve", lambda h: h.tensor_tensor(out=r32(wk[0]), in0=wk[0], in1=wk[1], op=ALU.mult), reads=[r_wk[0], r_wk[1]], writes=[r_wk[0]])
                    P.dma("pool", lambda h, fc=fc, bs=bs: h.dma_start(out=yas[fc, :, bs], in_=wk[0]), reads=[r_wk[0]], writes=[R("ya", fc)])

        def glu():
            y2 = AV(0, SC * 512, 512)
            yat = AV(SC * 512, SC * 512, 512)
            gw = [AV(2 * SC * 512 + i * SC * 128, SC * 128) for i in range(3)]
            r_y2, r_ya2 = [R("y2", i) for i in range(SC)], [R("yat", i) for i in range(SC)]
            r_gw = [R("gw", i) for i in range(3)]
            sqg = [AV(2 * SC * 512 + 3 * SC * 128 + i * 512, 512) for i in range(2)]
            r_sqg = [R("sqg", i) for i in range(2)]
            r_rs2 = R("rsg")
            rsg = AF_(0, 512)
            k = 0
            for bk in range(NTOK // 512):
                bs = slice(bk * 512, (bk + 1) * 512)
                for c in range(SC):
                    P.dma("sp", lambda h, c=c, bs=bs: h.dma_start(out=r32(yat[:, c, :]), in_=r32(yas[c, :, bs])), reads=[R("ya", c)], writes=[r_ya2[c]])
                for oc in range(SC):
                    wi = k % 3
                    k += 1
                    P.dma("sp", lambda h, oc=oc, wi=wi: h.dma_start(out=r32(gw[wi]), in_=r32(w_glu[oc])), writes=[r_gw[wi]])
                    b = k % 4
                    for kc in range(SC):
                        P.op("pe", lambda h, kc=kc, wi=wi, b=b: h.matmul(ps[b][:], r32(gw[wi][:, kc * 128:(kc + 1) * 128]), r32(yat[:, kc, :]),
                                                                        start=(kc == 0), stop=(kc == SC - 1)), reads=[r_gw[wi], r_ya2[kc]], writes=[r_ps[b]])
                    P.op("act", lambda h, oc=oc, b=b: h.activation(out=r32(y2[:, oc, :]), in_=ps[b][:], func=AF.Sigmoid, bias=vcol(cfg.V_BGLU, oc), scale=1.0),
                         reads=[r_ps[b], r_vec], writes=[r_y2[oc]])
                    P.op("dve", lambda h, oc=oc: h.tensor_tensor(out=r32(y2[:, oc, :]), in0=y2[:, oc, :], in1=yat[:, oc, :], op=ALU.mult),
                         reads=[r_y2[oc], r_ya2[oc]], writes=[r_y2[oc]])
                    si = oc % 2
                    P.op("act", lambda h, oc=oc, si=si: h.activation(out=r32(sqg[si]), in_=y2[:, oc, :], func=AF.Square), reads=[r_y2[oc]], writes=[r_sqg[si]])
                    P.op("pe", lambda h, oc=oc, si=si: h.matmul(ps[7][:], r32(ones[:]), r32(sqg[si]), start=(oc == 0), stop=(oc == SC - 1)),
                         reads=[r_sqg[si], r_c], writes=[r_ps[7]])
                P.op("act", lambda h: h.activation(out=rsg, in_=ps[7][:], func=AF.Sqrt, bias=EPS, scale=1.0 / SW), reads=[r_ps[7]], writes=[r_rs2])
                P.op("dve", lambda h: h.reciprocal(out=rsg, in_=rsg), reads=[r_rs2], writes=[r_rs2])
                for oc in range(SC):
                    P.op("dve", lambda h, oc=oc: h.scalar_tensor_tensor(out=r32(y2[:, oc, :]), in0=y2[:, oc, :], scalar=vcol(cfg.V_S5NW, oc), in1=rsg,
                                                                        op0=ALU.mult, op1=ALU.mult), reads=[r_y2[oc], r_rs2, r_vec], writes=[r_y2[oc]])
                    P.dma("pool", lambda h, oc=oc, bs=bs: h.dma_start(out=mTs[GC + oc, :, bs], in_=y2[:, oc, :]), reads=[r_y2[oc]],
                          writes=[R("mT", i, bk) for i in range(MC)])

        SROWS = SDIM
        def zero_state():
            for hh in range(H):
                pass
            P.op("dve", lambda h: h.memset(S, 0.0), writes=r_S)
            P.op("dve", lambda h: h.memset(zc, 0.0), writes=[r_zc])
            P.op("pool", lambda h: h.tensor_copy(out=r32(S_r), in_=S), reads=r_S, writes=r_Sr)

        SUB = os.environ.get("MIXSUB", "params,gla,s5,glu").split(",")
        if "params" in SUB:
            s5_params()
        if cfg.pair:
            zero_state()
            gla_pass(True)
            RG = [[2 * i, 2 * i + 1] for i in range(cfg.ncores // 2)]
            for k in range(4):
                P.dma("pool", lambda h, k=k: h.dma_start(out=st_src[k].rearrange("(p j) w -> p (j w)", p=32), in_=S[32 * k:32 * k + 32, :]),
                      reads=r_S, writes=[R("stsrc", k)])
            P.barrier()
            s5_pass(True)
            P.dma("pool", lambda h: h.dma_start(out=sz_src[:, 0:2 * NGP], in_=zc), reads=[r_zc], writes=[R("szsrc")])
            P.barrier()
            for k in range(4):
                P.coll("pool", lambda h, k=k: h.collective_compute("AllGather", ALU.bypass, replica_groups=RG, ins=[st_src[k][:]], outs=[st_dst[k][:]]),
                       reads=[R("stsrc", k)], writes=[R("stdst", k)])
            P.coll("pool", lambda h: h.collective_compute("AllGather", ALU.bypass, replica_groups=RG, ins=[sz_src[:]], outs=[sz_dst[:]]),
                   reads=[R("szsrc")], writes=[R("szdst")])
            for k in range(4):
                P.dma("pool", lambda h, k=k: h.dma_start(out=S[32 * k:32 * k + 32, :], in_=st_dst[k][0:32 * SJ, :].rearrange("(p j) w -> p (j w)", p=32)),
                      reads=[R("stdst", k)], writes=r_S)
            P.dma("pool", lambda h: h.dma_start(out=zc, in_=sz_dst[0:128, 0:2 * NGP]), reads=[R("szdst")], writes=[r_zc])
            P.op("act", lambda h: h.activation(out=S, in_=S, func=AF.Identity, scale=flg[:, 0:1]), reads=r_S + [r_vec], writes=r_S)
            P.op("act", lambda h: h.activation(out=zc, in_=zc, func=AF.Identity, scale=flg[:, 0:1]), reads=[r_zc, r_vec], writes=[r_zc])
            P.op("pool", lambda h: h.tensor_copy(out=r32(S_r), in_=S), reads=r_S, writes=r_Sr)
        else:
            zero_state()
        if "gla" in SUB:
            gla_pass(False)
        P.barrier()
        if "s5so" in SUB:
            s5_pass(True)
            P.barrier()
        if "s5" in SUB:
            s5_pass(False)
            P.barrier()
        if "glu" in SUB:
            glu()
        P.barrier()

    outs = []
    if "B" in stages:
        for ti in range(NT):
            load_tile(mTs, MC, ti, "mT")
            for m in range(DC):
                sw = wload(w_out[m], MC * 128)
                pa = 4 + nxt("pa", 2)
                for kc in range(MC):
                    P.op("pe", lambda h, kc=kc, sw=sw, pa=pa: h.matmul(ps[pa][:], r32(wsl[sw][:, kc * 128:(kc + 1) * 128]), r32(nT[:, kc, :]),
                                                                      start=(kc == 0), stop=(kc == MC - 1)),
                         reads=[r_w[sw], r_nT[kc]], writes=[r_ps[pa]])
                rbi = nxt("rb", 2)
                P.dma("pool", lambda h, m=m, rbi=rbi, ti=ti: h.dma_start(out=rb[rbi], in_=res[m, :, tsl(ti)]),
                      reads=[R("res", m, ti)], writes=[r_rb[rbi]])
                obi = nxt("ob", 2)
                P.op("dve", lambda h, m=m, pa=pa, rbi=rbi, obi=obi: h.scalar_tensor_tensor(
                    out=ob[obi], in0=ps[pa][:], scalar=mcol(5, m), in1=rb[rbi], op0=ALU.mult, op1=ALU.add),
                    reads=[r_ps[pa], r_rb[rbi], r_mod], writes=[r_ob[obi]])
                P.dma("pool", lambda h, m=m, obi=obi, ti=ti: h.dma_start(out=res[m, :, tsl(ti)], in_=ob[obi]),
                      reads=[r_ob[obi]], writes=[R("res", m, ti)])
            load_tile(res, DC, ti, "res")
            modnorm(6, 7)
            ffn(ti, wg[1], wu[1], wd[1], 8, res, "res")
            sumsq(DC, D)
            for c in range(DC):
                obi = nxt("ob", 2)
                P.op("dve", lambda h, c=c, obi=obi: h.scalar_tensor_tensor(out=ob[obi], in0=nT[:, c, :], scalar=vcol(cfg.V_FNW, c), in1=rstd,
                                                                          op0=ALU.mult, op1=ALU.mult),
                     reads=[r_nT[c], r_rs, r_vec], writes=[r_ob[obi]])
                outs.append(P.dma("pool", lambda h, c=c, obi=obi, ti=ti: h.dma_start(out=outT[c, :, tsl(ti)], in_=ob[obi]),
                                  reads=[r_ob[obi]]))
    if not outs:
        P.op("pool", lambda h: h.memset(arena[:, 0:NTOK], 0.0))
        for c in range(DC):
            outs.append(P.dma("pool", lambda h, c=c: h.dma_start(out=outT[c], in_=arena[:, 0:NTOK])))
    P.emit(outs)
    return nc


def build_mixer(cfg, nc, P, env):
    pass


def tile_kxn(w, ncols):
    K, N = w.shape
    a = w.reshape(K // 128, 128, N // ncols, ncols).transpose(2, 1, 0, 3)
    return np.ascontiguousarray(a).reshape(N // ncols, 128, (K // 128) * ncols)


def fm_vec(v):
    return np.ascontiguousarray(v.reshape(-1, 128).T)


def layout_shared(cfg, inp):
    L = 0
    m = {}
    wa = tile_kxn(inp["w_ada"][L], 512)
    nads = 3 if wa.shape[0] % 3 == 0 else 1
    for i in range(nads):
        m["w_ada%d" % i] = np.ascontiguousarray(wa[i * (wa.shape[0] // nads):(i + 1) * (wa.shape[0] // nads)])
    for i, pre in enumerate(("ffn1", "ffn2")):
        m["wg%d" % i] = tile_kxn(inp[pre + "_w_gate"][L], 128)
        m["wu%d" % i] = tile_kxn(inp[pre + "_w_up"][L], 128)
        m["wd%d" % i] = tile_kxn(inp[pre + "_w_down"][L], 128)
    w_in = inp["w_in"][L]
    KW, GW, SW, RK = cfg.KW, cfg.GW, cfg.SW, cfg.R
    o_q, o_k, o_v, o_g, o_lr, o_u = 0, KW, 2 * KW, 2 * KW + GW, 2 * KW + 2 * GW, 2 * KW + 2 * GW + RK
    fm_cols = np.concatenate([np.arange(o_q, o_q + KW), np.arange(o_k, o_k + KW), np.arange(o_g, o_g + GW), np.arange(o_u, o_u + SW)])
    m["win_fm"] = tile_kxn(w_in[:, fm_cols], 128)
    m["win_lr"] = tile_kxn(w_in[:, o_lr:o_lr + RK], RK)[0]
    tm_cols = np.concatenate([np.arange(o_k, o_k + KW), np.arange(o_v, o_v + GW)])
    m["win_tm"] = tile_kxn(w_in[:, tm_cols], 512)
    m["w_gk2"] = np.ascontiguousarray(inp["w_gk2"][L])
    m["b_gkr"] = np.ascontiguousarray(inp["b_gk"][L][None, :])
    m["w_glu"] = tile_kxn(inp["w_glu"][L], 128)
    m["w_out"] = tile_kxn(inp["w_out"][L], 128)
    vec = np.concatenate([fm_vec(inp["b_ada"][L]), fm_vec(inp["final_norm_w"]), fm_vec(inp["b_gk"][L]),
                          fm_vec(inp["gla_norm_w"][L]), fm_vec(inp["s5_d"][L]), fm_vec(inp["b_glu"][L]),
                          fm_vec(inp["s5_norm_w"][L])], axis=1)
    assert vec.shape == (128, cfg.NV), vec.shape
    m["vecs"] = np.ascontiguousarray(vec)
    NGP = cfg.NGP
    lam = np.zeros((128, 3 * NGP), np.float32)
    lam[:, 0:NGP] = inp["s5_lambda_re"][L].reshape(NGP, 128).T
    lam[:, NGP:2 * NGP] = inp["s5_lambda_im"][L].reshape(NGP, 128).T
    lam[:, 2 * NGP:] = np.repeat(inp["s5_log_dt"][L].reshape(NGP, 2), 64, axis=1).T
    m["s5lam"] = lam
    B = np.zeros((NGP, 128, 256), np.float32)
    C = np.zeros((NGP, 128, 256), np.float32)
    bre, bim = inp["s5_b_re"][L], inp["s5_b_im"][L]
    cre, cim = inp["s5_c_re"][L], inp["s5_c_im"][L]
    for gp in range(NGP):
        for g2 in range(2):
            g = 2 * gp + g2
            r0 = (g % 8) * 16
            B[gp, r0:r0 + 16, 64 * g2:64 * g2 + 64] = bre[g].T
            B[gp, r0:r0 + 16, 128 + 64 * g2:128 + 64 * g2 + 64] = bim[g].T
            C[gp, 64 * g2:64 * g2 + 64, r0:r0 + 16] = cre[g].T
            C[gp, 64 * g2:64 * g2 + 64, 128 + r0:128 + r0 + 16] = cim[g].T
    m["s5B"], m["s5C"] = B, C
    return m


def layout_core(cfg, inp, core, ncores_per_seq):
    b, half = divmod(core, ncores_per_seq)
    t0 = half * cfg.NTOK
    xs = inp["x"][b, t0:t0 + cfg.NTOK, :]
    m = {}
    m["xT"] = np.ascontiguousarray(xs.T).reshape(cfg.DC, 128, cfg.NTOK)
    m["cT"] = fm_vec(inp["c"][b])
    m["flag"] = np.full((128, 1), 1.0 if half > 0 else 0.0, np.float32)
    return m


_CACHE = {}


def kernel(**inputs):
    cfg = Cfg(**FULL)
    inp = {k: np.asarray(v) for k, v in inputs.items()}
    if "nc" not in _CACHE:
        _CACHE["nc"] = build(cfg)
    nc = _CACHE["nc"]
    shared = layout_shared(cfg, inp)
    in_maps = []
    for core in range(8):
        m = dict(shared)
        m.update(layout_core(cfg, inp, core, 2))
        in_maps.append(m)
    r = run_bass_kernel_spmd(nc, in_maps, core_ids=list(range(8)))
    out = np.empty((4, 4096, cfg.D), np.float32)
    for core in range(8):
        b, half = divmod(core, 2)
        oT = np.asarray(r.results[core]["outT"]).reshape(cfg.D, cfg.NTOK)
        out[b, half * cfg.NTOK:(half + 1) * cfg.NTOK, :] = oT.T
    return out
```

```python
import contextlib
import os
import math
import numpy as np
import concourse.bass as bass
import concourse.mybir as mybir
from concourse.bass_utils import run_bass_kernel_spmd

F32 = mybir.dt.float32
F32R = mybir.dt.float32r
AF = mybir.ActivationFunctionType
ALU = mybir.AluOpType

ENGS = ("pe", "act", "dve", "pool", "sp")


class Res:
    __slots__ = ("w", "rs")

    def __init__(self):
        self.w = None
        self.rs = []


class Op:
    __slots__ = ("eng", "fn", "deps", "inc", "val", "dma", "dsem", "dval", "prewait", "dinc")

    def __init__(self, eng, fn, dma=False):
        self.eng = eng
        self.fn = fn
        self.deps = []
        self.inc = False
        self.val = 0
        self.dma = dma
        self.dsem = None
        self.dval = 0
        self.prewait = None
        self.dinc = 16


class Prog:
    KDMA = 8

    def __init__(self, nc, same_engine_sync=True):
        self.nc = nc
        self.q = {e: [] for e in ENGS}
        self.ndma = {e: 0 for e in ENGS}
        self.same = same_engine_sync
        self.stack = contextlib.ExitStack()
        self.esem = {}
        self.dsems = {}
        self.lastdma = {}
        self.rd = {}

    def R(self, *key):
        r = self.rd.get(key)
        if r is None:
            r = self.rd[key] = Res()
        return r

    def sbuf(self, name, shape, dt=F32):
        return self.stack.enter_context(self.nc.sbuf_tensor(name, list(shape), dt))

    def psum(self, name, shape, dt=F32):
        return self.stack.enter_context(self.nc.psum_tensor(name, list(shape), dt))

    def sem(self, name):
        return self.stack.enter_context(self.nc.semaphore(name))

    def _track(self, op, reads, writes):
        deps = op.deps
        for r in reads:
            if r.w is not None:
                deps.append(r.w)
            if not op.dma:
                rs = r.rs
                for i in range(len(rs)):
                    if (not rs[i].dma) and rs[i].eng == op.eng:
                        rs[i] = op
                        break
                else:
                    rs.append(op)
            else:
                r.rs.append(op)
        for w in writes:
            if w.w is not None:
                deps.append(w.w)
            for x in w.rs:
                if x is not op:
                    deps.append(x)
            w.w = op
            w.rs = []
        for d in deps:
            if d.dma:
                continue
            if d.eng == op.eng and not op.dma and (op.eng == "pe" or not self.same):
                continue
            d.inc = True

    def op(self, eng, fn, reads=(), writes=()):
        o = Op(eng, fn)
        self._track(o, reads, writes)
        self.q[eng].append(o)
        return o

    def dma(self, eng, fn, reads=(), writes=()):
        o = Op(eng, fn, dma=True)
        self._track(o, reads, writes)
        n = self.ndma[eng]
        self.ndma[eng] = n + 1
        o.dsem = (eng, n % self.KDMA)
        o.dval = 16 * (n // self.KDMA + 1)
        if n >= self.KDMA:
            o.prewait = (o.dsem, 16 * (n // self.KDMA))
        self.lastdma[o.dsem] = o
        self.q[eng].append(o)
        return o

    def coll(self, eng, fn, reads=(), writes=()):
        o = Op(eng, fn, dma=True)
        self._track(o, reads, writes)
        self.ncoll = getattr(self, "ncoll", 0) + 1
        o.dsem = ("cc", self.ncoll)
        o.dval = 1
        o.dinc = 1
        self.lastdma[o.dsem] = o
        self.q[eng].append(o)
        return o

    def barrier(self):
        lasts = []
        for e in ENGS:
            for o in reversed(self.q[e]):
                if not o.dma and o.fn is not None:
                    lasts.append(o)
                    break
        lasts += list(self.lastdma.values())
        for e in ENGS:
            o = Op(e, None)
            o.deps = list(lasts)
            for d in lasts:
                d.inc = True
            self.q[e].append(o)

    def emit(self, final_wait_ops=()):
        nc = self.nc
        for e in ENGS:
            self.esem[e] = self.sem("es_" + e)
            for k in range(self.KDMA):
                if self.ndma[e] > k:
                    self.dsems[(e, k)] = self.sem("ds_%s%d" % (e, k))
        for i in range(getattr(self, "ncoll", 0)):
            self.dsems[("cc", i + 1)] = self.sem("cc_%d" % (i + 1))
        for e in ENGS:
            c = 0
            for o in self.q[e]:
                if not o.dma and o.inc and o.fn is not None:
                    c += 1
                o.val = c
        same = self.same

        def run(e, h):
            known = {}
            esem = self.esem
            for o in self.q[e]:
                if o.prewait is not None:
                    sk, v = o.prewait
                    if known.get(sk, 0) < v:
                        h.wait_ge(self.dsems[sk], v)
                        known[sk] = v
                for d in o.deps:
                    if d.dma:
                        sk, v, s = d.dsem, d.dval, self.dsems[d.dsem]
                    else:
                        if d.eng == e and (e == "pe" or not same):
                            continue
                        sk, v, s = d.eng, d.val, esem[d.eng]
                    if known.get(sk, 0) < v:
                        h.wait_ge(s, v)
                        known[sk] = v
                if o.fn is None:
                    continue
                ins = o.fn(h)
                if o.dma:
                    ins.then_inc(self.dsems[o.dsem], o.dinc)
                elif o.inc:
                    ins.then_inc(esem[e], 1)
            if e == "sp":
                for o in final_wait_ops:
                    sk, v = o.dsem, o.dval
                    if known.get(sk, 0) < v:
                        h.wait_ge(self.dsems[sk], v)
                        known[sk] = v

        with nc.Block() as block:
            @block.tensor
            def _(h):
                run("pe", h)

            @block.scalar
            def _(h):
                run("act", h)

            @block.vector
            def _(h):
                run("dve", h)

            @block.gpsimd
            def _(h):
                run("pool", h)

            @block.sync
            def _(h):
                run("sp", h)
        self.stack.close()


class Cfg:
    def __init__(self, D, DFF, NTOK, H, DK, DV, G, T=512, R=16, NS=4, pair=True, ncores=8):
        self.ncores = ncores
        self.D, self.DFF, self.NTOK, self.H, self.DK, self.DV, self.G = D, DFF, NTOK, H, DK, DV, G
        self.T, self.R, self.NS, self.pair = T, R, NS, pair
        self.DC = D // 128
        self.FC = DFF // 128
        self.NT = NTOK // T
        self.KW, self.GW, self.SW = H * DK, H * DV, 16 * G
        self.KC, self.GC, self.SC = self.KW // 128, self.GW // 128, self.SW // 128
        self.KC2, self.VC = DK // 128, DV // 128
        self.NGP = G // 2
        self.MC = self.GC + self.SC
        self.INW = 2 * self.KW + 2 * self.GW + R + self.SW
        c = 0
        self.V_BADA = c; c += 9 * self.DC
        self.V_FNW = c; c += self.DC
        self.V_BGK = c; c += self.KC
        self.V_GNW = c; c += self.VC
        self.V_S5D = c; c += self.SC
        self.V_BGLU = c; c += self.SC
        self.V_S5NW = c; c += self.SC
        self.NV = c
        base, rem = divmod(self.FC, NS)
        self.fsplit = []
        f = 0
        for s in range(NS):
            n = base + (1 if s < rem else 0)
            self.fsplit.append((f, f + n))
            f += n
        self.FSUB = base + (1 if rem else 0)


FULL = dict(D=4096, DFF=11008, NTOK=2048, H=4, DK=256, DV=512, G=128)
DEBUG_OUT = False
_LASTP = None
GB = 256
CH = 128
EPS = 1e-6


def build(cfg, stages=("mod", "A", "M", "B")):
    nc = bass.Bass("TRN2", target_bir_lowering=False)
    nc.dge_precook = False
    P = Prog(nc)
    global _LASTP
    _LASTP = P
    R = P.R
    D, DC, FC, T, NT, NTOK = cfg.D, cfg.DC, cfg.FC, cfg.T, cfg.NT, cfg.NTOK
    KC, GC, SC, MC, KC2, VC, H = cfg.KC, cfg.GC, cfg.SC, cfg.MC, cfg.KC2, cfg.VC, cfg.H
    KW, GW, SW, NGP, RK = cfg.KW, cfg.GW, cfg.SW, cfg.NGP, cfg.R
    XC = max(DC, MC)

    def din(name, shape, dt=F32):
        return nc.dram_tensor(name, list(shape), dt, kind="ExternalInput").ap()

    def dscr(name, shape, dt=F32):
        if DEBUG_OUT:
            return nc.dram_tensor(name, list(shape), dt, kind="ExternalOutput").ap()
        return nc.dram_tensor(name, list(shape), dt).ap()

    xT = din("xT", [DC, 128, NTOK])
    cT = din("cT", [128, DC])
    vecs = din("vecs", [128, cfg.NV])
    flag = din("flag", [128, 1])
    NADA = 9 * DC * 128 // 512
    NADS = 3 if NADA % 3 == 0 else 1
    NADP = NADA // NADS
    w_ada_parts = [din("w_ada%d" % i, [NADP, 128, DC * 512]) for i in range(NADS)]
    wg = [din("wg%d" % i, [FC, 128, DC * 128]) for i in range(2)]
    wu = [din("wu%d" % i, [FC, 128, DC * 128]) for i in range(2)]
    wd = [din("wd%d" % i, [DC, 128, FC * 128]) for i in range(2)]
    NFM = 2 * KC + GC + SC
    win_fm = din("win_fm", [NFM, 128, DC * 128])
    win_lr = din("win_lr", [128, DC * RK])
    NTMB = (KW + GW) // 512
    win_tm = din("win_tm", [NTMB, 128, DC * 512])
    w_gk2 = din("w_gk2", [RK, KW])
    b_gkr = din("b_gkr", [1, KW])
    w_glu = din("w_glu", [SC, 128, SC * 128])
    w_out = din("w_out", [DC, 128, MC * 128])
    s5lam = din("s5lam", [128, 3 * NGP])
    s5B = din("s5B", [NGP, 128, 256])
    s5C = din("s5C", [NGP, 128, 256])
    outT = nc.dram_tensor("outT", [DC, 128, NTOK], F32, kind="ExternalOutput").ap()

    res = dscr("res", [DC, 128, NTOK])
    qTs = dscr("qTs", [KC, 128, NTOK])
    kTs = dscr("kTs", [KC, 128, NTOK])
    gTs = dscr("gTs", [GC, 128, NTOK])
    uTs = dscr("uTs", [SC, 128, NTOK])
    lrs = dscr("lrs", [RK, NTOK])
    ktm = dscr("ktm", [NTOK, KW])
    vtm = dscr("vtm", [NTOK, GW])
    yas = dscr("yas", [SC, 128, NTOK])
    mTs = dscr("mTs", [MC, 128, NTOK])
    SJ = KC2 * H * cfg.DV // 128
    st_src = [nc.dram_tensor("st_src%d" % k, [32 * SJ, 128], F32).ap() for k in range(4)]
    st_dst = [nc.dram_tensor("st_dst%d" % k, [2 * 32 * SJ, 128], F32).ap() for k in range(4)]
    sz_src = nc.dram_tensor("sz_src", [128, 128], F32).ap()
    sz_dst = nc.dram_tensor("sz_dst", [256, 128], F32).ap()

    ones = P.sbuf("ones", [128, 128])
    maskT = P.sbuf("maskT", [128, 128])
    Umask = P.sbuf("Umask", [128, 128])
    rmask = P.sbuf("rmask", [128, 512])
    iot = P.sbuf("iot", [128, 512])
    vec = P.sbuf("vec", [128, cfg.NV])
    modT = P.sbuf("modT", [128, 9 * DC])
    csl = P.sbuf("csl", [128, DC])
    flg = P.sbuf("flg", [128, 1])
    AW = 45056
    AWF = 5700
    arenaF = P.sbuf("arenaF", [128, AWF])
    craw = P.sbuf("craw", [128, DC])
    arena = P.sbuf("arena", [128, AW])
    ps = [P.psum("ps%d" % i, [128, 512]) for i in range(8)]
    r_ps = [R("ps", i) for i in range(8)]
    r_c = R("consts")

    def AV(off, n, inner=None):
        v = arena[:, off:off + n]
        if inner is not None:
            v = v.rearrange("p (c t) -> p c t", t=inner)
        return v

    def r32(ap):
        return ap.bitcast(F32R)

    def AF_(off, n, inner=None):
        v = arenaF[:, off:off + n]
        if inner is not None:
            v = v.rearrange("p (c t) -> p c t", t=inner)
        return v

    tmpc = P.sbuf("tmpc", [128, 128])
    P.op("pool", lambda h: h.memset(tmpc[:], 1.0), writes=[r_c])
    P.op("pool", lambda h: h.tensor_copy(out=r32(ones[:]), in_=tmpc[:]), reads=[r_c], writes=[r_c])
    P.op("pool", lambda h: h.memset(maskT[:], 1.0), writes=[r_c])
    P.op("pool", lambda h: h.affine_select(out=maskT[:], in_=maskT[:], pattern=[[1, 128]], compare_op=ALU.is_ge,
                                           fill=0.0, base=0, channel_multiplier=-1), reads=[r_c], writes=[r_c])
    P.op("pool", lambda h: h.affine_select(out=tmpc[:], in_=tmpc[:], pattern=[[-1, 128]], compare_op=ALU.is_ge,
                                           fill=0.0, base=-1, channel_multiplier=1), reads=[r_c], writes=[r_c])
    P.op("pool", lambda h: h.tensor_copy(out=r32(Umask[:]), in_=tmpc[:]), reads=[r_c], writes=[r_c])
    P.op("pool", lambda h: h.memset(rmask[:], 1.0), writes=[r_c])
    for c in range(512 // CH):
        P.op("pool", lambda h, c=c: h.memset(rmask[:, c * CH:c * CH + 1], 0.0), reads=[r_c], writes=[r_c])
    P.op("pool", lambda h: h.iota(iot[:], pattern=[[1, 512]], base=0, channel_multiplier=0,
                                  allow_small_or_imprecise_dtypes=True), writes=[r_c])
    r_vec = R("vec")
    P.dma("pool", lambda h: h.dma_start(out=vec[:], in_=vecs), writes=[r_vec])
    P.dma("pool", lambda h: h.dma_start(out=flg[:], in_=flag), writes=[r_vec])
    r_mod = R("mod")

    if "mod" in stages:
        r_csl = R("csl")
        P.dma("pool", lambda h: h.dma_start(out=craw[:], in_=cT), writes=[R("craw")])
        P.op("act", lambda h: h.activation(out=csl[:], in_=craw[:], func=AF.Silu), reads=[R("craw")], writes=[r_csl])
        WT = DC * 512
        nsl = 2 if 2 * WT <= AW else 1
        r_aw = [R("adaw", i) for i in range(nsl)]
        for j in range(NADA):
            s = j % nsl
            wt = AV(s * WT, WT)
            wsrc = w_ada_parts[j // NADP][j % NADP]
            NPC = 4 if DC % 4 == 0 else 1
            CH_ = WT // NPC
            for pc in range(NPC):
                P.dma("sp", lambda h, wsrc=wsrc, wt=wt, pc=pc: h.dma_start(out=r32(wt[:, pc * CH_:(pc + 1) * CH_]),
                                                                          in_=r32(wsrc[:, pc * CH_:(pc + 1) * CH_])), writes=[r_aw[s]])
            for sub in range(4):
                col = j * 4 + sub
                for kc in range(DC):
                    P.op("pe", lambda h, wt=wt, sub=sub, kc=kc, col=col: h.matmul(
                        ps[7][:, col:col + 1], wt[:, kc * 512 + sub * 128: kc * 512 + (sub + 1) * 128],
                        csl[:, kc:kc + 1], start=(kc == 0), stop=(kc == DC - 1)),
                        reads=[r_aw[s], r_csl], writes=[r_ps[7]])
        P.op("dve", lambda h: h.tensor_tensor(out=modT[:], in0=ps[7][:, 0:9 * DC], in1=vec[:, cfg.V_BADA:cfg.V_BADA + 9 * DC],
                                              op=ALU.add), reads=[r_ps[7], r_vec], writes=[r_mod])
        for j in (1, 4, 7):
            P.op("dve", lambda h, j=j: h.tensor_scalar_add(out=modT[:, j * DC:(j + 1) * DC], in0=modT[:, j * DC:(j + 1) * DC],
                                                           scalar1=1.0), reads=[r_mod], writes=[r_mod])
        for j in (2, 8):
            P.op("dve", lambda h, j=j: h.tensor_scalar_mul(out=modT[:, j * DC:(j + 1) * DC], in0=modT[:, j * DC:(j + 1) * DC],
                                                           scalar1=0.5), reads=[r_mod], writes=[r_mod])
        P.barrier()

    def mcol(j, c):
        return modT[:, j * DC + c: j * DC + c + 1]

    def vcol(base, c):
        return vec[:, base + c: base + c + 1]

    O_NT = 0
    O_HID = O_NT + XC * T
    O_W = O_HID + cfg.FSUB * T
    WSL = max(DC * 128, MC * 128, min(8, DC) * 512, cfg.FSUB * 128)
    NW = 4
    O_SQ = O_W + NW * WSL
    O_END = O_SQ + 2 * T
    assert O_END <= AW, (O_END, AW)
    O_RB = 0
    O_OB = O_RB + 2 * T
    O_RS = O_OB + 2 * T
    O_SG = O_RS + T
    assert O_SG + 2 * T <= AWF
    nT = AV(O_NT, XC * T, T)
    hid = AV(O_HID, cfg.FSUB * T, T)
    wsl = [AV(O_W + i * WSL, WSL) for i in range(NW)]
    rb = [AF_(O_RB + i * T, T) for i in range(2)]
    ob = [AF_(O_OB + i * T, T) for i in range(2)]
    sq = [AV(O_SQ + i * T, T) for i in range(2)]
    rstd = AF_(O_RS, T)
    sgb = [AF_(O_SG + i * T, T) for i in range(2)]
    r_sg = [R("sg", i) for i in range(2)]
    r_nT = [R("nT", c) for c in range(XC)]
    r_hid = [R("hid", i) for i in range(cfg.FSUB)]
    r_w = [R("w", i) for i in range(NW)]
    r_rb = [R("rb", i) for i in range(2)]
    r_ob = [R("ob", i) for i in range(2)]
    r_sq = [R("sq", i) for i in range(2)]
    r_rs = R("rstd")
    cnt = {"w": 0, "rb": 0, "ob": 0, "sq": 0, "pa": 0, "pg": 0, "sg": 0}

    def nxt(k, n):
        v = cnt[k] % n
        cnt[k] += 1
        return v

    def wload(src_ap, n):
        s = nxt("w", NW)
        P.dma("sp", lambda h, s=s, n=n, src_ap=src_ap: h.dma_start(out=r32(wsl[s][:, 0:n]), in_=r32(src_ap)), writes=[r_w[s]])
        return s

    def tsl(ti):
        return slice(ti * T, (ti + 1) * T)

    def load_tile(src, nch, ti, rsrc):
        for c in range(nch):
            P.dma("pool", lambda h, c=c: h.dma_start(out=r32(nT[:, c, :]), in_=r32(src[c, :, tsl(ti)])),
                  reads=[R(rsrc, c, ti)], writes=[r_nT[c]])

    def sumsq(nch, Dn):
        for c in range(nch):
            b = nxt("sq", 2)
            P.op("act", lambda h, c=c, b=b: h.activation(out=r32(sq[b]), in_=nT[:, c, :], func=AF.Square),
                 reads=[r_nT[c]], writes=[r_sq[b]])
            P.op("pe", lambda h, c=c, b=b: h.matmul(ps[6][:], r32(ones[:]), r32(sq[b]), start=(c == 0), stop=(c == nch - 1)),
                 reads=[r_sq[b], r_c], writes=[r_ps[6]])
        P.op("act", lambda h: h.activation(out=rstd, in_=ps[6][:], func=AF.Sqrt, bias=EPS, scale=1.0 / Dn),
             reads=[r_ps[6]], writes=[r_rs])
        P.op("dve", lambda h: h.reciprocal(out=rstd, in_=rstd), reads=[r_rs], writes=[r_rs])

    def modnorm(jsh, jsc):
        sumsq(DC, D)
        for c in range(DC):
            P.op("dve", lambda h, c=c: h.scalar_tensor_tensor(out=r32(nT[:, c, :]), in0=nT[:, c, :], scalar=mcol(jsc, c), in1=rstd,
                                                              op0=ALU.mult, op1=ALU.mult),
                 reads=[r_nT[c], r_rs, r_mod], writes=[r_nT[c]])
            P.op("act", lambda h, c=c: h.activation(out=r32(nT[:, c, :]), in_=nT[:, c, :], func=AF.Identity,
                                                    bias=mcol(jsh, c), scale=1.0),
                 reads=[r_nT[c], r_mod], writes=[r_nT[c]])

    def ffn(ti, wgd, wud, wdd, jg, src0, rsrc0):
        for s, (f0, f1) in enumerate(cfg.fsplit):
            last = s == cfg.NS - 1
            for f in range(f0, f1):
                sg_ = wload(wgd[f], DC * 128)
                su_ = wload(wud[f], DC * 128)
                pg = 2 * nxt("pg", 2)
                for kc in range(DC):
                    P.op("pe", lambda h, kc=kc, sg_=sg_, pg=pg: h.matmul(ps[pg][:], r32(wsl[sg_][:, kc * 128:(kc + 1) * 128]),
                                                                       r32(nT[:, kc, :]), start=(kc == 0), stop=(kc == DC - 1)),
                         reads=[r_w[sg_], r_nT[kc]], writes=[r_ps[pg]])
                for kc in range(DC):
                    P.op("pe", lambda h, kc=kc, su_=su_, pg=pg: h.matmul(ps[pg + 1][:], r32(wsl[su_][:, kc * 128:(kc + 1) * 128]),
                                                                       r32(nT[:, kc, :]), start=(kc == 0), stop=(kc == DC - 1)),
                         reads=[r_w[su_], r_nT[kc]], writes=[r_ps[pg + 1]])
                b = nxt("sg", 2)
                P.op("act", lambda h, pg=pg, b=b: h.activation(out=sgb[b], in_=ps[pg][:], func=AF.Silu),
                     reads=[r_ps[pg]], writes=[r_sg[b]])
                P.op("dve", lambda h, pg=pg, b=b, f=f, f0=f0: h.tensor_tensor(out=r32(hid[:, f - f0, :]), in0=sgb[b], in1=ps[pg + 1][:],
                                                                             op=ALU.mult),
                     reads=[r_sg[b], r_ps[pg + 1]], writes=[r_hid[f - f0]])
            nf = f1 - f0
            for m in range(DC):
                sd = wload(wdd[m][:, f0 * 128:f1 * 128], nf * 128)
                pa = 4 + nxt("pa", 2)
                for i in range(nf):
                    P.op("pe", lambda h, i=i, sd=sd, pa=pa: h.matmul(ps[pa][:], r32(wsl[sd][:, i * 128:(i + 1) * 128]), r32(hid[:, i, :]),
                                                                    start=(i == 0), stop=(i == nf - 1)),
                         reads=[r_w[sd], r_hid[i]], writes=[r_ps[pa]])
                rbi = nxt("rb", 2)
                if s == 0:
                    P.dma("pool", lambda h, m=m, rbi=rbi: h.dma_start(out=rb[rbi], in_=src0[m, :, tsl(ti)]),
                          reads=[R(rsrc0, m, ti)], writes=[r_rb[rbi]])
                else:
                    P.dma("pool", lambda h, m=m, rbi=rbi: h.dma_start(out=rb[rbi], in_=res[m, :, tsl(ti)]),
                          reads=[R("res", m, ti)], writes=[r_rb[rbi]])
                if last:
                    P.op("dve", lambda h, m=m, pa=pa, rbi=rbi: h.scalar_tensor_tensor(
                        out=r32(nT[:, m, :]), in0=ps[pa][:], scalar=mcol(jg, m), in1=rb[rbi], op0=ALU.mult, op1=ALU.add),
                        reads=[r_ps[pa], r_rb[rbi], r_mod], writes=[r_nT[m]])
                    P.dma("pool", lambda h, m=m: h.dma_start(out=res[m, :, tsl(ti)], in_=nT[:, m, :]),
                          reads=[r_nT[m]], writes=[R("res", m, ti)])
                else:
                    obi = nxt("ob", 2)
                    P.op("dve", lambda h, m=m, pa=pa, rbi=rbi, obi=obi: h.scalar_tensor_tensor(
                        out=ob[obi], in0=ps[pa][:], scalar=mcol(jg, m), in1=rb[rbi], op0=ALU.mult, op1=ALU.add),
                        reads=[r_ps[pa], r_rb[rbi], r_mod], writes=[r_ob[obi]])
                    P.dma("pool", lambda h, m=m, obi=obi: h.dma_start(out=res[m, :, tsl(ti)], in_=ob[obi]),
                          reads=[r_ob[obi]], writes=[R("res", m, ti)])

    def proj_fm(ti, wsrc, nk, dst, rdst, oc, rhs_nk=None):
        sw = wload(wsrc, nk * 128)
        pa = 4 + nxt("pa", 2)
        for kc in range(nk):
            P.op("pe", lambda h, kc=kc, sw=sw, pa=pa: h.matmul(ps[pa][:], r32(wsl[sw][:, kc * 128:(kc + 1) * 128]), r32(nT[:, kc, :]),
                                                              start=(kc == 0), stop=(kc == nk - 1)),
                 reads=[r_w[sw], r_nT[kc]], writes=[r_ps[pa]])
        return pa

    if "A" in stages:
        for ti in range(NT):
            load_tile(xT, DC, ti, "xin")
            modnorm(0, 1)
            ffn(ti, wg[0], wu[0], wd[0], 2, xT, "xin")
            modnorm(3, 4)
            dsts = ([(qTs, "qT", i) for i in range(KC)] + [(kTs, "kT", i) for i in range(KC)]
                    + [(gTs, "gT", i) for i in range(GC)] + [(uTs, "uT", i) for i in range(SC)])
            for oc, (dst, rn, i) in enumerate(dsts):
                pa = proj_fm(ti, win_fm[oc], DC, dst, rn, i)
                obi = nxt("ob", 2)
                P.op("act", lambda h, pa=pa, obi=obi: h.activation(out=ob[obi], in_=ps[pa][:], func=AF.Copy),
                     reads=[r_ps[pa]], writes=[r_ob[obi]])
                P.dma("pool", lambda h, dst=dst, i=i, obi=obi, ti=ti: h.dma_start(out=dst[i, :, tsl(ti)], in_=ob[obi]),
                      reads=[r_ob[obi]], writes=[R(rn, i, ti)])
            sw = wload(win_lr, DC * RK)
            pa = 4 + nxt("pa", 2)
            for kc in range(DC):
                P.op("pe", lambda h, kc=kc, sw=sw, pa=pa: h.matmul(ps[pa][0:RK, :], r32(wsl[sw][:, kc * RK:(kc + 1) * RK]), r32(nT[:, kc, :]),
                                                                  start=(kc == 0), stop=(kc == DC - 1)),
                     reads=[r_w[sw], r_nT[kc]], writes=[r_ps[pa]])
            obi = nxt("ob", 2)
            P.op("act", lambda h, pa=pa, obi=obi: h.activation(out=ob[obi][0:RK, :], in_=ps[pa][0:RK, :], func=AF.Copy),
                 reads=[r_ps[pa]], writes=[r_ob[obi]])
            P.dma("pool", lambda h, obi=obi, ti=ti: h.dma_start(out=lrs[:, tsl(ti)], in_=ob[obi][0:RK, :]),
                  reads=[r_ob[obi]], writes=[R("lr", ti)])
            KG = 8 if DC >= 8 else DC
            for blk in range(NTMB):
                for kg in range(DC // KG):
                    sw = wload(win_tm[blk][:, kg * KG * 512:(kg + 1) * KG * 512], KG * 512)
                    for tc in range(T // 128):
                        for k8 in range(KG):
                            kc = kg * KG + k8
                            P.op("pe", lambda h, tc=tc, k8=k8, kc=kc, sw=sw: h.matmul(
                                ps[tc][:], r32(nT[:, kc, tc * 128:(tc + 1) * 128]), r32(wsl[sw][:, k8 * 512:(k8 + 1) * 512]),
                                start=(kc == 0), stop=(kc == DC - 1)),
                                reads=[r_w[sw], r_nT[kc]], writes=[r_ps[tc]])
                for tc in range(T // 128):
                    obi = nxt("ob", 2)
                    P.op("act", lambda h, tc=tc, obi=obi: h.activation(out=ob[obi], in_=ps[tc][:], func=AF.Copy),
                         reads=[r_ps[tc]], writes=[r_ob[obi]])
                    r0 = ti * T + tc * 128
                    if blk < KW // 512:
                        P.dma("pool", lambda h, obi=obi, r0=r0, blk=blk: h.dma_start(out=ktm[r0:r0 + 128, blk * 512:(blk + 1) * 512], in_=ob[obi]),
                              reads=[r_ob[obi]], writes=[R("ktm", ti)])
                    else:
                        b2 = blk - KW // 512
                        P.dma("pool", lambda h, obi=obi, r0=r0, b2=b2: h.dma_start(out=vtm[r0:r0 + 128, b2 * 512:(b2 + 1) * 512], in_=ob[obi]),
                              reads=[r_ob[obi]], writes=[R("vtm", ti)])
        P.barrier()

    if "M" in stages:
        DK, DV = cfg.DK, cfg.DV
        NCH = GB // CH
        NBL = NTOK // GB
        SDIM = KC2 * H * DV
        o = 0
        def take(n):
            nonlocal o
            v = o
            o += n
            return v
        qTb = AV(take(KC * GB), KC * GB, GB)
        kTb = AV(take(KC * GB), KC * GB, GB)
        cum = AV(take(KC * GB), KC * GB, GB)
        eq = AV(take(KC * GB), KC * GB, GB)
        gate = AV(take(GC * GB), GC * GB, GB)
        oT = AV(take(GC * GB), GC * GB, GB)
        ktmb = AV(take(NCH * KW), NCH * KW, KW)
        vtmb = AV(take(NCH * GW), NCH * GW, GW)
        gtm = AV(take(KW), KW)
        kdec = AV(take(KW), KW)
        attT = [AV(take(128), 128) for _ in range(2)]
        S_r = AV(take(SDIM), SDIM)
        glr = AV(take(GB), GB)
        wgk = AV(take(KW), KW)
        bgk = AV(take(KW), KW)
        sqm = [AV(take(GB), GB) for _ in range(2)]
        assert o <= AW, o
        of = 0
        def takef(n):
            nonlocal of
            v = of
            of += n
            return v
        S = AF_(takef(SDIM), SDIM)
        rstm = AF_(takef(GB), GB)
        zc = AF_(takef(2 * NGP), 2 * NGP)
        NPAR = 14
        s5p = AF_(takef(NPAR * NGP), NPAR * NGP, NGP)
        tiny = AF_(takef(8), 8)
        assert of <= AWF, of
        r_S = [R("S", h) for h in range(H)]
        r_Sr = [R("Sr", h) for h in range(H)]
        r_zc = R("zc")
        r_s5p = R("s5p")
        L_R, L_I, L_DT, P_R, P_TH, P_FR, P_FI, P_CN, P_SN, P_T0, P_T1, P_T2, P_T3, P_T4 = range(14)
        PI = math.pi

        def pcol(k, gp):
            return s5p[:, k, gp:gp + 1]


        MAGIC = 12582912.0
        def sincos(xa, tb_, o_sin, o_cos, rx, rt, ro, typed):
            cast = r32 if typed else (lambda a: a)
            for which, dst in ((0, o_sin), (1, o_cos)):
                if which == 1:
                    P.op("dve", lambda h: h.tensor_scalar_add(out=xa, in0=xa, scalar1=0.5 * PI), reads=[rx], writes=[rx])
                P.op("dve", lambda h: h.tensor_scalar(out=tb_, in0=xa, scalar1=1.0 / (2 * PI), scalar2=MAGIC, op0=ALU.mult, op1=ALU.add),
                     reads=[rx], writes=[rt])
                P.op("dve", lambda h: h.tensor_scalar(out=tb_, in0=tb_, scalar1=-MAGIC, scalar2=-2 * PI, op0=ALU.add, op1=ALU.mult),
                     reads=[rt], writes=[rt])
                P.op("dve", lambda h: h.tensor_tensor(out=tb_, in0=tb_, in1=xa, op=ALU.add), reads=[rt, rx], writes=[rt])
                P.op("dve", lambda h: h.tensor_scalar(out=tb_, in0=tb_, scalar1=-PI, scalar2=PI, op0=ALU.max, op1=ALU.min),
                     reads=[rt], writes=[rt])
                if typed:
                    P.op("act", lambda h: h.activation(out=tb_, in_=tb_, func=AF.Sin), reads=[rt], writes=[rt])
                    P.op("dve", lambda h, dst=dst: h.tensor_copy(out=r32(dst), in_=tb_), reads=[rt], writes=[ro])
                else:
                    P.op("act", lambda h, dst=dst: h.activation(out=dst, in_=tb_, func=AF.Sin), reads=[rt], writes=[ro])

        def s5_params():
            P.dma("sp", lambda h: h.dma_start(out=s5p[:, 0:3, :], in_=s5lam.rearrange("p (k g) -> p k g", k=3)), writes=[r_s5p])
            def A(fn):
                P.op("act", fn, reads=[r_s5p], writes=[r_s5p])
            def V(fn):
                P.op("pool", fn, reads=[r_s5p], writes=[r_s5p])
            A(lambda h: h.activation(out=s5p[:, L_DT, :], in_=s5p[:, L_DT, :], func=AF.Exp))
            V(lambda h: h.tensor_tensor(out=s5p[:, P_R, :], in0=s5p[:, L_R, :], in1=s5p[:, L_DT, :], op=ALU.mult))
            A(lambda h: h.activation(out=s5p[:, P_R, :], in_=s5p[:, P_R, :], func=AF.Exp))
            V(lambda h: h.tensor_tensor(out=s5p[:, P_TH, :], in0=s5p[:, L_I, :], in1=s5p[:, L_DT, :], op=ALU.mult))
            V(lambda h: h.tensor_copy(out=s5p[:, P_T2, :], in_=s5p[:, P_TH, :]))
            sincos(s5p[:, P_T2, :], s5p[:, P_T3, :], s5p[:, P_T1, :], s5p[:, P_T0, :], r_s5p, r_s5p, r_s5p, False)
            V(lambda h: h.tensor_tensor(out=s5p[:, P_T2, :], in0=s5p[:, P_R, :], in1=s5p[:, P_T0, :], op=ALU.mult))
            V(lambda h: h.tensor_scalar_add(out=s5p[:, P_T2, :], in0=s5p[:, P_T2, :], scalar1=-1.0))
            V(lambda h: h.tensor_tensor(out=s5p[:, P_T3, :], in0=s5p[:, P_R, :], in1=s5p[:, P_T1, :], op=ALU.mult))
            V(lambda h: h.tensor_tensor(out=s5p[:, P_T4, :], in0=s5p[:, L_R, :], in1=s5p[:, L_R, :], op=ALU.mult))
            V(lambda h: h.tensor_tensor(out=s5p[:, P_T0, :], in0=s5p[:, L_I, :], in1=s5p[:, L_I, :], op=ALU.mult))
            V(lambda h: h.tensor_tensor(out=s5p[:, P_T4, :], in0=s5p[:, P_T4, :], in1=s5p[:, P_T0, :], op=ALU.add))
            P.op("dve", lambda h: h.reciprocal(out=s5p[:, P_T4, :], in_=s5p[:, P_T4, :]), reads=[r_s5p], writes=[r_s5p])
            V(lambda h: h.tensor_tensor(out=s5p[:, P_FR, :], in0=s5p[:, P_T2, :], in1=s5p[:, L_R, :], op=ALU.mult))
            V(lambda h: h.tensor_tensor(out=s5p[:, P_T0, :], in0=s5p[:, P_T3, :], in1=s5p[:, L_I, :], op=ALU.mult))
            V(lambda h: h.tensor_tensor(out=s5p[:, P_FR, :], in0=s5p[:, P_FR, :], in1=s5p[:, P_T0, :], op=ALU.add))
            V(lambda h: h.tensor_tensor(out=s5p[:, P_FR, :], in0=s5p[:, P_FR, :], in1=s5p[:, P_T4, :], op=ALU.mult))
            V(lambda h: h.tensor_tensor(out=s5p[:, P_FI, :], in0=s5p[:, P_T3, :], in1=s5p[:, L_R, :], op=ALU.mult))
            V(lambda h: h.tensor_tensor(out=s5p[:, P_T0, :], in0=s5p[:, P_T2, :], in1=s5p[:, L_I, :], op=ALU.mult))
            V(lambda h: h.tensor_tensor(out=s5p[:, P_FI, :], in0=s5p[:, P_FI, :], in1=s5p[:, P_T0, :], op=ALU.subtract))
            V(lambda h: h.tensor_tensor(out=s5p[:, P_FI, :], in0=s5p[:, P_FI, :], in1=s5p[:, P_T4, :], op=ALU.mult))
            V(lambda h: h.tensor_scalar(out=s5p[:, P_T2, :], in0=s5p[:, P_TH, :], scalar1=512.0, scalar2=None, op0=ALU.mult))
            sincos(s5p[:, P_T2, :], s5p[:, P_T3, :], s5p[:, P_SN, :], s5p[:, P_CN, :], r_s5p, r_s5p, r_s5p, False)

        def gla_pass(state_only):
            r_q, r_k, r_cum, r_eq, r_gate, r_oT = R("qTb"), R("kTb"), R("cum"), R("eqb"), R("gateb"), R("oTb")
            r_ktm, r_vtm, r_gtm, r_kdec, r_glr = R("ktmb"), R("vtmb"), R("gtm"), R("kdec"), R("glr")
            r_att = [R("att", i) for i in range(2)]
            r_sqm = [R("sqm", i) for i in range(2)]
            r_rstm = R("rstm")
            r_wgk = R("wgk")
            P.dma("sp", lambda h: h.dma_start(out=r32(wgk[0:RK, :]), in_=r32(w_gk2)), writes=[r_wgk])
            P.dma("sp", lambda h: h.dma_start(out=r32(bgk[0:1, :]), in_=r32(b_gkr)), writes=[r_wgk])
            kcnt = [0]
            def pb():
                kcnt[0] += 1
                return kcnt[0] % 4
            for bi in range(NBL):
                t0 = bi * GB
                tb = slice(t0, t0 + GB)
                ti_src = t0 // T
                if not state_only:
                    P.dma("sp", lambda h, tb=tb: h.dma_start(out=r32(qTb), in_=r32(qTs[:, :, tb].rearrange("c p t -> p c t"))),
                          reads=[R("qT", i, ti_src) for i in range(KC)], writes=[r_q])
                    P.dma("sp", lambda h, tb=tb: h.dma_start(out=r32(kTb), in_=r32(kTs[:, :, tb].rearrange("c p t -> p c t"))),
                          reads=[R("kT", i, ti_src) for i in range(KC)], writes=[r_k])
                    P.dma("sp", lambda h, tb=tb: h.dma_start(out=r32(gate), in_=r32(gTs[:, :, tb].rearrange("c p t -> p c t"))),
                          reads=[R("gT", i, ti_src) for i in range(GC)], writes=[r_gate])
                P.dma("sp", lambda h, tb=tb: h.dma_start(out=r32(ktmb), in_=r32(ktm[tb, :].rearrange("(c p) w -> p c w", p=128))),
                      reads=[R("ktm", ti_src)], writes=[r_ktm])
                P.dma("sp", lambda h, tb=tb: h.dma_start(out=r32(vtmb), in_=r32(vtm[tb, :].rearrange("(c p) w -> p c w", p=128))),
                      reads=[R("vtm", ti_src)], writes=[r_vtm])
                P.dma("sp", lambda h, tb=tb: h.dma_start(out=r32(glr[0:RK, :]), in_=r32(lrs[:, tb])), reads=[R("lr", ti_src)], writes=[r_glr])
                for kc in range(KC):
                    b = pb()
                    P.op("pe", lambda h, kc=kc, b=b: h.matmul(ps[b][:, 0:GB], r32(wgk[0:RK, kc * 128:(kc + 1) * 128]), r32(glr[0:RK, :]),
                                                              start=True, stop=True), reads=[r_wgk, r_glr], writes=[r_ps[b]])
                    P.op("act", lambda h, kc=kc, b=b: h.activation(out=r32(eq[:, kc, :]), in_=ps[b][:, 0:GB], func=AF.Sigmoid,
                                                                   bias=vcol(cfg.V_BGK, kc), scale=1.0),
                         reads=[r_ps[b], r_vec], writes=[r_eq])
                P.op("act", lambda h: h.activation(out=r32(eq), in_=eq, func=AF.Ln), reads=[r_eq], writes=[r_eq])
                for kc in range(KC):
                    P.op("dve", lambda h, kc=kc: h.tensor_tensor_scan(out=r32(cum[:, kc, :]), data0=rmask[:, 0:GB], data1=eq[:, kc, :],
                                                                      initial=0.0, op0=ALU.mult, op1=ALU.add),
                         reads=[r_eq, r_c], writes=[r_cum])
                P.op("act", lambda h: h.activation(out=r32(eq), in_=cum, func=AF.Exp, scale=1.0 / 16.0), reads=[r_cum], writes=[r_eq])
                if not state_only:
                    P.op("act", lambda h: h.activation(out=r32(cum), in_=cum, func=AF.Exp, scale=-1.0 / 16.0), reads=[r_cum], writes=[r_cum])
                    P.op("dve", lambda h: h.scalar_tensor_tensor(out=r32(qTb), in0=qTb, scalar=float(DK) ** -0.5, in1=eq,
                                                                 op0=ALU.mult, op1=ALU.mult), reads=[r_q, r_eq], writes=[r_q])
                    P.op("pool", lambda h: h.tensor_tensor(out=r32(kTb), in0=kTb, in1=cum, op=ALU.mult), reads=[r_k, r_cum], writes=[r_k])
                for c in range(NCH):
                    cs = slice(c * CH, (c + 1) * CH)
                    for hf in range(KW // 512):
                        b = pb()
                        hs = slice(hf * 512, (hf + 1) * 512)
                        P.op("pe", lambda h, cs=cs, hs=hs, b=b: h.matmul(ps[b][:], r32(glr[0:RK, cs]), r32(wgk[0:RK, hs]), start=True, stop=False),
                             reads=[r_glr, r_wgk], writes=[r_ps[b]])
                        P.op("pe", lambda h, hs=hs, b=b: h.matmul(ps[b][:], r32(ones[0:1, :]), r32(bgk[0:1, hs]), start=False, stop=True),
                             reads=[r_c, r_wgk], writes=[r_ps[b]])
                        P.op("act", lambda h, hs=hs, b=b: h.activation(out=r32(gtm[:, hs]), in_=ps[b][:], func=AF.Sigmoid),
                             reads=[r_ps[b]], writes=[r_gtm])
                    P.op("act", lambda h: h.activation(out=r32(gtm), in_=gtm, func=AF.Ln), reads=[r_gtm], writes=[r_gtm])
                    for hf in range(KW // 512):
                        b = pb()
                        hs = slice(hf * 512, (hf + 1) * 512)
                        P.op("pe", lambda h, hs=hs, b=b: h.matmul(ps[b][:], r32(Umask[:]), r32(gtm[:, hs]), start=True, stop=True),
                             reads=[r_c, r_gtm], writes=[r_ps[b]])
                        P.op("act", lambda h, hs=hs, b=b: h.activation(out=r32(kdec[:, hs]), in_=ps[b][:], func=AF.Exp, scale=1.0 / 16.0),
                             reads=[r_ps[b]], writes=[r_kdec])
                    P.op("dve", lambda h, c=c: h.tensor_tensor(out=r32(kdec), in0=kdec, in1=ktmb[:, c, :], op=ALU.mult),
                         reads=[r_kdec, r_ktm], writes=[r_kdec])
                    for hh in range(H):
                        if not state_only:
                            b = pb()
                            for k2 in range(KC2):
                                P.op("pe", lambda h, hh=hh, k2=k2, cs=cs, b=b: h.matmul(
                                    ps[b][:, 0:CH], r32(kTb[:, hh * KC2 + k2, cs]), r32(qTb[:, hh * KC2 + k2, cs]),
                                    start=(k2 == 0), stop=(k2 == KC2 - 1)), reads=[r_k, r_q], writes=[r_ps[b]])
                            ai = (c * H + hh) % 2
                            P.op("dve", lambda h, b=b, ai=ai: h.tensor_tensor(out=r32(attT[ai]), in0=ps[b][:, 0:CH], in1=maskT[:], op=ALU.mult),
                                 reads=[r_ps[b], r_c], writes=[r_att[ai]])
                            b2 = pb()
                            for vc in range(VC):
                                osl = slice(vc * 128, (vc + 1) * 128)
                                P.op("pe", lambda h, hh=hh, vc=vc, c=c, ai=ai, b2=b2, osl=osl: h.matmul(
                                    ps[b2][:, osl], r32(vtmb[:, c, hh * DV + vc * 128: hh * DV + (vc + 1) * 128]), r32(attT[ai]),
                                    start=True, stop=False), reads=[r_vtm, r_att[ai]], writes=[r_ps[b2]])
                                for k2 in range(KC2):
                                    so = (k2 * H + hh) * DV + vc * 128
                                    P.op("pe", lambda h, hh=hh, k2=k2, cs=cs, b2=b2, osl=osl, so=so: h.matmul(
                                        ps[b2][:, osl], r32(S_r[:, so:so + 128]), r32(qTb[:, hh * KC2 + k2, cs]),
                                        start=False, stop=(k2 == KC2 - 1)), reads=[r_Sr[hh], r_q], writes=[r_ps[b2]])
                            P.op("act", lambda h, hh=hh, cs=cs, b2=b2: h.activation(
                                out=r32(oT[:, hh * VC:(hh + 1) * VC, cs]), in_=ps[b2][:, 0:VC * 128].rearrange("p (v t) -> p v t", t=128),
                                func=AF.Copy), reads=[r_ps[b2]], writes=[r_oT])
                        for k2 in range(KC2):
                            b = pb()
                            so = (k2 * H + hh) * DV
                            P.op("pe", lambda h, hh=hh, k2=k2, c=c, b=b: h.matmul(
                                ps[b][:, 0:DV], r32(kdec[:, hh * DK + k2 * 128: hh * DK + (k2 + 1) * 128]), r32(vtmb[:, c, hh * DV:(hh + 1) * DV]),
                                start=True, stop=True), reads=[r_kdec, r_vtm], writes=[r_ps[b]])
                            col = c * CH + CH - 1
                            P.op("dve", lambda h, hh=hh, k2=k2, b=b, so=so, col=col: h.scalar_tensor_tensor(
                                out=S[:, so:so + DV], in0=S[:, so:so + DV], scalar=eq[:, hh * KC2 + k2, col:col + 1], in1=ps[b][:, 0:DV],
                                op0=ALU.mult, op1=ALU.add), reads=[r_S[hh], r_eq, r_ps[b]], writes=[r_S[hh]])
                            P.op("pool", lambda h, so=so: h.tensor_copy(out=r32(S_r[:, so:so + DV]), in_=S[:, so:so + DV]),
                                 reads=[r_S[hh]], writes=[r_Sr[hh]])
                if state_only:
                    continue
                P.op("act", lambda h: h.activation(out=r32(gate), in_=gate, func=AF.Silu), reads=[r_gate], writes=[r_gate])
                for hh in range(H):
                    b = pb()
                    for vc in range(VC):
                        si = vc % 2
                        P.op("act", lambda h, hh=hh, vc=vc, si=si: h.activation(out=r32(sqm[si]), in_=oT[:, hh * VC + vc, :], func=AF.Square),
                             reads=[r_oT], writes=[r_sqm[si]])
                        P.op("pe", lambda h, vc=vc, si=si, b=b: h.matmul(ps[b][:, 0:GB], r32(ones[:]), r32(sqm[si]), start=(vc == 0), stop=(vc == VC - 1)),
                             reads=[r_sqm[si], r_c], writes=[r_ps[b]])
                    P.op("act", lambda h, b=b: h.activation(out=rstm, in_=ps[b][:, 0:GB], func=AF.Sqrt, bias=EPS, scale=1.0 / DV),
                         reads=[r_ps[b]], writes=[r_rstm])
                    P.op("dve", lambda h: h.reciprocal(out=rstm, in_=rstm), reads=[r_rstm], writes=[r_rstm])
                    for vc in range(VC):
                        ch = hh * VC + vc
                        P.op("dve", lambda h, ch=ch: h.tensor_tensor(out=r32(oT[:, ch, :]), in0=oT[:, ch, :], in1=rstm, op=ALU.mult),
                             reads=[r_oT, r_rstm], writes=[r_oT])
                        P.op("dve", lambda h, ch=ch, vc=vc: h.scalar_tensor_tensor(out=r32(oT[:, ch, :]), in0=oT[:, ch, :], scalar=vcol(cfg.V_GNW, vc),
                                                                                     in1=gate[:, ch, :], op0=ALU.mult, op1=ALU.mult),
                             reads=[r_oT, r_gate, r_vec], writes=[r_oT])
                P.dma("pool", lambda h, tb=tb: h.dma_start(out=mTs[0:GC, :, tb].rearrange("c p t -> p c t"), in_=oT),
                      reads=[r_oT], writes=[R("mT", i, ti_src) for i in range(MC)])

        S5DBG = int(os.environ.get("S5DBG", "9"))

        def s5_pass(state_only):
            NB = 512
            NBK = NTOK // NB
            o5 = [0]
            def t5(n):
                v = o5[0]
                o5[0] += n
                return v
            uTc = AV(t5(NTOK), NTOK)
            EinR, EinI, Ec, Es = (AV(t5(NB), NB) for _ in range(4))
            wks = [[AV(t5(NB), NB) for _ in range(10)] for _ in range(2)]
            wk = wks[0]
            tinys = [tiny[:, 0:2], tiny[:, 2:4]]
            r_tinys = [R("tiny", 0), R("tiny", 1)]
            rtab = AV(t5(NB), NB)
            r_rtab = R("rtab")
            Bw = [AV(t5(256), 256) for _ in range(2)]
            Cw = [AV(t5(256), 256) for _ in range(2)]
            r_u, r_tab = R("uTc"), R("s5tab")
            angf, angg = AF_(0, NB), AF_(NB, NB)
            r_angf, r_angg = R("angf"), R("angg")
            tabsets = [(EinR, EinI, Ec, Es, rtab, angf, angg, r_tab, r_rtab, r_angf, r_angg),
                       (AV(t5(NB), NB), AV(t5(NB), NB), AV(t5(NB), NB), AV(t5(NB), NB), AV(t5(NB), NB), AF_(2 * NB, NB), AF_(3 * NB, NB),
                        R("s5tab", 1), R("rtab", 1), R("angf", 1), R("angg", 1))]
            r_wks = [[R("s5wk", j, i) for i in range(10)] for j in range(2)]
            r_wk = r_wks[0]
            r_bw = [R("s5bw", i) for i in range(2)]
            r_cw = [R("s5cw", i) for i in range(2)]
            for fc in range(SC):
                P.dma("sp", lambda h, fc=fc: h.dma_start(out=r32(uTc), in_=r32(uTs[fc])), reads=[R("uT", fc, ti) for ti in range(NT)], writes=[r_u])
                def s5_gp(g4, EinR, EinI, Ec, Es, rtab, angf, angg, r_tab, r_rtab, r_angf, r_angg):
                    gp = fc * 4 + g4
                    wb = gp % 2
                    P.dma("sp", lambda h, gp=gp, wb=wb: h.dma_start(out=r32(Bw[wb]), in_=r32(s5B[gp])), writes=[r_bw[wb]])
                    if not state_only:
                        P.dma("sp", lambda h, gp=gp, wb=wb: h.dma_start(out=r32(Cw[wb]), in_=r32(s5C[gp])), writes=[r_cw[wb]])
                    TW = [r_tab, r_s5p]
                    P.op("act", lambda h, gp=gp: h.activation(out=angf, in_=iot[:, 0:NB], func=AF.Identity, scale=pcol(P_TH, gp)),
                         reads=[r_c, r_s5p], writes=[r_angf])
                    sincos(angf, angg, Es, Ec, r_angf, r_angg, r_tab, True)
                    P.op("act", lambda h, gp=gp: h.activation(out=r32(rtab), in_=iot[:, 0:NB], func=AF.Identity, bias=pcol(P_R, gp), scale=0.0),
                         reads=[r_c, r_s5p], writes=[r_rtab])
                    P.op("act", lambda h, gp=gp: h.activation(out=r32(wk[0]), in_=Es, func=AF.Identity, scale=pcol(P_FI, gp)),
                         reads=[r_tab, r_s5p], writes=[r_wk[0]])
                    P.op("dve", lambda h, gp=gp: h.scalar_tensor_tensor(out=r32(EinR), in0=Ec, scalar=pcol(P_FR, gp), in1=wk[0], op0=ALU.mult, op1=ALU.add),
                         reads=[r_tab, r_s5p, r_wk[0]], writes=[r_tab])
                    P.op("act", lambda h, gp=gp: h.activation(out=r32(wk[1]), in_=Es, func=AF.Identity, scale=pcol(P_FR, gp)),
                         reads=[r_tab, r_s5p], writes=[r_wk[1]])
                    P.op("dve", lambda h, gp=gp: h.scalar_tensor_tensor(out=r32(EinI), in0=Ec, scalar=pcol(P_FI, gp), in1=wk[1], op0=ALU.mult, op1=ALU.subtract),
                         reads=[r_tab, r_s5p, r_wk[1]], writes=[r_tab])
                    def s5_block(bk, wk, r_wk, tn, r_tn):
                        bs = slice(bk * NB, (bk + 1) * NB)
                        pr, pi_ = (0, 1) if (gp * NBK + bk) % 2 == 0 else (2, 3)
                        P.op("pe", lambda h, wb=wb, bs=bs, pr=pr: h.matmul(ps[pr][:], r32(Bw[wb][:, 0:128]), r32(uTc[:, bs]), start=True, stop=True),
                             reads=[r_bw[wb], r_u], writes=[r_ps[pr]])
                        P.op("pe", lambda h, wb=wb, bs=bs, pi_=pi_: h.matmul(ps[pi_][:], r32(Bw[wb][:, 128:256]), r32(uTc[:, bs]), start=True, stop=True),
                             reads=[r_bw[wb], r_u], writes=[r_ps[pi_]])
                        e1, e2 = ("dve", "pool")
                        P.op("act", lambda h, pr=pr: h.activation(out=r32(wk[4]), in_=ps[pr][:], func=AF.Copy), reads=[r_ps[pr]], writes=[r_wk[4]])
                        P.op("act", lambda h, pi_=pi_: h.activation(out=r32(wk[5]), in_=ps[pi_][:], func=AF.Copy), reads=[r_ps[pi_]], writes=[r_wk[5]])
                        P.op("dve", lambda h: h.tensor_tensor(out=r32(wk[2]), in0=wk[4], in1=EinR, op=ALU.mult), reads=[r_wk[4], r_tab], writes=[r_wk[2]])
                        P.op("dve", lambda h: h.tensor_tensor(out=r32(wk[3]), in0=wk[5], in1=EinI, op=ALU.mult), reads=[r_wk[5], r_tab], writes=[r_wk[3]])
                        P.op("dve", lambda h: h.tensor_tensor(out=r32(wk[2]), in0=wk[2], in1=wk[3], op=ALU.subtract), reads=[r_wk[2], r_wk[3]], writes=[r_wk[2]])
                        P.op("dve", lambda h: h.tensor_tensor(out=r32(wk[4]), in0=wk[4], in1=EinI, op=ALU.mult), reads=[r_wk[4], r_tab], writes=[r_wk[4]])
                        P.op("dve", lambda h: h.tensor_tensor(out=r32(wk[5]), in0=wk[5], in1=EinR, op=ALU.mult), reads=[r_wk[5], r_tab], writes=[r_wk[5]])
                        P.op("dve", lambda h: h.tensor_tensor(out=r32(wk[4]), in0=wk[4], in1=wk[5], op=ALU.add), reads=[r_wk[4], r_wk[5]], writes=[r_wk[4]])
                        P.op("dve", lambda h, gp=gp: h.scalar_tensor_tensor(out=r32(wk[2][:, 0:1]), in0=zc[:, 2 * gp:2 * gp + 1], scalar=pcol(P_R, gp),
                                                                            in1=wk[2][:, 0:1], op0=ALU.mult, op1=ALU.add),
                             reads=[r_wk[2], r_s5p, r_zc], writes=[r_wk[2]])
                        P.op("dve", lambda h, gp=gp: h.scalar_tensor_tensor(out=r32(wk[4][:, 0:1]), in0=zc[:, 2 * gp + 1:2 * gp + 2], scalar=pcol(P_R, gp),
                                                                            in1=wk[4][:, 0:1], op0=ALU.mult, op1=ALU.add),
                             reads=[r_wk[4], r_s5p, r_zc], writes=[r_wk[4]])
                        P.op("dve", lambda h: h.tensor_tensor_scan(out=r32(wk[6]), data0=rtab, data1=wk[2], initial=0.0, op0=ALU.mult, op1=ALU.add),
                             reads=[r_wk[2], r_rtab], writes=[r_wk[6]])
                        P.op("dve", lambda h: h.tensor_tensor_scan(out=r32(wk[7]), data0=rtab, data1=wk[4], initial=0.0, op0=ALU.mult, op1=ALU.add),
                             reads=[r_wk[4], r_rtab], writes=[r_wk[7]])
                        zl = NB - 1
                        P.op("act", lambda h, gp=gp: h.activation(out=tn[:, 0:1], in_=wk[7][:, zl:zl + 1], func=AF.Identity, scale=pcol(P_SN, gp)),
                             reads=[r_wk[7], r_s5p], writes=[r_tn])
                        P.op("act", lambda h, gp=gp: h.activation(out=tn[:, 1:2], in_=wk[6][:, zl:zl + 1], func=AF.Identity, scale=pcol(P_SN, gp)),
                             reads=[r_wk[6], r_s5p], writes=[r_tn])
                        P.op("dve", lambda h, gp=gp: h.scalar_tensor_tensor(out=zc[:, 2 * gp:2 * gp + 1], in0=wk[6][:, zl:zl + 1], scalar=pcol(P_CN, gp),
                                                                            in1=tn[:, 0:1], op0=ALU.mult, op1=ALU.subtract),
                             reads=[r_wk[6], r_s5p, r_tn], writes=[r_zc])
                        P.op("dve", lambda h, gp=gp: h.scalar_tensor_tensor(out=zc[:, 2 * gp + 1:2 * gp + 2], in0=wk[7][:, zl:zl + 1], scalar=pcol(P_CN, gp),
                                                                            in1=tn[:, 1:2], op0=ALU.mult, op1=ALU.add),
                             reads=[r_wk[7], r_s5p, r_tn], writes=[r_zc])
                        if state_only or S5DBG == 0:
                            return
                        P.op("dve", lambda h: h.tensor_tensor(out=r32(wk[3]), in0=wk[7], in1=Es, op=ALU.mult), reads=[r_wk[7], r_tab], writes=[r_wk[3]])
                        P.op("dve", lambda h: h.tensor_tensor(out=r32(wk[8]), in0=wk[6], in1=Ec, op=ALU.mult), reads=[r_wk[6], r_tab], writes=[r_wk[8]])
                        P.op("dve", lambda h: h.tensor_tensor(out=r32(wk[8]), in0=wk[8], in1=wk[3], op=ALU.subtract), reads=[r_wk[8], r_wk[3]], writes=[r_wk[8]])
                        if S5DBG == 2:
                            return
                        P.op("dve", lambda h: h.tensor_tensor(out=r32(wk[5]), in0=wk[6], in1=Es, op=ALU.mult), reads=[r_wk[6], r_tab], writes=[r_wk[5]])
                        P.op("dve", lambda h: h.tensor_tensor(out=r32(wk[9]), in0=wk[7], in1=Ec, op=ALU.mult), reads=[r_wk[7], r_tab], writes=[r_wk[9]])
                        if S5DBG == 3:
                            return
                        P.op("dve", lambda h: h.tensor_tensor(out=r32(wk[9]), in0=wk[9], in1=wk[5], op=ALU.add), reads=[r_wk[9], r_wk[5]], writes=[r_wk[9]])
                        P.op("act", lambda h: h.activation(out=r32(wk[9]), in_=wk[9], func=AF.Identity, scale=-1.0), reads=[r_wk[9]], writes=[r_wk[9]])
                        py = 4 + bk
                        if S5DBG == 1:
                            return
                        S5V = os.environ.get("S5V", "")
                        if S5V == "":
                            P.op("pe", lambda h, wb=wb, py=py, g4=g4: h.matmul(ps[py][:], r32(Cw[wb][:, 0:128]), r32(wk[8]), start=(g4 == 0), stop=False),
                                 reads=[r_cw[wb], r_wk[8]], writes=[r_ps[py]])
                            P.op("pe", lambda h, wb=wb, py=py, g4=g4: h.matmul(ps[py][:], r32(Cw[wb][:, 128:256]), r32(wk[9]), start=False, stop=(g4 == 3)),
                                 reads=[r_cw[wb], r_wk[9]], writes=[r_ps[py]])
                        elif S5V == "rhs":
                            P.op("pe", lambda h, wb=wb, py=py, g4=g4, bs=bs: h.matmul(ps[py][:], r32(Cw[wb][:, 0:128]), r32(uTc[:, bs]), start=(g4 == 0), stop=False),
                                 reads=[r_cw[wb], r_u], writes=[r_ps[py]])
                            P.op("pe", lambda h, wb=wb, py=py, g4=g4, bs=bs: h.matmul(ps[py][:], r32(Cw[wb][:, 128:256]), r32(uTc[:, bs]), start=False, stop=(g4 == 3)),
                                 reads=[r_cw[wb], r_u], writes=[r_ps[py]])
                        elif S5V == "sync":
                            P.op("pe", lambda h, wb=wb, py=py, g4=g4, bs=bs: h.matmul(ps[py][:], r32(Cw[wb][:, 0:128]), r32(uTc[:, bs]), start=(g4 == 0), stop=False),
                                 reads=[r_cw[wb], r_u, r_wk[8]], writes=[r_ps[py]])
                            P.op("pe", lambda h, wb=wb, py=py, g4=g4, bs=bs: h.matmul(ps[py][:], r32(Cw[wb][:, 128:256]), r32(uTc[:, bs]), start=False, stop=(g4 == 3)),
                                 reads=[r_cw[wb], r_u, r_wk[9]], writes=[r_ps[py]])
                        elif S5V == "lhs":
                            P.op("pe", lambda h, wb=wb, py=py, g4=g4: h.matmul(ps[py][:], r32(Bw[wb][:, 0:128]), r32(wk[8]), start=(g4 == 0), stop=False),
                                 reads=[r_bw[wb], r_wk[8]], writes=[r_ps[py]])
                            P.op("pe", lambda h, wb=wb, py=py, g4=g4: h.matmul(ps[py][:], r32(Bw[wb][:, 128:256]), r32(wk[9]), start=False, stop=(g4 == 3)),
                                 reads=[r_bw[wb], r_wk[9]], writes=[r_ps[py]])
                    for bk in range(NBK):
                        st_ = (gp * NBK + bk) % 2
                        s5_block(bk, wks[st_], r_wks[st_], tinys[st_], r_tinys[st_])
                for g4 in range(4):
                    s5_gp(g4, *tabsets[(fc * 4 + g4) % 2])
                if state_only or os.environ.get("S5NOOUT"):
                    continue
                for bk in range(NBK):
                    bs = slice(bk * NB, (bk + 1) * NB)
                    py = 4 + bk
                    P.op("dve", lambda h, fc=fc, bs=bs, py=py: h.scalar_tensor_tensor(out=r32(wk[0]), in0=uTc[:, bs], scalar=vcol(cfg.V_S5D, fc), in1=ps[py][:],
                                                                                      op0=ALU.mult, op1=ALU.add), reads=[r_u, r_vec, r_ps[py]], writes=[r_wk[0]])
                    P.op("act", lambda h: h.activation(out=r32(wk[1]), in_=wk[0], func=AF.Square), reads=[r_wk[0]], writes=[r_wk[1]])
                    P.op("dve", lambda h: h.tensor_scalar(out=r32(wk[1]), in0=wk[1], scalar1=0.044715, scalar2=1.0, op0=ALU.mult, op1=ALU.add),
                         reads=[r_wk[1]], writes=[r_wk[1]])
                    P.op("dve", lambda h: h.tensor_tensor(out=r32(wk[1]), in0=wk[1], in1=wk[0], op=ALU.mult), reads=[r_wk[1], r_wk[0]], writes=[r_wk[1]])
                    P.op("act", lambda h: h.activation(out=r32(wk[1]), in_=wk[1], func=AF.Sigmoid, scale=2.0 * math.sqrt(2.0 / math.pi)), reads=[r_wk[1]], writes=[r_wk[1]])
                    P.op("dve", lambda h: h.tensor_tensor(out=r32(wk[0]), in0=wk[0], in1=wk[1], op=ALU.mult), reads=[r_wk[0], r_wk[1]], writes=[r_wk[0]])
                    P.dma("pool", lambda h, fc=fc, bs=bs: h.dma_start(out=yas[fc, :, bs], in_=wk[0]), reads=[r_wk[0]], writes=[R("ya", fc)])

        def glu():
            y2 = AV(0, SC * 512, 512)
            yat = AV(SC * 512, SC * 512, 512)
            gw = [AV(2 * SC * 512 + i * SC * 128, SC * 128) for i in range(3)]
            r_y2, r_ya2 = [R("y2", i) for i in range(SC)], [R("yat", i) for i in range(SC)]
            r_gw = [R("gw", i) for i in range(3)]
            sqg = [AV(2 * SC * 512 + 3 * SC * 128 + i * 512, 512) for i in range(2)]
            r_sqg = [R("sqg", i) for i in range(2)]
            r_rs2 = R("rsg")
            rsg = AF_(0, 512)
            k = 0
            for bk in range(NTOK // 512):
                bs = slice(bk * 512, (bk + 1) * 512)
                for c in range(SC):
                    P.dma("sp", lambda h, c=c, bs=bs: h.dma_start(out=r32(yat[:, c, :]), in_=r32(yas[c, :, bs])), reads=[R("ya", c)], writes=[r_ya2[c]])
                for oc in range(SC):
                    wi = k % 3
                    k += 1
                    P.dma("sp", lambda h, oc=oc, wi=wi: h.dma_start(out=r32(gw[wi]), in_=r32(w_glu[oc])), writes=[r_gw[wi]])
                    b = k % 4
                    for kc in range(SC):
                        P.op("pe", lambda h, kc=kc, wi=wi, b=b: h.matmul(ps[b][:], r32(gw[wi][:, kc * 128:(kc + 1) * 128]), r32(yat[:, kc, :]),
                                                                        start=(kc == 0), stop=(kc == SC - 1)), reads=[r_gw[wi], r_ya2[kc]], writes=[r_ps[b]])
                    P.op("act", lambda h, oc=oc, b=b: h.activation(out=r32(y2[:, oc, :]), in_=ps[b][:], func=AF.Sigmoid, bias=vcol(cfg.V_BGLU, oc), scale=1.0),
                         reads=[r_ps[b], r_vec], writes=[r_y2[oc]])
                    P.op("dve", lambda h, oc=oc: h.tensor_tensor(out=r32(y2[:, oc, :]), in0=y2[:, oc, :], in1=yat[:, oc, :], op=ALU.mult),
                         reads=[r_y2[oc], r_ya2[oc]], writes=[r_y2[oc]])
                    si = oc % 2
                    P.op("act", lambda h, oc=oc, si=si: h.activation(out=r32(sqg[si]), in_=y2[:, oc, :], func=AF.Square), reads=[r_y2[oc]], writes=[r_sqg[si]])
                    P.op("pe", lambda h, oc=oc, si=si: h.matmul(ps[7][:], r32(ones[:]), r32(sqg[si]), start=(oc == 0), stop=(oc == SC - 1)),
                         reads=[r_sqg[si], r_c], writes=[r_ps[7]])
                P.op("act", lambda h: h.activation(out=rsg, in_=ps[7][:], func=AF.Sqrt, bias=EPS, scale=1.0 / SW), reads=[r_ps[7]], writes=[r_rs2])
                P.op("dve", lambda h: h.reciprocal(out=rsg, in_=rsg), reads=[r_rs2], writes=[r_rs2])
                for oc in range(SC):
                    P.op("dve", lambda h, oc=oc: h.scalar_tensor_tensor(out=r32(y2[:, oc, :]), in0=y2[:, oc, :], scalar=vcol(cfg.V_S5NW, oc), in1=rsg,
                                                                        op0=ALU.mult, op1=ALU.mult), reads=[r_y2[oc], r_rs2, r_vec], writes=[r_y2[oc]])
                    P.dma("pool", lambda h, oc=oc, bs=bs: h.dma_start(out=mTs[GC + oc, :, bs], in_=y2[:, oc, :]), reads=[r_y2[oc]],
                          writes=[R("mT", i, bk) for i in range(MC)])

        SROWS = SDIM
        def zero_state():
            for hh in range(H):
                pass
            P.op("dve", lambda h: h.memset(S, 0.0), writes=r_S)
            P.op("dve", lambda h: h.memset(zc, 0.0), writes=[r_zc])
            P.op("pool", lambda h: h.tensor_copy(out=r32(S_r), in_=S), reads=r_S, writes=r_Sr)

        SUB = os.environ.get("MIXSUB", "params,gla,s5,glu").split(",")
        if "params" in SUB:
            s5_params()
        if cfg.pair:
            zero_state()
            gla_pass(True)
            RG = [[2 * i, 2 * i + 1] for i in range(cfg.ncores // 2)]
            for k in range(4):
                P.dma("pool", lambda h, k=k: h.dma_start(out=st_src[k].rearrange("(p j) w -> p (j w)", p=32), in_=S[32 * k:32 * k + 32, :]),
                      reads=r_S, writes=[R("stsrc", k)])
            P.barrier()
            s5_pass(True)
            P.dma("pool", lambda h: h.dma_start(out=sz_src[:, 0:2 * NGP], in_=zc), reads=[r_zc], writes=[R("szsrc")])
            P.barrier()
            for k in range(4):
                P.coll("pool", lambda h, k=k: h.collective_compute("AllGather", ALU.bypass, replica_groups=RG, ins=[st_src[k][:]], outs=[st_dst[k][:]]),
                       reads=[R("stsrc", k)], writes=[R("stdst", k)])
            P.coll("pool", lambda h: h.collective_compute("AllGather", ALU.bypass, replica_groups=RG, ins=[sz_src[:]], outs=[sz_dst[:]]),
                   reads=[R("szsrc")], writes=[R("szdst")])
            for k in range(4):
                P.dma("pool", lambda h, k=k: h.dma_start(out=S[32 * k:32 * k + 32, :], in_=st_dst[k][0:32 * SJ, :].rearrange("(p j) w -> p (j w)", p=32)),
                      reads=[R("stdst", k)], writes=r_S)
            P.dma("pool", lambda h: h.dma_start(out=zc, in_=sz_dst[0:128, 0:2 * NGP]), reads=[R("szdst")], writes=[r_zc])
            P.op("act", lambda h: h.activation(out=S, in_=S, func=AF.Identity, scale=flg[:, 0:1]), reads=r_S + [r_vec], writes=r_S)
            P.op("act", lambda h: h.activation(out=zc, in_=zc, func=AF.Identity, scale=flg[:, 0:1]), reads=[r_zc, r_vec], writes=[r_zc])
            P.op("pool", lambda h: h.tensor_copy(out=r32(S_r), in_=S), reads=r_S, writes=r_Sr)
        else:
            zero_state()
        if "gla" in SUB:
            gla_pass(False)
        P.barrier()
        if "s5so" in SUB:
            s5_pass(True)
            P.barrier()
        if "s5" in SUB:
            s5_pass(False)
            P.barrier()
        if "glu" in SUB:
            glu()
        P.barrier()

    outs = []
    if "B" in stages:
        for ti in range(NT):
            load_tile(mTs, MC, ti, "mT")
            for m in range(DC):
                sw = wload(w_out[m], MC * 128)
                pa = 4 + nxt("pa", 2)
                for kc in range(MC):
                    P.op("pe", lambda h, kc=kc, sw=sw, pa=pa: h.matmul(ps[pa][:], r32(wsl[sw][:, kc * 128:(kc + 1) * 128]), r32(nT[:, kc, :]),
                                                                      start=(kc == 0), stop=(kc == MC - 1)),
                         reads=[r_w[sw], r_nT[kc]], writes=[r_ps[pa]])
                rbi = nxt("rb", 2)
                P.dma("pool", lambda h, m=m, rbi=rbi, ti=ti: h.dma_start(out=rb[rbi], in_=res[m, :, tsl(ti)]),
                      reads=[R("res", m, ti)], writes=[r_rb[rbi]])
                obi = nxt("ob", 2)
                P.op("dve", lambda h, m=m, pa=pa, rbi=rbi, obi=obi: h.scalar_tensor_tensor(
                    out=ob[obi], in0=ps[pa][:], scalar=mcol(5, m), in1=rb[rbi], op0=ALU.mult, op1=ALU.add),
                    reads=[r_ps[pa], r_rb[rbi], r_mod], writes=[r_ob[obi]])
                P.dma("pool", lambda h, m=m, obi=obi, ti=ti: h.dma_start(out=res[m, :, tsl(ti)], in_=ob[obi]),
                      reads=[r_ob[obi]], writes=[R("res", m, ti)])
            load_tile(res, DC, ti, "res")
            modnorm(6, 7)
            ffn(ti, wg[1], wu[1], wd[1], 8, res, "res")
            sumsq(DC, D)
            for c in range(DC):
                obi = nxt("ob", 2)
                P.op("dve", lambda h, c=c, obi=obi: h.scalar_tensor_tensor(out=ob[obi], in0=nT[:, c, :], scalar=vcol(cfg.V_FNW, c), in1=rstd,
                                                                          op0=ALU.mult, op1=ALU.mult),
                     reads=[r_nT[c], r_rs, r_vec], writes=[r_ob[obi]])
                outs.append(P.dma("pool", lambda h, c=c, obi=obi, ti=ti: h.dma_start(out=outT[c, :, tsl(ti)], in_=ob[obi]),
                                  reads=[r_ob[obi]]))
    if not outs:
        P.op("pool", lambda h: h.memset(arena[:, 0:NTOK], 0.0))
        for c in range(DC):
            outs.append(P.dma("pool", lambda h, c=c: h.dma_start(out=outT[c], in_=arena[:, 0:NTOK])))
    P.emit(outs)
    return nc


def build_mixer(cfg, nc, P, env):
    pass


def tile_kxn(w, ncols):
    K, N = w.shape
    a = w.reshape(K // 128, 128, N // ncols, ncols).transpose(2, 1, 0, 3)
    return np.ascontiguousarray(a).reshape(N // ncols, 128, (K // 128) * ncols)


def fm_vec(v):
    return np.ascontiguousarray(v.reshape(-1, 128).T)


def layout_shared(cfg, inp):
    L = 0
    m = {}
    wa = tile_kxn(inp["w_ada"][L], 512)
    nads = 3 if wa.shape[0] % 3 == 0 else 1
    for i in range(nads):
        m["w_ada%d" % i] = np.ascontiguousarray(wa[i * (wa.shape[0] // nads):(i + 1) * (wa.shape[0] // nads)])
    for i, pre in enumerate(("ffn1", "ffn2")):
        m["wg%d" % i] = tile_kxn(inp[pre + "_w_gate"][L], 128)
        m["wu%d" % i] = tile_kxn(inp[pre + "_w_up"][L], 128)
        m["wd%d" % i] = tile_kxn(inp[pre + "_w_down"][L], 128)
    w_in = inp["w_in"][L]
    KW, GW, SW, RK = cfg.KW, cfg.GW, cfg.SW, cfg.R
    o_q, o_k, o_v, o_g, o_lr, o_u = 0, KW, 2 * KW, 2 * KW + GW, 2 * KW + 2 * GW, 2 * KW + 2 * GW + RK
    fm_cols = np.concatenate([np.arange(o_q, o_q + KW), np.arange(o_k, o_k + KW), np.arange(o_g, o_g + GW), np.arange(o_u, o_u + SW)])
    m["win_fm"] = tile_kxn(w_in[:, fm_cols], 128)
    m["win_lr"] = tile_kxn(w_in[:, o_lr:o_lr + RK], RK)[0]
    tm_cols = np.concatenate([np.arange(o_k, o_k + KW), np.arange(o_v, o_v + GW)])
    m["win_tm"] = tile_kxn(w_in[:, tm_cols], 512)
    m["w_gk2"] = np.ascontiguousarray(inp["w_gk2"][L])
    m["b_gkr"] = np.ascontiguousarray(inp["b_gk"][L][None, :])
    m["w_glu"] = tile_kxn(inp["w_glu"][L], 128)
    m["w_out"] = tile_kxn(inp["w_out"][L], 128)
    vec = np.concatenate([fm_vec(inp["b_ada"][L]), fm_vec(inp["final_norm_w"]), fm_vec(inp["b_gk"][L]),
                          fm_vec(inp["gla_norm_w"][L]), fm_vec(inp["s5_d"][L]), fm_vec(inp["b_glu"][L]),
                          fm_vec(inp["s5_norm_w"][L])], axis=1)
    assert vec.shape == (128, cfg.NV), vec.shape
    m["vecs"] = np.ascontiguousarray(vec)
    NGP = cfg.NGP
    lam = np.zeros((128, 3 * NGP), np.float32)
    lam[:, 0:NGP] = inp["s5_lambda_re"][L].reshape(NGP, 128).T
    lam[:, NGP:2 * NGP] = inp["s5_lambda_im"][L].reshape(NGP, 128).T
    lam[:, 2 * NGP:] = np.repeat(inp["s5_log_dt"][L].reshape(NGP, 2), 64, axis=1).T
    m["s5lam"] = lam
    B = np.zeros((NGP, 128, 256), np.float32)
    C = np.zeros((NGP, 128, 256), np.float32)
    bre, bim = inp["s5_b_re"][L], inp["s5_b_im"][L]
    cre, cim = inp["s5_c_re"][L], inp["s5_c_im"][L]
    for gp in range(NGP):
        for g2 in range(2):
            g = 2 * gp + g2
            r0 = (g % 8) * 16
            B[gp, r0:r0 + 16, 64 * g2:64 * g2 + 64] = bre[g].T
            B[gp, r0:r0 + 16, 128 + 64 * g2:128 + 64 * g2 + 64] = bim[g].T
            C[gp, 64 * g2:64 * g2 + 64, r0:r0 + 16] = cre[g].T
            C[gp, 64 * g2:64 * g2 + 64, 128 + r0:128 + r0 + 16] = cim[g].T
    m["s5B"], m["s5C"] = B, C
    return m


def layout_core(cfg, inp, core, ncores_per_seq):
    b, half = divmod(core, ncores_per_seq)
    t0 = half * cfg.NTOK
    xs = inp["x"][b, t0:t0 + cfg.NTOK, :]
    m = {}
    m["xT"] = np.ascontiguousarray(xs.T).reshape(cfg.DC, 128, cfg.NTOK)
    m["cT"] = fm_vec(inp["c"][b])
    m["flag"] = np.full((128, 1), 1.0 if half > 0 else 0.0, np.float32)
    return m


_CACHE = {}


def kernel(**inputs):
    cfg = Cfg(**FULL)
    inp = {k: np.asarray(v) for k, v in inputs.items()}
    if "nc" not in _CACHE:
        _CACHE["nc"] = build(cfg)
    nc = _CACHE["nc"]
    shared = layout_shared(cfg, inp)
    in_maps = []
    for core in range(8):
        m = dict(shared)
        m.update(layout_core(cfg, inp, core, 2))
        in_maps.append(m)
    r = run_bass_kernel_spmd(nc, in_maps, core_ids=list(range(8)))
    out = np.empty((4, 4096, cfg.D), np.float32)
    for core in range(8):
        b, half = divmod(core, 2)
        oT = np.asarray(r.results[core]["outT"]).reshape(cfg.D, cfg.NTOK)
        out[b, half * cfg.NTOK:(half + 1) * cfg.NTOK, :] = oT.T
    return out
```

```python
import contextlib
import os
import math
import numpy as np
import concourse.bass as bass
import concourse.mybir as mybir
from concourse.bass_utils import run_bass_kernel_spmd

F32 = mybir.dt.float32
F32R = mybir.dt.float32r
AF = mybir.ActivationFunctionType
ALU = mybir.AluOpType

ENGS = ("pe", "act", "dve", "pool", "sp")


class Res:
    __slots__ = ("w", "rs")

    def __init__(self):
        self.w = None
        self.rs = []


class Op:
    __slots__ = ("eng", "fn", "deps", "inc", "val", "dma", "dsem", "dval", "prewait", "dinc")

    def __init__(self, eng, fn, dma=False):
        self.eng = eng
        self.fn = fn
        self.deps = []
        self.inc = False
        self.val = 0
        self.dma = dma
        self.dsem = None
        self.dval = 0
        self.prewait = None
        self.dinc = 16


class Prog:
    KDMA = 8

    def __init__(self, nc, same_engine_sync=True):
        self.nc = nc
        self.q = {e: [] for e in ENGS}
        self.ndma = {e: 0 for e in ENGS}
        self.same = same_engine_sync
        self.stack = contextlib.ExitStack()
        self.esem = {}
        self.dsems = {}
        self.lastdma = {}
        self.rd = {}

    def R(self, *key):
        r = self.rd.get(key)
        if r is None:
            r = self.rd[key] = Res()
        return r

    def sbuf(self, name, shape, dt=F32):
        return self.stack.enter_context(self.nc.sbuf_tensor(name, list(shape), dt))

    def psum(self, name, shape, dt=F32):
        return self.stack.enter_context(self.nc.psum_tensor(name, list(shape), dt))

    def sem(self, name):
        return self.stack.enter_context(self.nc.semaphore(name))

    def _track(self, op, reads, writes):
        deps = op.deps
        for r in reads:
            if r.w is not None:
                deps.append(r.w)
            if not op.dma:
                rs = r.rs
                for i in range(len(rs)):
                    if (not rs[i].dma) and rs[i].eng == op.eng:
                        rs[i] = op
                        break
                else:
                    rs.append(op)
            else:
                r.rs.append(op)
        for w in writes:
            if w.w is not None:
                deps.append(w.w)
            for x in w.rs:
                if x is not op:
                    deps.append(x)
            w.w = op
            w.rs = []
        for d in deps:
            if d.dma:
                continue
            if d.eng == op.eng and not op.dma and (op.eng == "pe" or not self.same):
                continue
            d.inc = True

    def op(self, eng, fn, reads=(), writes=()):
        o = Op(eng, fn)
        self._track(o, reads, writes)
        self.q[eng].append(o)
        return o

    def dma(self, eng, fn, reads=(), writes=()):
        o = Op(eng, fn, dma=True)
        self._track(o, reads, writes)
        n = self.ndma[eng]
        self.ndma[eng] = n + 1
        o.dsem = (eng, n % self.KDMA)
        o.dval = 16 * (n // self.KDMA + 1)
        if n >= self.KDMA:
            o.prewait = (o.dsem, 16 * (n // self.KDMA))
        self.lastdma[o.dsem] = o
        self.q[eng].append(o)
        return o

    def coll(self, eng, fn, reads=(), writes=()):
        o = Op(eng, fn, dma=True)
        self._track(o, reads, writes)
        self.ncoll = getattr(self, "ncoll", 0) + 1
        o.dsem = ("cc", self.ncoll)
        o.dval = 1
        o.dinc = 1
        self.lastdma[o.dsem] = o
        self.q[eng].append(o)
        return o

    def barrier(self):
        lasts = []
        for e in ENGS:
            for o in reversed(self.q[e]):
                if not o.dma and o.fn is not None:
                    lasts.append(o)
                    break
        lasts += list(self.lastdma.values())
        for e in ENGS:
            o = Op(e, None)
            o.deps = list(lasts)
            for d in lasts:
                d.inc = True
            self.q[e].append(o)

    def emit(self, final_wait_ops=()):
        nc = self.nc
        for e in ENGS:
            self.esem[e] = self.sem("es_" + e)
            for k in range(self.KDMA):
                if self.ndma[e] > k:
                    self.dsems[(e, k)] = self.sem("ds_%s%d" % (e, k))
        for i in range(getattr(self, "ncoll", 0)):
            self.dsems[("cc", i + 1)] = self.sem("cc_%d" % (i + 1))
        for e in ENGS:
            c = 0
            for o in self.q[e]:
                if not o.dma and o.inc and o.fn is not None:
                    c += 1
                o.val = c
        same = self.same

        def run(e, h):
            known = {}
            esem = self.esem
            for o in self.q[e]:
                if o.prewait is not None:
                    sk, v = o.prewait
                    if known.get(sk, 0) < v:
                        h.wait_ge(self.dsems[sk], v)
                        known[sk] = v
                for d in o.deps:
                    if d.dma:
                        sk, v, s = d.dsem, d.dval, self.dsems[d.dsem]
                    else:
                        if d.eng == e and (e == "pe" or not same):
                            continue
                        sk, v, s = d.eng, d.val, esem[d.eng]
                    if known.get(sk, 0) < v:
                        h.wait_ge(s, v)
                        known[sk] = v
                if o.fn is None:
                    continue
                ins = o.fn(h)
                if o.dma:
                    ins.then_inc(self.dsems[o.dsem], o.dinc)
                elif o.inc:
                    ins.then_inc(esem[e], 1)
            if e == "sp":
                for o in final_wait_ops:
                    sk, v = o.dsem, o.dval
                    if known.get(sk, 0) < v:
                        h.wait_ge(self.dsems[sk], v)
                        known[sk] = v

        with nc.Block() as block:
            @block.tensor
            def _(h):
                run("pe", h)

            @block.scalar
            def _(h):
                run("act", h)

            @block.vector
            def _(h):
                run("dve", h)

            @block.gpsimd
            def _(h):
                run("pool", h)

            @block.sync
            def _(h):
                run("sp", h)
        self.stack.close()


class Cfg:
    def __init__(self, D, DFF, NTOK, H, DK, DV, G, T=512, R=16, NS=4, pair=True, ncores=8):
        self.ncores = ncores
        self.D, self.DFF, self.NTOK, self.H, self.DK, self.DV, self.G = D, DFF, NTOK, H, DK, DV, G
        self.T, self.R, self.NS, self.pair = T, R, NS, pair
        self.DC = D // 128
        self.FC = DFF // 128
        self.NT = NTOK // T
        self.KW, self.GW, self.SW = H * DK, H * DV, 16 * G
        self.KC, self.GC, self.SC = self.KW // 128, self.GW // 128, self.SW // 128
        self.KC2, self.VC = DK // 128, DV // 128
        self.NGP = G // 2
        self.MC = self.GC + self.SC
        self.INW = 2 * self.KW + 2 * self.GW + R + self.SW
        c = 0
        self.V_BADA = c; c += 9 * self.DC
        self.V_FNW = c; c += self.DC
        self.V_BGK = c; c += self.KC
        self.V_GNW = c; c += self.VC
        self.V_S5D = c; c += self.SC
        self.V_BGLU = c; c += self.SC
        self.V_S5NW = c; c += self.SC
        self.NV = c
        base, rem = divmod(self.FC, NS)
        self.fsplit = []
        f = 0
        for s in range(NS):
            n = base + (1 if s < rem else 0)
            self.fsplit.append((f, f + n))
            f += n
        self.FSUB = base + (1 if rem else 0)


FULL = dict(D=4096, DFF=11008, NTOK=2048, H=4, DK=256, DV=512, G=128)
DEBUG_OUT = False
_LASTP = None
GB = 256
CH = 128
EPS = 1e-6


def build(cfg, stages=("mod", "A", "M", "B")):
    nc = bass.Bass("TRN2", target_bir_lowering=False)
    nc.dge_precook = False
    P = Prog(nc)
    global _LASTP
    _LASTP = P
    R = P.R
    D, DC, FC, T, NT, NTOK = cfg.D, cfg.DC, cfg.FC, cfg.T, cfg.NT, cfg.NTOK
    KC, GC, SC, MC, KC2, VC, H = cfg.KC, cfg.GC, cfg.SC, cfg.MC, cfg.KC2, cfg.VC, cfg.H
    KW, GW, SW, NGP, RK = cfg.KW, cfg.GW, cfg.SW, cfg.NGP, cfg.R
    XC = max(DC, MC)

    def din(name, shape, dt=F32):
        return nc.dram_tensor(name, list(shape), dt, kind="ExternalInput").ap()

    def dscr(name, shape, dt=F32):
        if DEBUG_OUT:
            return nc.dram_tensor(name, list(shape), dt, kind="ExternalOutput").ap()
        return nc.dram_tensor(name, list(shape), dt).ap()

    xT = din("xT", [DC, 128, NTOK])
    cT = din("cT", [128, DC])
    vecs = din("vecs", [128, cfg.NV])
    flag = din("flag", [128, 1])
    NADA = 9 * DC * 128 // 512
    NADS = 3 if NADA % 3 == 0 else 1
    NADP = NADA // NADS
    w_ada_parts = [din("w_ada%d" % i, [NADP, 128, DC * 512]) for i in range(NADS)]
    wg = [din("wg%d" % i, [FC, 128, DC * 128]) for i in range(2)]
    wu = [din("wu%d" % i, [FC, 128, DC * 128]) for i in range(2)]
    wd = [din("wd%d" % i, [DC, 128, FC * 128]) for i in range(2)]
    NFM = 2 * KC + GC + SC
    win_fm = din("win_fm", [NFM, 128, DC * 128])
    win_lr = din("win_lr", [128, DC * RK])
    NTMB = (KW + GW) // 512
    win_tm = din("win_tm", [NTMB, 128, DC * 512])
    w_gk2 = din("w_gk2", [RK, KW])
    b_gkr = din("b_gkr", [1, KW])
    w_glu = din("w_glu", [SC, 128, SC * 128])
    w_out = din("w_out", [DC, 128, MC * 128])
    s5lam = din("s5lam", [128, 3 * NGP])
    s5B = din("s5B", [NGP, 128, 256])
    s5C = din("s5C", [NGP, 128, 256])
    outT = nc.dram_tensor("outT", [DC, 128, NTOK], F32, kind="ExternalOutput").ap()

    res = dscr("res", [DC, 128, NTOK])
    qTs = dscr("qTs", [KC, 128, NTOK])
    kTs = dscr("kTs", [KC, 128, NTOK])
    gTs = dscr("gTs", [GC, 128, NTOK])
    uTs = dscr("uTs", [SC, 128, NTOK])
    lrs = dscr("lrs", [RK, NTOK])
    ktm = dscr("ktm", [NTOK, KW])
    vtm = dscr("vtm", [NTOK, GW])
    yas = dscr("yas", [SC, 128, NTOK])
    mTs = dscr("mTs", [MC, 128, NTOK])
    SJ = KC2 * H * cfg.DV // 128
    st_src = [nc.dram_tensor("st_src%d" % k, [32 * SJ, 128], F32).ap() for k in range(4)]
    st_dst = [nc.dram_tensor("st_dst%d" % k, [2 * 32 * SJ, 128], F32).ap() for k in range(4)]
    sz_src = nc.dram_tensor("sz_src", [128, 128], F32).ap()
    sz_dst = nc.dram_tensor("sz_dst", [256, 128], F32).ap()

    ones = P.sbuf("ones", [128, 128])
    maskT = P.sbuf("maskT", [128, 128])
    Umask = P.sbuf("Umask", [128, 128])
    rmask = P.sbuf("rmask", [128, 512])
    iot = P.sbuf("iot", [128, 512])
    vec = P.sbuf("vec", [128, cfg.NV])
    modT = P.sbuf("modT", [128, 9 * DC])
    csl = P.sbuf("csl", [128, DC])
    flg = P.sbuf("flg", [128, 1])
    AW = 45056
    AWF = 5700
    arenaF = P.sbuf("arenaF", [128, AWF])
    craw = P.sbuf("craw", [128, DC])
    arena = P.sbuf("arena", [128, AW])
    ps = [P.psum("ps%d" % i, [128, 512]) for i in range(8)]
    r_ps = [R("ps", i) for i in range(8)]
    r_c = R("consts")

    def AV(off, n, inner=None):
        v = arena[:, off:off + n]
        if inner is not None:
            v = v.rearrange("p (c t) -> p c t", t=inner)
        return v

    def r32(ap):
        return ap.bitcast(F32R)

    def AF_(off, n, inner=None):
        v = arenaF[:, off:off + n]
        if inner is not None:
            v = v.rearrange("p (c t) -> p c t", t=inner)
        return v

    tmpc = P.sbuf("tmpc", [128, 128])
    P.op("pool", lambda h: h.memset(tmpc[:], 1.0), writes=[r_c])
    P.op("pool", lambda h: h.tensor_copy(out=r32(ones[:]), in_=tmpc[:]), reads=[r_c], writes=[r_c])
    P.op("pool", lambda h: h.memset(maskT[:], 1.0), writes=[r_c])
    P.op("pool", lambda h: h.affine_select(out=maskT[:], in_=maskT[:], pattern=[[1, 128]], compare_op=ALU.is_ge,
                                           fill=0.0, base=0, channel_multiplier=-1), reads=[r_c], writes=[r_c])
    P.op("pool", lambda h: h.affine_select(out=tmpc[:], in_=tmpc[:], pattern=[[-1, 128]], compare_op=ALU.is_ge,
                                           fill=0.0, base=-1, channel_multiplier=1), reads=[r_c], writes=[r_c])
    P.op("pool", lambda h: h.tensor_copy(out=r32(Umask[:]), in_=tmpc[:]), reads=[r_c], writes=[r_c])
    P.op("pool", lambda h: h.memset(rmask[:], 1.0), writes=[r_c])
    for c in range(512 // CH):
        P.op("pool", lambda h, c=c: h.memset(rmask[:, c * CH:c * CH + 1], 0.0), reads=[r_c], writes=[r_c])
    P.op("pool", lambda h: h.iota(iot[:], pattern=[[1, 512]], base=0, channel_multiplier=0,
                                  allow_small_or_imprecise_dtypes=True), writes=[r_c])
    r_vec = R("vec")
    P.dma("pool", lambda h: h.dma_start(out=vec[:], in_=vecs), writes=[r_vec])
    P.dma("pool", lambda h: h.dma_start(out=flg[:], in_=flag), writes=[r_vec])
    r_mod = R("mod")

    if "mod" in stages:
        r_csl = R("csl")
        P.dma("pool", lambda h: h.dma_start(out=craw[:], in_=cT), writes=[R("craw")])
        P.op("act", lambda h: h.activation(out=csl[:], in_=craw[:], func=AF.Silu), reads=[R("craw")], writes=[r_csl])
        WT = DC * 512
        nsl = 2 if 2 * WT <= AW else 1
        r_aw = [R("adaw", i) for i in range(nsl)]
        for j in range(NADA):
            s = j % nsl
            wt = AV(s * WT, WT)
            wsrc = w_ada_parts[j // NADP][j % NADP]
            NPC = 4 if DC % 4 == 0 else 1
            CH_ = WT // NPC
            for pc in range(NPC):
                P.dma("sp", lambda h, wsrc=wsrc, wt=wt, pc=pc: h.dma_start(out=r32(wt[:, pc * CH_:(pc + 1) * CH_]),
                                                                          in_=r32(wsrc[:, pc * CH_:(pc + 1) * CH_])), writes=[r_aw[s]])
            for sub in range(4):
                col = j * 4 + sub
                for kc in range(DC):
                    P.op("pe", lambda h, wt=wt, sub=sub, kc=kc, col=col: h.matmul(
                        ps[7][:, col:col + 1], wt[:, kc * 512 + sub * 128: kc * 512 + (sub + 1) * 128],
                        csl[:, kc:kc + 1], start=(kc == 0), stop=(kc == DC - 1)),
                        reads=[r_aw[s], r_csl], writes=[r_ps[7]])
        P.op("dve", lambda h: h.tensor_tensor(out=modT[:], in0=ps[7][:, 0:9 * DC], in1=vec[:, cfg.V_BADA:cfg.V_BADA + 9 * DC],
                                              op=ALU.add), reads=[r_ps[7], r_vec], writes=[r_mod])
        for j in (1, 4, 7):
            P.op("dve", lambda h, j=j: h.tensor_scalar_add(out=modT[:, j * DC:(j + 1) * DC], in0=modT[:, j * DC:(j + 1) * DC],
                                                           scalar1=1.0), reads=[r_mod], writes=[r_mod])
        for j in (2, 8):
            P.op("dve", lambda h, j=j: h.tensor_scalar_mul(out=modT[:, j * DC:(j + 1) * DC], in0=modT[:, j * DC:(j + 1) * DC],
                                                           scalar1=0.5), reads=[r_mod], writes=[r_mod])
        P.barrier()

    def mcol(j, c):
        return modT[:, j * DC + c: j * DC + c + 1]

    def vcol(base, c):
        return vec[:, base + c: base + c + 1]

    O_NT = 0
    O_HID = O_NT + XC * T
    O_W = O_HID + cfg.FSUB * T
    WSL = max(DC * 128, MC * 128, min(8, DC) * 512, cfg.FSUB * 128)
    NW = 4
    O_SQ = O_W + NW * WSL
    O_END = O_SQ + 2 * T
    assert O_END <= AW, (O_END, AW)
    O_RB = 0
    O_OB = O_RB + 2 * T
    O_RS = O_OB + 2 * T
    O_SG = O_RS + T
    assert O_SG + 2 * T <= AWF
    nT = AV(O_NT, XC * T, T)
    hid = AV(O_HID, cfg.FSUB * T, T)
    wsl = [AV(O_W + i * WSL, WSL) for i in range(NW)]
    rb = [AF_(O_RB + i * T, T) for i in range(2)]
    ob = [AF_(O_OB + i * T, T) for i in range(2)]
    sq = [AV(O_SQ + i * T, T) for i in range(2)]
    rstd = AF_(O_RS, T)
    sgb = [AF_(O_SG + i * T, T) for i in range(2)]
    r_sg = [R("sg", i) for i in range(2)]
    r_nT = [R("nT", c) for c in range(XC)]
    r_hid = [R("hid", i) for i in range(cfg.FSUB)]
    r_w = [R("w", i) for i in range(NW)]
    r_rb = [R("rb", i) for i in range(2)]
    r_ob = [R("ob", i) for i in range(2)]
    r_sq = [R("sq", i) for i in range(2)]
    r_rs = R("rstd")
    cnt = {"w": 0, "rb": 0, "ob": 0, "sq": 0, "pa": 0, "pg": 0, "sg": 0}

    def nxt(k, n):
        v = cnt[k] % n
        cnt[k] += 1
        return v

    def wload(src_ap, n):
        s = nxt("w", NW)
        P.dma("sp", lambda h, s=s, n=n, src_ap=src_ap: h.dma_start(out=r32(wsl[s][:, 0:n]), in_=r32(src_ap)), writes=[r_w[s]])
        return s

    def tsl(ti):
        return slice(ti * T, (ti + 1) * T)

    def load_tile(src, nch, ti, rsrc):
        for c in range(nch):
            P.dma("pool", lambda h, c=c: h.dma_start(out=r32(nT[:, c, :]), in_=r32(src[c, :, tsl(ti)])),
                  reads=[R(rsrc, c, ti)], writes=[r_nT[c]])

    def sumsq(nch, Dn):
        for c in range(nch):
            b = nxt("sq", 2)
            P.op("act", lambda h, c=c, b=b: h.activation(out=r32(sq[b]), in_=nT[:, c, :], func=AF.Square),
                 reads=[r_nT[c]], writes=[r_sq[b]])
            P.op("pe", lambda h, c=c, b=b: h.matmul(ps[6][:], r32(ones[:]), r32(sq[b]), start=(c == 0), stop=(c == nch - 1)),
                 reads=[r_sq[b], r_c], writes=[r_ps[6]])
        P.op("act", lambda h: h.activation(out=rstd, in_=ps[6][:], func=AF.Sqrt, bias=EPS, scale=1.0 / Dn),
             reads=[r_ps[6]], writes=[r_rs])
        P.op("dve", lambda h: h.reciprocal(out=rstd, in_=rstd), reads=[r_rs], writes=[r_rs])

    def modnorm(jsh, jsc):
        sumsq(DC, D)
        for c in range(DC):
            P.op("dve", lambda h, c=c: h.scalar_tensor_tensor(out=r32(nT[:, c, :]), in0=nT[:, c, :], scalar=mcol(jsc, c), in1=rstd,
                                                              op0=ALU.mult, op1=ALU.mult),
                 reads=[r_nT[c], r_rs, r_mod], writes=[r_nT[c]])
            P.op("act", lambda h, c=c: h.activation(out=r32(nT[:, c, :]), in_=nT[:, c, :], func=AF.Identity,
                                                    bias=mcol(jsh, c), scale=1.0),
                 reads=[r_nT[c], r_mod], writes=[r_nT[c]])

    def ffn(ti, wgd, wud, wdd, jg, src0, rsrc0):
        for s, (f0, f1) in enumerate(cfg.fsplit):
            last = s == cfg.NS - 1
            for f in range(f0, f1):
                sg_ = wload(wgd[f], DC * 128)
                su_ = wload(wud[f], DC * 128)
                pg = 2 * nxt("pg", 2)
                for kc in range(DC):
                    P.op("pe", lambda h, kc=kc, sg_=sg_, pg=pg: h.matmul(ps[pg][:], r32(wsl[sg_][:, kc * 128:(kc + 1) * 128]),
                                                                       r32(nT[:, kc, :]), start=(kc == 0), stop=(kc == DC - 1)),
                         reads=[r_w[sg_], r_nT[kc]], writes=[r_ps[pg]])
                for kc in range(DC):
                    P.op("pe", lambda h, kc=kc, su_=su_, pg=pg: h.matmul(ps[pg + 1][:], r32(wsl[su_][:, kc * 128:(kc + 1) * 128]),
                                                                       r32(nT[:, kc, :]), start=(kc == 0), stop=(kc == DC - 1)),
                         reads=[r_w[su_], r_nT[kc]], writes=[r_ps[pg + 1]])
                b = nxt("sg", 2)
                P.op("act", lambda h, pg=pg, b=b: h.activation(out=sgb[b], in_=ps[pg][:], func=AF.Silu),
                     reads=[r_ps[pg]], writes=[r_sg[b]])
                P.op("dve", lambda h, pg=pg, b=b, f=f, f0=f0: h.tensor_tensor(out=r32(hid[:, f - f0, :]), in0=sgb[b], in1=ps[pg + 1][:],
                                                                             op=ALU.mult),
                     reads=[r_sg[b], r_ps[pg + 1]], writes=[r_hid[f - f0]])
            nf = f1 - f0
            for m in range(DC):
                sd = wload(wdd[m][:, f0 * 128:f1 * 128], nf * 128)
                pa = 4 + nxt("pa", 2)
                for i in range(nf):
                    P.op("pe", lambda h, i=i, sd=sd, pa=pa: h.matmul(ps[pa][:], r32(wsl[sd][:, i * 128:(i + 1) * 128]), r32(hid[:, i, :]),
                                                                    start=(i == 0), stop=(i == nf - 1)),
                         reads=[r_w[sd], r_hid[i]], writes=[r_ps[pa]])
                rbi = nxt("rb", 2)
                if s == 0:
                    P.dma("pool", lambda h, m=m, rbi=rbi: h.dma_start(out=rb[rbi], in_=src0[m, :, tsl(ti)]),
                          reads=[R(rsrc0, m, ti)], writes=[r_rb[rbi]])
                else:
                    P.dma("pool", lambda h, m=m, rbi=rbi: h.dma_start(out=rb[rbi], in_=res[m, :, tsl(ti)]),
                          reads=[R("res", m, ti)], writes=[r_rb[rbi]])
                if last:
                    P.op("dve", lambda h, m=m, pa=pa, rbi=rbi: h.scalar_tensor_tensor(
                        out=r32(nT[:, m, :]), in0=ps[pa][:], scalar=mcol(jg, m), in1=rb[rbi], op0=ALU.mult, op1=ALU.add),
                        reads=[r_ps[pa], r_rb[rbi], r_mod], writes=[r_nT[m]])
                    P.dma("pool", lambda h, m=m: h.dma_start(out=res[m, :, tsl(ti)], in_=nT[:, m, :]),
                          reads=[r_nT[m]], writes=[R("res", m, ti)])
                else:
                    obi = nxt("ob", 2)
                    P.op("dve", lambda h, m=m, pa=pa, rbi=rbi, obi=obi: h.scalar_tensor_tensor(
                        out=ob[obi], in0=ps[pa][:], scalar=mcol(jg, m), in1=rb[rbi], op0=ALU.mult, op1=ALU.add),
                        reads=[r_ps[pa], r_rb[rbi], r_mod], writes=[r_ob[obi]])
                    P.dma("pool", lambda h, m=m, obi=obi: h.dma_start(out=res[m, :, tsl(ti)], in_=ob[obi]),
                          reads=[r_ob[obi]], writes=[R("res", m, ti)])

    def proj_fm(ti, wsrc, nk, dst, rdst, oc, rhs_nk=None):
        sw = wload(wsrc, nk * 128)
        pa = 4 + nxt("pa", 2)
        for kc in range(nk):
            P.op("pe", lambda h, kc=kc, sw=sw, pa=pa: h.matmul(ps[pa][:], r32(wsl[sw][:, kc * 128:(kc + 1) * 128]), r32(nT[:, kc, :]),
                                                              start=(kc == 0), stop=(kc == nk - 1)),
                 reads=[r_w[sw], r_nT[kc]], writes=[r_ps[pa]])
        return pa

    if "A" in stages:
        for ti in range(NT):
            load_tile(xT, DC, ti, "xin")
            modnorm(0, 1)
            ffn(ti, wg[0], wu[0], wd[0], 2, xT, "xin")
            modnorm(3, 4)
            dsts = ([(qTs, "qT", i) for i in range(KC)] + [(kTs, "kT", i) for i in range(KC)]
                    + [(gTs, "gT", i) for i in range(GC)] + [(uTs, "uT", i) for i in range(SC)])
            for oc, (dst, rn, i) in enumerate(dsts):
                pa = proj_fm(ti, win_fm[oc], DC, dst, rn, i)
                obi = nxt("ob", 2)
                P.op("act", lambda h, pa=pa, obi=obi: h.activation(out=ob[obi], in_=ps[pa][:], func=AF.Copy),
                     reads=[r_ps[pa]], writes=[r_ob[obi]])
                P.dma("pool", lambda h, dst=dst, i=i, obi=obi, ti=ti: h.dma_start(out=dst[i, :, tsl(ti)], in_=ob[obi]),
                      reads=[r_ob[obi]], writes=[R(rn, i, ti)])
            sw = wload(win_lr, DC * RK)
            pa = 4 + nxt("pa", 2)
            for kc in range(DC):
                P.op("pe", lambda h, kc=kc, sw=sw, pa=pa: h.matmul(ps[pa][0:RK, :], r32(wsl[sw][:, kc * RK:(kc + 1) * RK]), r32(nT[:, kc, :]),
                                                                  start=(kc == 0), stop=(kc == DC - 1)),
                     reads=[r_w[sw], r_nT[kc]], writes=[r_ps[pa]])
            obi = nxt("ob", 2)
            P.op("act", lambda h, pa=pa, obi=obi: h.activation(out=ob[obi][0:RK, :], in_=ps[pa][0:RK, :], func=AF.Copy),
                 reads=[r_ps[pa]], writes=[r_ob[obi]])
            P.dma("pool", lambda h, obi=obi, ti=ti: h.dma_start(out=lrs[:, tsl(ti)], in_=ob[obi][0:RK, :]),
                  reads=[r_ob[obi]], writes=[R("lr", ti)])
            KG = 8 if DC >= 8 else DC
            for blk in range(NTMB):
                for kg in range(DC // KG):
                    sw = wload(win_tm[blk][:, kg * KG * 512:(kg + 1) * KG * 512], KG * 512)
                    for tc in range(T // 128):
                        for k8 in range(KG):
                            kc = kg * KG + k8
                            P.op("pe", lambda h, tc=tc, k8=k8, kc=kc, sw=sw: h.matmul(
                                ps[tc][:], r32(nT[:, kc, tc * 128:(tc + 1) * 128]), r32(wsl[sw][:, k8 * 512:(k8 + 1) * 512]),
                                start=(kc == 0), stop=(kc == DC - 1)),
                                reads=[r_w[sw], r_nT[kc]], writes=[r_ps[tc]])
                for tc in range(T // 128):
                    obi = nxt("ob", 2)
                    P.op("act", lambda h, tc=tc, obi=obi: h.activation(out=ob[obi], in_=ps[tc][:], func=AF.Copy),
                         reads=[r_ps[tc]], writes=[r_ob[obi]])
                    r0 = ti * T + tc * 128
                    if blk < KW // 512:
                        P.dma("pool", lambda h, obi=obi, r0=r0, blk=blk: h.dma_start(out=ktm[r0:r0 + 128, blk * 512:(blk + 1) * 512], in_=ob[obi]),
                              reads=[r_ob[obi]], writes=[R("ktm", ti)])
                    else:
                        b2 = blk - KW // 512
                        P.dma("pool", lambda h, obi=obi, r0=r0, b2=b2: h.dma_start(out=vtm[r0:r0 + 128, b2 * 512:(b2 + 1) * 512], in_=ob[obi]),
                              reads=[r_ob[obi]], writes=[R("vtm", ti)])
        P.barrier()

    if "M" in stages:
        DK, DV = cfg.DK, cfg.DV
        NCH = GB // CH
        NBL = NTOK // GB
        SDIM = KC2 * H * DV
        o = 0
        def take(n):
            nonlocal o
            v = o
            o += n
            return v
        qTb = AV(take(KC * GB), KC * GB, GB)
        kTb = AV(take(KC * GB), KC * GB, GB)
        cum = AV(take(KC * GB), KC * GB, GB)
        eq = AV(take(KC * GB), KC * GB, GB)
        gate = AV(take(GC * GB), GC * GB, GB)
        oT = AV(take(GC * GB), GC * GB, GB)
        ktmb = AV(take(NCH * KW), NCH * KW, KW)
        vtmb = AV(take(NCH * GW), NCH * GW, GW)
        gtm = AV(take(KW), KW)
        kdec = AV(take(KW), KW)
        attT = [AV(take(128), 128) for _ in range(2)]
        S_r = AV(take(SDIM), SDIM)
        glr = AV(take(GB), GB)
        wgk = AV(take(KW), KW)
        bgk = AV(take(KW), KW)
        sqm = [AV(take(GB), GB) for _ in range(2)]
        assert o <= AW, o
        of = 0
        def takef(n):
            nonlocal of
            v = of
            of += n
            return v
        S = AF_(takef(SDIM), SDIM)
        rstm = AF_(takef(GB), GB)
        zc = AF_(takef(2 * NGP), 2 * NGP)
        NPAR = 14
        s5p = AF_(takef(NPAR * NGP), NPAR * NGP, NGP)
        tiny = AF_(takef(8), 8)
        assert of <= AWF, of
        r_S = [R("S", h) for h in range(H)]
        r_Sr = [R("Sr", h) for h in range(H)]
        r_zc = R("zc")
        r_s5p = R("s5p")
        L_R, L_I, L_DT, P_R, P_TH, P_FR, P_FI, P_CN, P_SN, P_T0, P_T1, P_T2, P_T3, P_T4 = range(14)
        PI = math.pi

        def pcol(k, gp):
            return s5p[:, k, gp:gp + 1]


        MAGIC = 12582912.0
        def sincos(xa, tb_, o_sin, o_cos, rx, rt, ro, typed):
            cast = r32 if typed else (lambda a: a)
            for which, dst in ((0, o_sin), (1, o_cos)):
                if which == 1:
                    P.op("dve", lambda h: h.tensor_scalar_add(out=xa, in0=xa, scalar1=0.5 * PI), reads=[rx], writes=[rx])
                P.op("dve", lambda h: h.tensor_scalar(out=tb_, in0=xa, scalar1=1.0 / (2 * PI), scalar2=MAGIC, op0=ALU.mult, op1=ALU.add),
                     reads=[rx], writes=[rt])
                P.op("dve", lambda h: h.tensor_scalar(out=tb_, in0=tb_, scalar1=-MAGIC, scalar2=-2 * PI, op0=ALU.add, op1=ALU.mult),
                     reads=[rt], writes=[rt])
                P.op("dve", lambda h: h.tensor_tensor(out=tb_, in0=tb_, in1=xa, op=ALU.add), reads=[rt, rx], writes=[rt])
                P.op("dve", lambda h: h.tensor_scalar(out=tb_, in0=tb_, scalar1=-PI, scalar2=PI, op0=ALU.max, op1=ALU.min),
                     reads=[rt], writes=[rt])
                if typed:
                    P.op("act", lambda h: h.activation(out=tb_, in_=tb_, func=AF.Sin), reads=[rt], writes=[rt])
                    P.op("dve", lambda h, dst=dst: h.tensor_copy(out=r32(dst), in_=tb_), reads=[rt], writes=[ro])
                else:
                    P.op("act", lambda h, dst=dst: h.activation(out=dst, in_=tb_, func=AF.Sin), reads=[rt], writes=[ro])

        def s5_params():
            P.dma("sp", lambda h: h.dma_start(out=s5p[:, 0:3, :], in_=s5lam.rearrange("p (k g) -> p k g", k=3)), writes=[r_s5p])
            def A(fn):
                P.op("act", fn, reads=[r_s5p], writes=[r_s5p])
            def V(fn):
                P.op("pool", fn, reads=[r_s5p], writes=[r_s5p])
            A(lambda h: h.activation(out=s5p[:, L_DT, :], in_=s5p[:, L_DT, :], func=AF.Exp))
            V(lambda h: h.tensor_tensor(out=s5p[:, P_R, :], in0=s5p[:, L_R, :], in1=s5p[:, L_DT, :], op=ALU.mult))
            A(lambda h: h.activation(out=s5p[:, P_R, :], in_=s5p[:, P_R, :], func=AF.Exp))
            V(lambda h: h.tensor_tensor(out=s5p[:, P_TH, :], in0=s5p[:, L_I, :], in1=s5p[:, L_DT, :], op=ALU.mult))
            V(lambda h: h.tensor_copy(out=s5p[:, P_T2, :], in_=s5p[:, P_TH, :]))
            sincos(s5p[:, P_T2, :], s5p[:, P_T3, :], s5p[:, P_T1, :], s5p[:, P_T0, :], r_s5p, r_s5p, r_s5p, False)
            V(lambda h: h.tensor_tensor(out=s5p[:, P_T2, :], in0=s5p[:, P_R, :], in1=s5p[:, P_T0, :], op=ALU.mult))
            V(lambda h: h.tensor_scalar_add(out=s5p[:, P_T2, :], in0=s5p[:, P_T2, :], scalar1=-1.0))
            V(lambda h: h.tensor_tensor(out=s5p[:, P_T3, :], in0=s5p[:, P_R, :], in1=s5p[:, P_T1, :], op=ALU.mult))
            V(lambda h: h.tensor_tensor(out=s5p[:, P_T4, :], in0=s5p[:, L_R, :], in1=s5p[:, L_R, :], op=ALU.mult))
            V(lambda h: h.tensor_tensor(out=s5p[:, P_T0, :], in0=s5p[:, L_I, :], in1=s5p[:, L_I, :], op=ALU.mult))
            V(lambda h: h.tensor_tensor(out=s5p[:, P_T4, :], in0=s5p[:, P_T4, :], in1=s5p[:, P_T0, :], op=ALU.add))
            P.op("dve", lambda h: h.reciprocal(out=s5p[:, P_T4, :], in_=s5p[:, P_T4, :]), reads=[r_s5p], writes=[r_s5p])
            V(lambda h: h.tensor_tensor(out=s5p[:, P_FR, :], in0=s5p[:, P_T2, :], in1=s5p[:, L_R, :], op=ALU.mult))
            V(lambda h: h.tensor_tensor(out=s5p[:, P_T0, :], in0=s5p[:, P_T3, :], in1=s5p[:, L_I, :], op=ALU.mult))
            V(lambda h: h.tensor_tensor(out=s5p[:, P_FR, :], in0=s5p[:, P_FR, :], in1=s5p[:, P_T0, :], op=ALU.add))
            V(lambda h: h.tensor_tensor(out=s5p[:, P_FR, :], in0=s5p[:, P_FR, :], in1=s5p[:, P_T4, :], op=ALU.mult))
            V(lambda h: h.tensor_tensor(out=s5p[:, P_FI, :], in0=s5p[:, P_T3, :], in1=s5p[:, L_R, :], op=ALU.mult))
            V(lambda h: h.tensor_tensor(out=s5p[:, P_T0, :], in0=s5p[:, P_T2, :], in1=s5p[:, L_I, :], op=ALU.mult))
            V(lambda h: h.tensor_tensor(out=s5p[:, P_FI, :], in0=s5p[:, P_FI, :], in1=s5p[:, P_T0, :], op=ALU.subtract))
            V(lambda h: h.tensor_tensor(out=s5p[:, P_FI, :], in0=s5p[:, P_FI, :], in1=s5p[:, P_T4, :], op=ALU.mult))
            V(lambda h: h.tensor_scalar(out=s5p[:, P_T2, :], in0=s5p[:, P_TH, :], scalar1=512.0, scalar2=None, op0=ALU.mult))
            sincos(s5p[:, P_T2, :], s5p[:, P_T3, :], s5p[:, P_SN, :], s5p[:, P_CN, :], r_s5p, r_s5p, r_s5p, False)

        def gla_pass(state_only):
            r_q, r_k, r_cum, r_eq, r_gate, r_oT = R("qTb"), R("kTb"), R("cum"), R("eqb"), R("gateb"), R("oTb")
            r_ktm, r_vtm, r_gtm, r_kdec, r_glr = R("ktmb"), R("vtmb"), R("gtm"), R("kdec"), R("glr")
            r_att = [R("att", i) for i in range(2)]
            r_sqm = [R("sqm", i) for i in range(2)]
            r_rstm = R("rstm")
            r_wgk = R("wgk")
            P.dma("sp", lambda h: h.dma_start(out=r32(wgk[0:RK, :]), in_=r32(w_gk2)), writes=[r_wgk])
            P.dma("sp", lambda h: h.dma_start(out=r32(bgk[0:1, :]), in_=r32(b_gkr)), writes=[r_wgk])
            kcnt = [0]
            def pb():
                kcnt[0] += 1
                return kcnt[0] % 4
            for bi in range(NBL):
                t0 = bi * GB
                tb = slice(t0, t0 + GB)
                ti_src = t0 // T
                if not state_only:
                    P.dma("sp", lambda h, tb=tb: h.dma_start(out=r32(qTb), in_=r32(qTs[:, :, tb].rearrange("c p t -> p c t"))),
                          reads=[R("qT", i, ti_src) for i in range(KC)], writes=[r_q])
                    P.dma("sp", lambda h, tb=tb: h.dma_start(out=r32(kTb), in_=r32(kTs[:, :, tb].rearrange("c p t -> p c t"))),
                          reads=[R("kT", i, ti_src) for i in range(KC)], writes=[r_k])
                    P.dma("sp", lambda h, tb=tb: h.dma_start(out=r32(gate), in_=r32(gTs[:, :, tb].rearrange("c p t -> p c t"))),
                          reads=[R("gT", i, ti_src) for i in range(GC)], writes=[r_gate])
                P.dma("sp", lambda h, tb=tb: h.dma_start(out=r32(ktmb), in_=r32(ktm[tb, :].rearrange("(c p) w -> p c w", p=128))),
                      reads=[R("ktm", ti_src)], writes=[r_ktm])
                P.dma("sp", lambda h, tb=tb: h.dma_start(out=r32(vtmb), in_=r32(vtm[tb, :].rearrange("(c p) w -> p c w", p=128))),
                      reads=[R("vtm", ti_src)], writes=[r_vtm])
                P.dma("sp", lambda h, tb=tb: h.dma_start(out=r32(glr[0:RK, :]), in_=r32(lrs[:, tb])), reads=[R("lr", ti_src)], writes=[r_glr])
                for kc in range(KC):
                    b = pb()
                    P.op("pe", lambda h, kc=kc, b=b: h.matmul(ps[b][:, 0:GB], r32(wgk[0:RK, kc * 128:(kc + 1) * 128]), r32(glr[0:RK, :]),
                                                              start=True, stop=True), reads=[r_wgk, r_glr], writes=[r_ps[b]])
                    P.op("act", lambda h, kc=kc, b=b: h.activation(out=r32(eq[:, kc, :]), in_=ps[b][:, 0:GB], func=AF.Sigmoid,
                                                                   bias=vcol(cfg.V_BGK, kc), scale=1.0),
                         reads=[r_ps[b], r_vec], writes=[r_eq])
                P.op("act", lambda h: h.activation(out=r32(eq), in_=eq, func=AF.Ln), reads=[r_eq], writes=[r_eq])
                for kc in range(KC):
                    P.op("dve", lambda h, kc=kc: h.tensor_tensor_scan(out=r32(cum[:, kc, :]), data0=rmask[:, 0:GB], data1=eq[:, kc, :],
                                                                      initial=0.0, op0=ALU.mult, op1=ALU.add),
                         reads=[r_eq, r_c], writes=[r_cum])
                P.op("act", lambda h: h.activation(out=r32(eq), in_=cum, func=AF.Exp, scale=1.0 / 16.0), reads=[r_cum], writes=[r_eq])
                if not state_only:
                    P.op("act", lambda h: h.activation(out=r32(cum), in_=cum, func=AF.Exp, scale=-1.0 / 16.0), reads=[r_cum], writes=[r_cum])
                    P.op("dve", lambda h: h.scalar_tensor_tensor(out=r32(qTb), in0=qTb, scalar=float(DK) ** -0.5, in1=eq,
                                                                 op0=ALU.mult, op1=ALU.mult), reads=[r_q, r_eq], writes=[r_q])
                    P.op("dve", lambda h: h.tensor_tensor(out=r32(kTb), in0=kTb, in1=cum, op=ALU.mult), reads=[r_k, r_cum], writes=[r_k])
                for c in range(NCH):
                    cs = slice(c * CH, (c + 1) * CH)
                    for hf in range(KW // 512):
                        b = pb()
                        hs = slice(hf * 512, (hf + 1) * 512)
                        P.op("pe", lambda h, cs=cs, hs=hs, b=b: h.matmul(ps[b][:], r32(glr[0:RK, cs]), r32(wgk[0:RK, hs]), start=True, stop=False),
                             reads=[r_glr, r_wgk], writes=[r_ps[b]])
                        P.op("pe", lambda h, hs=hs, b=b: h.matmul(ps[b][:], r32(ones[0:1, :]), r32(bgk[0:1, hs]), start=False, stop=True),
                             reads=[r_c, r_wgk], writes=[r_ps[b]])
                        P.op("act", lambda h, hs=hs, b=b: h.activation(out=r32(gtm[:, hs]), in_=ps[b][:], func=AF.Sigmoid),
                             reads=[r_ps[b]], writes=[r_gtm])
                    P.op("act", lambda h: h.activation(out=r32(gtm), in_=gtm, func=AF.Ln), reads=[r_gtm], writes=[r_gtm])
                    for hf in range(KW // 512):
                        b = pb()
                        hs = slice(hf * 512, (hf + 1) * 512)
                        P.op("pe", lambda h, hs=hs, b=b: h.matmul(ps[b][:], r32(Umask[:]), r32(gtm[:, hs]), start=True, stop=True),
                             reads=[r_c, r_gtm], writes=[r_ps[b]])
                        P.op("act", lambda h, hs=hs, b=b: h.activation(out=r32(kdec[:, hs]), in_=ps[b][:], func=AF.Exp, scale=1.0 / 16.0),
                             reads=[r_ps[b]], writes=[r_kdec])
                    P.op("dve", lambda h, c=c: h.tensor_tensor(out=r32(kdec), in0=kdec, in1=ktmb[:, c, :], op=ALU.mult),
                         reads=[r_kdec, r_ktm], writes=[r_kdec])
                    for hh in range(H):
                        if not state_only:
                            b = pb()
                            for k2 in range(KC2):
                                P.op("pe", lambda h, hh=hh, k2=k2, cs=cs, b=b: h.matmul(
                                    ps[b][:, 0:CH], r32(kTb[:, hh * KC2 + k2, cs]), r32(qTb[:, hh * KC2 + k2, cs]),
                                    start=(k2 == 0), stop=(k2 == KC2 - 1)), reads=[r_k, r_q], writes=[r_ps[b]])
                            ai = (c * H + hh) % 2
                            P.op("dve", lambda h, b=b, ai=ai: h.tensor_tensor(out=r32(attT[ai]), in0=ps[b][:, 0:CH], in1=maskT[:], op=ALU.mult),
                                 reads=[r_ps[b], r_c], writes=[r_att[ai]])
                            b2 = pb()
                            for vc in range(VC):
                                osl = slice(vc * 128, (vc + 1) * 128)
                                P.op("pe", lambda h, hh=hh, vc=vc, c=c, ai=ai, b2=b2, osl=osl: h.matmul(
                                    ps[b2][:, osl], r32(vtmb[:, c, hh * DV + vc * 128: hh * DV + (vc + 1) * 128]), r32(attT[ai]),
                                    start=True, stop=False), reads=[r_vtm, r_att[ai]], writes=[r_ps[b2]])
                                for k2 in range(KC2):
                                    so = (k2 * H + hh) * DV + vc * 128
                                    P.op("pe", lambda h, hh=hh, k2=k2, cs=cs, b2=b2, osl=osl, so=so: h.matmul(
                                        ps[b2][:, osl], r32(S_r[:, so:so + 128]), r32(qTb[:, hh * KC2 + k2, cs]),
                                        start=False, stop=(k2 == KC2 - 1)), reads=[r_Sr[hh], r_q], writes=[r_ps[b2]])
                            P.op("act", lambda h, hh=hh, cs=cs, b2=b2: h.activation(
                                out=r32(oT[:, hh * VC:(hh + 1) * VC, cs]), in_=ps[b2][:, 0:VC * 128].rearrange("p (v t) -> p v t", t=128),
                                func=AF.Copy), reads=[r_ps[b2]], writes=[r_oT])
                        for k2 in range(KC2):
                            b = pb()
                            so = (k2 * H + hh) * DV
                            P.op("pe", lambda h, hh=hh, k2=k2, c=c, b=b: h.matmul(
                                ps[b][:, 0:DV], r32(kdec[:, hh * DK + k2 * 128: hh * DK + (k2 + 1) * 128]), r32(vtmb[:, c, hh * DV:(hh + 1) * DV]),
                                start=True, stop=True), reads=[r_kdec, r_vtm], writes=[r_ps[b]])
                            col = c * CH + CH - 1
                            P.op("dve", lambda h, hh=hh, k2=k2, b=b, so=so, col=col: h.scalar_tensor_tensor(
                                out=S[:, so:so + DV], in0=S[:, so:so + DV], scalar=eq[:, hh * KC2 + k2, col:col + 1], in1=ps[b][:, 0:DV],
                                op0=ALU.mult, op1=ALU.add), reads=[r_S[hh], r_eq, r_ps[b]], writes=[r_S[hh]])
                            P.op("act", lambda h, so=so: h.activation(out=r32(S_r[:, so:so + DV]), in_=S[:, so:so + DV], func=AF.Copy),
                                 reads=[r_S[hh]], writes=[r_Sr[hh]])
                if state_only:
                    continue
                P.op("act", lambda h: h.activation(out=r32(gate), in_=gate, func=AF.Silu), reads=[r_gate], writes=[r_gate])
                for hh in range(H):
                    b = pb()
                    for vc in range(VC):
                        si = vc % 2
                        P.op("act", lambda h, hh=hh, vc=vc, si=si: h.activation(out=r32(sqm[si]), in_=oT[:, hh * VC + vc, :], func=AF.Square),
                             reads=[r_oT], writes=[r_sqm[si]])
                        P.op("pe", lambda h, vc=vc, si=si, b=b: h.matmul(ps[b][:, 0:GB], r32(ones[:]), r32(sqm[si]), start=(vc == 0), stop=(vc == VC - 1)),
                             reads=[r_sqm[si], r_c], writes=[r_ps[b]])
                    P.op("act", lambda h, b=b: h.activation(out=rstm, in_=ps[b][:, 0:GB], func=AF.Sqrt, bias=EPS, scale=1.0 / DV),
                         reads=[r_ps[b]], writes=[r_rstm])
                    P.op("dve", lambda h: h.reciprocal(out=rstm, in_=rstm), reads=[r_rstm], writes=[r_rstm])
                    for vc in range(VC):
                        ch = hh * VC + vc
                        P.op("dve", lambda h, ch=ch: h.tensor_tensor(out=r32(oT[:, ch, :]), in0=oT[:, ch, :], in1=rstm, op=ALU.mult),
                             reads=[r_oT, r_rstm], writes=[r_oT])
                        P.op("dve", lambda h, ch=ch, vc=vc: h.scalar_tensor_tensor(out=r32(oT[:, ch, :]), in0=oT[:, ch, :], scalar=vcol(cfg.V_GNW, vc),
                                                                                     in1=gate[:, ch, :], op0=ALU.mult, op1=ALU.mult),
                             reads=[r_oT, r_gate, r_vec], writes=[r_oT])
                P.dma("pool", lambda h, tb=tb: h.dma_start(out=mTs[0:GC, :, tb].rearrange("c p t -> p c t"), in_=oT),
                      reads=[r_oT], writes=[R("mT", i, ti_src) for i in range(MC)])

        S5DBG = int(os.environ.get("S5DBG", "9"))

        def s5_pass(state_only):
            NB = 512
            NBK = NTOK // NB
            o5 = [0]
            def t5(n):
                v = o5[0]
                o5[0] += n
                return v
            uTc = AV(t5(NTOK), NTOK)
            EinR, EinI, Ec, Es = (AV(t5(NB), NB) for _ in range(4))
            wks = [[AV(t5(NB), NB) for _ in range(10)] for _ in range(2)]
            wk = wks[0]
            tinys = [tiny[:, 0:2], tiny[:, 2:4]]
            r_tinys = [R("tiny", 0), R("tiny", 1)]
            rtab = AV(t5(NB), NB)
            r_rtab = R("rtab")
            Bw = [AV(t5(256), 256) for _ in range(2)]
            Cw = [AV(t5(256), 256) for _ in range(2)]
            r_u, r_tab = R("uTc"), R("s5tab")
            angf, angg = AF_(0, NB), AF_(NB, NB)
            r_angf, r_angg = R("angf"), R("angg")
            tabsets = [(EinR, EinI, Ec, Es, rtab, angf, angg, r_tab, r_rtab, r_angf, r_angg),
                       (AV(t5(NB), NB), AV(t5(NB), NB), AV(t5(NB), NB), AV(t5(NB), NB), AV(t5(NB), NB), AF_(2 * NB, NB), AF_(3 * NB, NB),
                        R("s5tab", 1), R("rtab", 1), R("angf", 1), R("angg", 1))]
            r_wks = [[R("s5wk", j, i) for i in range(10)] for j in range(2)]
            r_wk = r_wks[0]
            r_bw = [R("s5bw", i) for i in range(2)]
            r_cw = [R("s5cw", i) for i in range(2)]
            for fc in range(SC):
                P.dma("sp", lambda h, fc=fc: h.dma_start(out=r32(uTc), in_=r32(uTs[fc])), reads=[R("uT", fc, ti) for ti in range(NT)], writes=[r_u])
                def s5_gp(g4, EinR, EinI, Ec, Es, rtab, angf, angg, r_tab, r_rtab, r_angf, r_angg):
                    gp = fc * 4 + g4
                    wb = gp % 2
                    P.dma("sp", lambda h, gp=gp, wb=wb: h.dma_start(out=r32(Bw[wb]), in_=r32(s5B[gp])), writes=[r_bw[wb]])
                    if not state_only:
                        P.dma("sp", lambda h, gp=gp, wb=wb: h.dma_start(out=r32(Cw[wb]), in_=r32(s5C[gp])), writes=[r_cw[wb]])
                    TW = [r_tab, r_s5p]
                    P.op("act", lambda h, gp=gp: h.activation(out=angf, in_=iot[:, 0:NB], func=AF.Identity, scale=pcol(P_TH, gp)),
                         reads=[r_c, r_s5p], writes=[r_angf])
                    sincos(angf, angg, Es, Ec, r_angf, r_angg, r_tab, True)
                    P.op("act", lambda h, gp=gp: h.activation(out=r32(rtab), in_=iot[:, 0:NB], func=AF.Identity, bias=pcol(P_R, gp), scale=0.0),
                         reads=[r_c, r_s5p], writes=[r_rtab])
                    P.op("act", lambda h, gp=gp: h.activation(out=r32(wk[0]), in_=Es, func=AF.Identity, scale=pcol(P_FI, gp)),
                         reads=[r_tab, r_s5p], writes=[r_wk[0]])
                    P.op("dve", lambda h, gp=gp: h.scalar_tensor_tensor(out=r32(EinR), in0=Ec, scalar=pcol(P_FR, gp), in1=wk[0], op0=ALU.mult, op1=ALU.add),
                         reads=[r_tab, r_s5p, r_wk[0]], writes=[r_tab])
                    P.op("act", lambda h, gp=gp: h.activation(out=r32(wk[1]), in_=Es, func=AF.Identity, scale=pcol(P_FR, gp)),
                         reads=[r_tab, r_s5p], writes=[r_wk[1]])
                    P.op("dve", lambda h, gp=gp: h.scalar_tensor_tensor(out=r32(EinI), in0=Ec, scalar=pcol(P_FI, gp), in1=wk[1], op0=ALU.mult, op1=ALU.subtract),
                         reads=[r_tab, r_s5p, r_wk[1]], writes=[r_tab])
                    def s5_block(bk, wk, r_wk, tn, r_tn):
                        bs = slice(bk * NB, (bk + 1) * NB)
                        pr, pi_ = (0, 1) if (gp * NBK + bk) % 2 == 0 else (2, 3)
                        P.op("pe", lambda h, wb=wb, bs=bs, pr=pr: h.matmul(ps[pr][:], r32(Bw[wb][:, 0:128]), r32(uTc[:, bs]), start=True, stop=True),
                             reads=[r_bw[wb], r_u], writes=[r_ps[pr]])
                        P.op("pe", lambda h, wb=wb, bs=bs, pi_=pi_: h.matmul(ps[pi_][:], r32(Bw[wb][:, 128:256]), r32(uTc[:, bs]), start=True, stop=True),
                             reads=[r_bw[wb], r_u], writes=[r_ps[pi_]])
                        e1, e2 = ("dve", "pool")
                        P.op("act", lambda h, pr=pr: h.activation(out=r32(wk[4]), in_=ps[pr][:], func=AF.Copy), reads=[r_ps[pr]], writes=[r_wk[4]])
                        P.op("act", lambda h, pi_=pi_: h.activation(out=r32(wk[5]), in_=ps[pi_][:], func=AF.Copy), reads=[r_ps[pi_]], writes=[r_wk[5]])
                        P.op("dve", lambda h: h.tensor_tensor(out=r32(wk[2]), in0=wk[4], in1=EinR, op=ALU.mult), reads=[r_wk[4], r_tab], writes=[r_wk[2]])
                        P.op("dve", lambda h: h.tensor_tensor(out=r32(wk[3]), in0=wk[5], in1=EinI, op=ALU.mult), reads=[r_wk[5], r_tab], writes=[r_wk[3]])
                        P.op("dve", lambda h: h.tensor_tensor(out=r32(wk[2]), in0=wk[2], in1=wk[3], op=ALU.subtract), reads=[r_wk[2], r_wk[3]], writes=[r_wk[2]])
                        P.op("dve", lambda h: h.tensor_tensor(out=r32(wk[4]), in0=wk[4], in1=EinI, op=ALU.mult), reads=[r_wk[4], r_tab], writes=[r_wk[4]])
                        P.op("dve", lambda h: h.tensor_tensor(out=r32(wk[5]), in0=wk[5], in1=EinR, op=ALU.mult), reads=[r_wk[5], r_tab], writes=[r_wk[5]])
                        P.op("dve", lambda h: h.tensor_tensor(out=r32(wk[4]), in0=wk[4], in1=wk[5], op=ALU.add), reads=[r_wk[4], r_wk[5]], writes=[r_wk[4]])
                        P.op("dve", lambda h, gp=gp: h.scalar_tensor_tensor(out=r32(wk[2][:, 0:1]), in0=zc[:, 2 * gp:2 * gp + 1], scalar=pcol(P_R, gp),
                                                                            in1=wk[2][:, 0:1], op0=ALU.mult, op1=ALU.add),
                             reads=[r_wk[2], r_s5p, r_zc], writes=[r_wk[2]])
                        P.op("dve", lambda h, gp=gp: h.scalar_tensor_tensor(out=r32(wk[4][:, 0:1]), in0=zc[:, 2 * gp + 1:2 * gp + 2], scalar=pcol(P_R, gp),
                                                                            in1=wk[4][:, 0:1], op0=ALU.mult, op1=ALU.add),
                             reads=[r_wk[4], r_s5p, r_zc], writes=[r_wk[4]])
                        P.op("dve", lambda h: h.tensor_tensor_scan(out=r32(wk[6]), data0=rtab, data1=wk[2], initial=0.0, op0=ALU.mult, op1=ALU.add),
                             reads=[r_wk[2], r_rtab], writes=[r_wk[6]])
                        P.op("dve", lambda h: h.tensor_tensor_scan(out=r32(wk[7]), data0=rtab, data1=wk[4], initial=0.0, op0=ALU.mult, op1=ALU.add),
                             reads=[r_wk[4], r_rtab], writes=[r_wk[7]])
                        zl = NB - 1
                        P.op("act", lambda h, gp=gp: h.activation(out=tn[:, 0:1], in_=wk[7][:, zl:zl + 1], func=AF.Identity, scale=pcol(P_SN, gp)),
                             reads=[r_wk[7], r_s5p], writes=[r_tn])
                        P.op("act", lambda h, gp=gp: h.activation(out=tn[:, 1:2], in_=wk[6][:, zl:zl + 1], func=AF.Identity, scale=pcol(P_SN, gp)),
                             reads=[r_wk[6], r_s5p], writes=[r_tn])
                        P.op("dve", lambda h, gp=gp: h.scalar_tensor_tensor(out=zc[:, 2 * gp:2 * gp + 1], in0=wk[6][:, zl:zl + 1], scalar=pcol(P_CN, gp),
                                                                            in1=tn[:, 0:1], op0=ALU.mult, op1=ALU.subtract),
                             reads=[r_wk[6], r_s5p, r_tn], writes=[r_zc])
                        P.op("dve", lambda h, gp=gp: h.scalar_tensor_tensor(out=zc[:, 2 * gp + 1:2 * gp + 2], in0=wk[7][:, zl:zl + 1], scalar=pcol(P_CN, gp),
                                                                            in1=tn[:, 1:2], op0=ALU.mult, op1=ALU.add),
                             reads=[r_wk[7], r_s5p, r_tn], writes=[r_zc])
                        if state_only or S5DBG == 0:
                            return
                        P.op("dve", lambda h: h.tensor_tensor(out=r32(wk[3]), in0=wk[7], in1=Es, op=ALU.mult), reads=[r_wk[7], r_tab], writes=[r_wk[3]])
                        P.op("dve", lambda h: h.tensor_tensor(out=r32(wk[8]), in0=wk[6], in1=Ec, op=ALU.mult), reads=[r_wk[6], r_tab], writes=[r_wk[8]])
                        P.op("dve", lambda h: h.tensor_tensor(out=r32(wk[8]), in0=wk[8], in1=wk[3], op=ALU.subtract), reads=[r_wk[8], r_wk[3]], writes=[r_wk[8]])
                        if S5DBG == 2:
                            return
                        P.op("dve", lambda h: h.tensor_tensor(out=r32(wk[5]), in0=wk[6], in1=Es, op=ALU.mult), reads=[r_wk[6], r_tab], writes=[r_wk[5]])
                        P.op("dve", lambda h: h.tensor_tensor(out=r32(wk[9]), in0=wk[7], in1=Ec, op=ALU.mult), reads=[r_wk[7], r_tab], writes=[r_wk[9]])
                        if S5DBG == 3:
                            return
                        P.op("dve", lambda h: h.tensor_tensor(out=r32(wk[9]), in0=wk[9], in1=wk[5], op=ALU.add), reads=[r_wk[9], r_wk[5]], writes=[r_wk[9]])
                        P.op("act", lambda h: h.activation(out=r32(wk[9]), in_=wk[9], func=AF.Identity, scale=-1.0), reads=[r_wk[9]], writes=[r_wk[9]])
                        py = 4 + bk
                        if S5DBG == 1:
                            return
                        S5V = os.environ.get("S5V", "")
                        if S5V == "":
                            P.op("pe", lambda h, wb=wb, py=py, g4=g4: h.matmul(ps[py][:], r32(Cw[wb][:, 0:128]), r32(wk[8]), start=(g4 == 0), stop=False),
                                 reads=[r_cw[wb], r_wk[8]], writes=[r_ps[py]])
                            P.op("pe", lambda h, wb=wb, py=py, g4=g4: h.matmul(ps[py][:], r32(Cw[wb][:, 128:256]), r32(wk[9]), start=False, stop=(g4 == 3)),
                                 reads=[r_cw[wb], r_wk[9]], writes=[r_ps[py]])
                        elif S5V == "rhs":
                            P.op("pe", lambda h, wb=wb, py=py, g4=g4, bs=bs: h.matmul(ps[py][:], r32(Cw[wb][:, 0:128]), r32(uTc[:, bs]), start=(g4 == 0), stop=False),
                                 reads=[r_cw[wb], r_u], writes=[r_ps[py]])
                            P.op("pe", lambda h, wb=wb, py=py, g4=g4, bs=bs: h.matmul(ps[py][:], r32(Cw[wb][:, 128:256]), r32(uTc[:, bs]), start=False, stop=(g4 == 3)),
                                 reads=[r_cw[wb], r_u], writes=[r_ps[py]])
                        elif S5V == "sync":
                            P.op("pe", lambda h, wb=wb, py=py, g4=g4, bs=bs: h.matmul(ps[py][:], r32(Cw[wb][:, 0:128]), r32(uTc[:, bs]), start=(g4 == 0), stop=False),
                                 reads=[r_cw[wb], r_u, r_wk[8]], writes=[r_ps[py]])
                            P.op("pe", lambda h, wb=wb, py=py, g4=g4, bs=bs: h.matmul(ps[py][:], r32(Cw[wb][:, 128:256]), r32(uTc[:, bs]), start=False, stop=(g4 == 3)),
                                 reads=[r_cw[wb], r_u, r_wk[9]], writes=[r_ps[py]])
                        elif S5V == "lhs":
                            P.op("pe", lambda h, wb=wb, py=py, g4=g4: h.matmul(ps[py][:], r32(Bw[wb][:, 0:128]), r32(wk[8]), start=(g4 == 0), stop=False),
                                 reads=[r_bw[wb], r_wk[8]], writes=[r_ps[py]])
                            P.op("pe", lambda h, wb=wb, py=py, g4=g4: h.matmul(ps[py][:], r32(Bw[wb][:, 128:256]), r32(wk[9]), start=False, stop=(g4 == 3)),
                                 reads=[r_bw[wb], r_wk[9]], writes=[r_ps[py]])
                    for bk in range(NBK):
                        st_ = (gp * NBK + bk) % 2
                        s5_block(bk, wks[st_], r_wks[st_], tinys[st_], r_tinys[st_])
                for g4 in range(4):
                    s5_gp(g4, *tabsets[(fc * 4 + g4) % 2])
                if state_only or os.environ.get("S5NOOUT"):
                    continue
                for bk in range(NBK):
                    bs = slice(bk * NB, (bk + 1) * NB)
                    py = 4 + bk
                    P.op("dve", lambda h, fc=fc, bs=bs, py=py: h.scalar_tensor_tensor(out=r32(wk[0]), in0=uTc[:, bs], scalar=vcol(cfg.V_S5D, fc), in1=ps[py][:],
                                                                                      op0=ALU.mult, op1=ALU.add), reads=[r_u, r_vec, r_ps[py]], writes=[r_wk[0]])
                    P.op("act", lambda h: h.activation(out=r32(wk[1]), in_=wk[0], func=AF.Square), reads=[r_wk[0]], writes=[r_wk[1]])
                    P.op("dve", lambda h: h.tensor_scalar(out=r32(wk[1]), in0=wk[1], scalar1=0.044715, scalar2=1.0, op0=ALU.mult, op1=ALU.add),
                         reads=[r_wk[1]], writes=[r_wk[1]])
                    P.op("dve", lambda h: h.tensor_tensor(out=r32(wk[1]), in0=wk[1], in1=wk[0], op=ALU.mult), reads=[r_wk[1], r_wk[0]], writes=[r_wk[1]])
                    P.op("act", lambda h: h.activation(out=r32(wk[1]), in_=wk[1], func=AF.Sigmoid, scale=2.0 * math.sqrt(2.0 / math.pi)), reads=[r_wk[1]], writes=[r_wk[1]])
                    P.op("dve", lambda h: h.tensor_tensor(out=r32(wk[0]), in0=wk[0], in1=wk[1], op=ALU.mult), reads=[r_wk[0], r_wk[1]], writes=[r_wk[0]])
                    P.dma("pool", lambda h, fc=fc, bs=bs: h.dma_start(out=yas[fc, :, bs], in_=wk[0]), reads=[r_wk[0]], writes=[R("ya", fc)])

        def glu():
            y2 = AV(0, SC * 512, 512)
            yat = AV(SC * 512, SC * 512, 512)
            gw = [AV(2 * SC * 512 + i * SC * 128, SC * 128) for i in range(3)]
            r_y2, r_ya2 = [R("y2", i) for i in range(SC)], [R("yat", i) for i in range(SC)]
            r_gw = [R("gw", i) for i in range(3)]
            sqg = [AV(2 * SC * 512 + 3 * SC * 128 + i * 512, 512) for i in range(2)]
            r_sqg = [R("sqg", i) for i in range(2)]
            r_rs2 = R("rsg")
            rsg = AF_(0, 512)
            k = 0
            for bk in range(NTOK // 512):
                bs = slice(bk * 512, (bk + 1) * 512)
                for c in range(SC):
                    P.dma("sp", lambda h, c=c, bs=bs: h.dma_start(out=r32(yat[:, c, :]), in_=r32(yas[c, :, bs])), reads=[R("ya", c)], writes=[r_ya2[c]])
                for oc in range(SC):
                    wi = k % 3
                    k += 1
                    P.dma("sp", lambda h, oc=oc, wi=wi: h.dma_start(out=r32(gw[wi]), in_=r32(w_glu[oc])), writes=[r_gw[wi]])
                    b = k % 4
                    for kc in range(SC):
                        P.op("pe", lambda h, kc=kc, wi=wi, b=b: h.matmul(ps[b][:], r32(gw[wi][:, kc * 128:(kc + 1) * 128]), r32(yat[:, kc, :]),
                                                                        start=(kc == 0), stop=(kc == SC - 1)), reads=[r_gw[wi], r_ya2[kc]], writes=[r_ps[b]])
                    P.op("act", lambda h, oc=oc, b=b: h.activation(out=r32(y2[:, oc, :]), in_=ps[b][:], func=AF.Sigmoid, bias=vcol(cfg.V_BGLU, oc), scale=1.0),
                         reads=[r_ps[b], r_vec], writes=[r_y2[oc]])
                    P.op("dve", lambda h, oc=oc: h.tensor_tensor(out=r32(y2[:, oc, :]), in0=y2[:, oc, :], in1=yat[:, oc, :], op=ALU.mult),
                         reads=[r_y2[oc], r_ya2[oc]], writes=[r_y2[oc]])
                    si = oc % 2
                    P.op("act", lambda h, oc=oc, si=si: h.activation(out=r32(sqg[si]), in_=y2[:, oc, :], func=AF.Square), reads=[r_y2[oc]], writes=[r_sqg[si]])
                    P.op("pe", lambda h, oc=oc, si=si: h.matmul(ps[7][:], r32(ones[:]), r32(sqg[si]), start=(oc == 0), stop=(oc == SC - 1)),
                         reads=[r_sqg[si], r_c], writes=[r_ps[7]])
                P.op("act", lambda h: h.activation(out=rsg, in_=ps[7][:], func=AF.Sqrt, bias=EPS, scale=1.0 / SW), reads=[r_ps[7]], writes=[r_rs2])
                P.op("dve", lambda h: h.reciprocal(out=rsg, in_=rsg), reads=[r_rs2], writes=[r_rs2])
                for oc in range(SC):
                    P.op("dve", lambda h, oc=oc: h.scalar_tensor_tensor(out=r32(y2[:, oc, :]), in0=y2[:, oc, :], scalar=vcol(cfg.V_S5NW, oc), in1=rsg,
                                                                        op0=ALU.mult, op1=ALU.mult), reads=[r_y2[oc], r_rs2, r_vec], writes=[r_y2[oc]])
                    P.dma("pool", lambda h, oc=oc, bs=bs: h.dma_start(out=mTs[GC + oc, :, bs], in_=y2[:, oc, :]), reads=[r_y2[oc]],
                          writes=[R("mT", i, bk) for i in range(MC)])

        SROWS = SDIM
        def zero_state():
            for hh in range(H):
                pass
            P.op("dve", lambda h: h.memset(S, 0.0), writes=r_S)
            P.op("dve", lambda h: h.memset(zc, 0.0), writes=[r_zc])
            P.op("pool", lambda h: h.tensor_copy(out=r32(S_r), in_=S), reads=r_S, writes=r_Sr)

        SUB = os.environ.get("MIXSUB", "params,gla,s5,glu").split(",")
        if "params" in SUB:
            s5_params()
        if cfg.pair:
            zero_state()
            gla_pass(True)
            RG = [[2 * i, 2 * i + 1] for i in range(cfg.ncores // 2)]
            for k in range(4):
                P.dma("pool", lambda h, k=k: h.dma_start(out=st_src[k].rearrange("(p j) w -> p (j w)", p=32), in_=S[32 * k:32 * k + 32, :]),
                      reads=r_S, writes=[R("stsrc", k)])
            P.barrier()
            s5_pass(True)
            P.dma("pool", lambda h: h.dma_start(out=sz_src[:, 0:2 * NGP], in_=zc), reads=[r_zc], writes=[R("szsrc")])
            P.barrier()
            for k in range(4):
                P.coll("pool", lambda h, k=k: h.collective_compute("AllGather", ALU.bypass, replica_groups=RG, ins=[st_src[k][:]], outs=[st_dst[k][:]]),
                       reads=[R("stsrc", k)], writes=[R("stdst", k)])
            P.coll("pool", lambda h: h.collective_compute("AllGather", ALU.bypass, replica_groups=RG, ins=[sz_src[:]], outs=[sz_dst[:]]),
                   reads=[R("szsrc")], writes=[R("szdst")])
            for k in range(4):
                P.dma("pool", lambda h, k=k: h.dma_start(out=S[32 * k:32 * k + 32, :], in_=st_dst[k][0:32 * SJ, :].rearrange("(p j) w -> p (j w)", p=32)),
                      reads=[R("stdst", k)], writes=r_S)
            P.dma("pool", lambda h: h.dma_start(out=zc, in_=sz_dst[0:128, 0:2 * NGP]), reads=[R("szdst")], writes=[r_zc])
            P.op("act", lambda h: h.activation(out=S, in_=S, func=AF.Identity, scale=flg[:, 0:1]), reads=r_S + [r_vec], writes=r_S)
            P.op("act", lambda h: h.activation(out=zc, in_=zc, func=AF.Identity, scale=flg[:, 0:1]), reads=[r_zc, r_vec], writes=[r_zc])
            P.op("pool", lambda h: h.tensor_copy(out=r32(S_r), in_=S), reads=r_S, writes=r_Sr)
        else:
            zero_state()
        if "gla" in SUB:
            gla_pass(False)
        P.barrier()
        if "s5so" in SUB:
            s5_pass(True)
            P.barrier()
        if "s5" in SUB:
            s5_pass(False)
            P.barrier()
        if "glu" in SUB:
            glu()
        P.barrier()

    outs = []
    if "B" in stages:
        for ti in range(NT):
            load_tile(mTs, MC, ti, "mT")
            for m in range(DC):
                sw = wload(w_out[m], MC * 128)
                pa = 4 + nxt("pa", 2)
                for kc in range(MC):
                    P.op("pe", lambda h, kc=kc, sw=sw, pa=pa: h.matmul(ps[pa][:], r32(wsl[sw][:, kc * 128:(kc + 1) * 128]), r32(nT[:, kc, :]),
                                                                      start=(kc == 0), stop=(kc == MC - 1)),
                         reads=[r_w[sw], r_nT[kc]], writes=[r_ps[pa]])
                rbi = nxt("rb", 2)
                P.dma("pool", lambda h, m=m, rbi=rbi, ti=ti: h.dma_start(out=rb[rbi], in_=res[m, :, tsl(ti)]),
                      reads=[R("res", m, ti)], writes=[r_rb[rbi]])
                obi = nxt("ob", 2)
                P.op("dve", lambda h, m=m, pa=pa, rbi=rbi, obi=obi: h.scalar_tensor_tensor(
                    out=ob[obi], in0=ps[pa][:], scalar=mcol(5, m), in1=rb[rbi], op0=ALU.mult, op1=ALU.add),
                    reads=[r_ps[pa], r_rb[rbi], r_mod], writes=[r_ob[obi]])
                P.dma("pool", lambda h, m=m, obi=obi, ti=ti: h.dma_start(out=res[m, :, tsl(ti)], in_=ob[obi]),
                      reads=[r_ob[obi]], writes=[R("res", m, ti)])
            load_tile(res, DC, ti, "res")
            modnorm(6, 7)
            ffn(ti, wg[1], wu[1], wd[1], 8, res, "res")
            sumsq(DC, D)
            for c in range(DC):
                obi = nxt("ob", 2)
                P.op("dve", lambda h, c=c, obi=obi: h.scalar_tensor_tensor(out=ob[obi], in0=nT[:, c, :], scalar=vcol(cfg.V_FNW, c), in1=rstd,
                                                                          op0=ALU.mult, op1=ALU.mult),
                     reads=[r_nT[c], r_rs, r_vec], writes=[r_ob[obi]])
                outs.append(P.dma("pool", lambda h, c=c, obi=obi, ti=ti: h.dma_start(out=outT[c, :, tsl(ti)], in_=ob[obi]),
                                  reads=[r_ob[obi]]))
    if not outs:
        P.op("pool", lambda h: h.memset(arena[:, 0:NTOK], 0.0))
        for c in range(DC):
            outs.append(P.dma("pool", lambda h, c=c: h.dma_start(out=outT[c], in_=arena[:, 0:NTOK])))
    P.emit(outs)
    return nc


def build_mixer(cfg, nc, P, env):
    pass


def tile_kxn(w, ncols):
    K, N = w.shape
    a = w.reshape(K // 128, 128, N // ncols, ncols).transpose(2, 1, 0, 3)
    return np.ascontiguousarray(a).reshape(N // ncols, 128, (K // 128) * ncols)


def fm_vec(v):
    return np.ascontiguousarray(v.reshape(-1, 128).T)


def layout_shared(cfg, inp):
    L = 0
    m = {}
    wa = tile_kxn(inp["w_ada"][L], 512)
    nads = 3 if wa.shape[0] % 3 == 0 else 1
    for i in range(nads):
        m["w_ada%d" % i] = np.ascontiguousarray(wa[i * (wa.shape[0] // nads):(i + 1) * (wa.shape[0] // nads)])
    for i, pre in enumerate(("ffn1", "ffn2")):
        m["wg%d" % i] = tile_kxn(inp[pre + "_w_gate"][L], 128)
        m["wu%d" % i] = tile_kxn(inp[pre + "_w_up"][L], 128)
        m["wd%d" % i] = tile_kxn(inp[pre + "_w_down"][L], 128)
    w_in = inp["w_in"][L]
    KW, GW, SW, RK = cfg.KW, cfg.GW, cfg.SW, cfg.R
    o_q, o_k, o_v, o_g, o_lr, o_u = 0, KW, 2 * KW, 2 * KW + GW, 2 * KW + 2 * GW, 2 * KW + 2 * GW + RK
    fm_cols = np.concatenate([np.arange(o_q, o_q + KW), np.arange(o_k, o_k + KW), np.arange(o_g, o_g + GW), np.arange(o_u, o_u + SW)])
    m["win_fm"] = tile_kxn(w_in[:, fm_cols], 128)
    m["win_lr"] = tile_kxn(w_in[:, o_lr:o_lr + RK], RK)[0]
    tm_cols = np.concatenate([np.arange(o_k, o_k + KW), np.arange(o_v, o_v + GW)])
    m["win_tm"] = tile_kxn(w_in[:, tm_cols], 512)
    m["w_gk2"] = np.ascontiguousarray(inp["w_gk2"][L])
    m["b_gkr"] = np.ascontiguousarray(inp["b_gk"][L][None, :])
    m["w_glu"] = tile_kxn(inp["w_glu"][L], 128)
    m["w_out"] = tile_kxn(inp["w_out"][L], 128)
    vec = np.concatenate([fm_vec(inp["b_ada"][L]), fm_vec(inp["final_norm_w"]), fm_vec(inp["b_gk"][L]),
                          fm_vec(inp["gla_norm_w"][L]), fm_vec(inp["s5_d"][L]), fm_vec(inp["b_glu"][L]),
                          fm_vec(inp["s5_norm_w"][L])], axis=1)
    assert vec.shape == (128, cfg.NV), vec.shape
    m["vecs"] = np.ascontiguousarray(vec)
    NGP = cfg.NGP
    lam = np.zeros((128, 3 * NGP), np.float32)
    lam[:, 0:NGP] = inp["s5_lambda_re"][L].reshape(NGP, 128).T
    lam[:, NGP:2 * NGP] = inp["s5_lambda_im"][L].reshape(NGP, 128).T
    lam[:, 2 * NGP:] = np.repeat(inp["s5_log_dt"][L].reshape(NGP, 2), 64, axis=1).T
    m["s5lam"] = lam
    B = np.zeros((NGP, 128, 256), np.float32)
    C = np.zeros((NGP, 128, 256), np.float32)
    bre, bim = inp["s5_b_re"][L], inp["s5_b_im"][L]
    cre, cim = inp["s5_c_re"][L], inp["s5_c_im"][L]
    for gp in range(NGP):
        for g2 in range(2):
            g = 2 * gp + g2
            r0 = (g % 8) * 16
            B[gp, r0:r0 + 16, 64 * g2:64 * g2 + 64] = bre[g].T
            B[gp, r0:r0 + 16, 128 + 64 * g2:128 + 64 * g2 + 64] = bim[g].T
            C[gp, 64 * g2:64 * g2 + 64, r0:r0 + 16] = cre[g].T
            C[gp, 64 * g2:64 * g2 + 64, 128 + r0:128 + r0 + 16] = cim[g].T
    m["s5B"], m["s5C"] = B, C
    return m


def layout_core(cfg, inp, core, ncores_per_seq):
    b, half = divmod(core, ncores_per_seq)
    t0 = half * cfg.NTOK
    xs = inp["x"][b, t0:t0 + cfg.NTOK, :]
    m = {}
    m["xT"] = np.ascontiguousarray(xs.T).reshape(cfg.DC, 128, cfg.NTOK)
    m["cT"] = fm_vec(inp["c"][b])
    m["flag"] = np.full((128, 1), 1.0 if half > 0 else 0.0, np.float32)
    return m


_CACHE = {}


def kernel(**inputs):
    cfg = Cfg(**FULL)
    inp = {k: np.asarray(v) for k, v in inputs.items()}
    if "nc" not in _CACHE:
        _CACHE["nc"] = build(cfg)
    nc = _CACHE["nc"]
    shared = layout_shared(cfg, inp)
    in_maps = []
    for core in range(8):
        m = dict(shared)
        m.update(layout_core(cfg, inp, core, 2))
        in_maps.append(m)
    r = run_bass_kernel_spmd(nc, in_maps, core_ids=list(range(8)))
    out = np.empty((4, 4096, cfg.D), np.float32)
    for core in range(8):
        b, half = divmod(core, 2)
        oT = np.asarray(r.results[core]["outT"]).reshape(cfg.D, cfg.NTOK)
        out[b, half * cfg.NTOK:(half + 1) * cfg.NTOK, :] = oT.T
    return out
```
